# Optimizing a Trainium2 kernel written in Bass

```python
import jax, jax.numpy as jnp
from jax import lax
import numpy as np

D_MODEL = 4096
BATCH = 4
SEQ = 2048
DEPTH = 1

N_META = 16
CHUNK = 64
RET_HEADS = 8
RET_DK = D_MODEL // 16
RET_DV = D_MODEL // RET_HEADS
GLA_HEADS = 4
GLA_DK = D_MODEL // 8
GLA_DV = D_MODEL // GLA_HEADS
GLA_RANK = 16
GLA_GATE_TAU = 16.0
D_FF = 11008
CONV_W = 3
ROPE_BASE = 10000.0
EPS = 1e-6

RET_QK = RET_HEADS * RET_DK
RET_V = RET_HEADS * RET_DV
GLA_QK = GLA_HEADS * GLA_DK
GLA_V = GLA_HEADS * GLA_DV
IN_WIDTHS = (RET_QK, RET_QK, RET_V, RET_V,
             GLA_QK, GLA_QK, GLA_V, GLA_V,
             GLA_RANK,
             D_MODEL, D_MODEL)
W_IN_COLS = 4 * RET_QK // 2 + 2 * RET_V + 2 * GLA_QK + 2 * GLA_V + GLA_RANK + 2 * D_MODEL

kernel_name = "hybrid_retnet_gla_convffn_meta"


def _rmsnorm(x, g):
    xf = x.astype(jnp.float32)
    y = xf * lax.rsqrt(jnp.mean(xf * xf, axis=-1, keepdims=True) + EPS)
    return (y * g.astype(jnp.float32)).astype(x.dtype)


def _rope(x, pos):
    half = x.shape[-1] // 2
    inv_freq = ROPE_BASE ** (-jnp.arange(half, dtype=jnp.float32) / half)
    ang = pos.astype(jnp.float32)[..., None] * inv_freq
    cos = jnp.cos(ang)[:, :, None, :]
    sin = jnp.sin(ang)[:, :, None, :]
    x1 = x[..., :half].astype(jnp.float32)
    x2 = x[..., half:].astype(jnp.float32)
    return jnp.concatenate([x1 * cos - x2 * sin, x2 * cos + x1 * sin], axis=-1).astype(x.dtype)


def _to_chunks(t):
    B, L, H, d = t.shape
    t = jnp.pad(t, ((0, 0), (CHUNK - N_META, 0), (0, 0), (0, 0)))
    n = t.shape[1] // CHUNK
    return t.reshape(B, n, CHUNK, H, d).transpose(1, 0, 3, 2, 4)


def _from_chunks(o):
    n, B, H, C, d = o.shape
    return o.transpose(1, 0, 3, 2, 4).reshape(B, n * C, H, d)[:, CHUNK - N_META:]


def _retention_chunked(q, k, v):
    H, dk, dv = q.shape[2], q.shape[3], v.shape[3]
    B = q.shape[0]
    qc, kc, vc = _to_chunks(q), _to_chunks(k), _to_chunks(v)
    log_gamma = jnp.log1p(-jnp.exp2(-5.0 - jnp.arange(H, dtype=jnp.float32)))
    idx = jnp.arange(CHUNK)
    rel = idx[:, None] - idx[None, :]
    causal = rel >= 0
    decay_intra = jnp.where(causal, jnp.exp(log_gamma[:, None, None] * jnp.where(causal, rel, 0)), 0.0)
    decay_q = jnp.exp(log_gamma[:, None] * (idx + 1))[None, :, :, None]
    decay_k = jnp.exp(log_gamma[:, None] * (CHUNK - 1 - idx))[None, :, :, None]
    decay_chunk = jnp.exp(log_gamma * CHUNK)[None, :, None, None]

    def step(state, inp):
        qi, ki, vi = inp
        scores = jnp.einsum('bhqd,bhkd->bhqk', qi, ki) * decay_intra
        inner = jnp.einsum('bhqk,bhkv->bhqv', scores, vi)
        cross = jnp.einsum('bhqd,bhdv->bhqv', qi, state) * decay_q
        new_state = state * decay_chunk + jnp.einsum('bhkd,bhkv->bhdv', ki * decay_k, vi)
        return new_state, (inner + cross).astype(jnp.float32)

    state0 = jnp.zeros((B, H, dk, dv), jnp.float32)
    _, out = lax.scan(step, state0, (qc, kc, vc))
    return _from_chunks(out)


def _gla_chunked(q, k, v, log_a):
    B, H, dk, dv = q.shape[0], q.shape[2], q.shape[3], v.shape[3]
    qc, kc, vc, gc = _to_chunks(q), _to_chunks(k), _to_chunks(v), _to_chunks(log_a)
    idx = jnp.arange(CHUNK)
    causal = (idx[:, None] >= idx[None, :])[None, None, :, :, None]

    def step(state, inp):
        qi, ki, vi, gi = inp
        b = jnp.cumsum(gi.astype(jnp.float32), axis=2)
        cross = jnp.einsum('bhqd,bhdv->bhqv', qi * jnp.exp(b), state)
        diff = b[:, :, :, None, :] - b[:, :, None, :, :]
        w = jnp.where(causal, jnp.exp(jnp.where(causal, diff, 0.0)), 0.0)
        scores = jnp.einsum('bhqd,bhkd,bhqkd->bhqk', qi, ki, w)
        inner = jnp.einsum('bhqk,bhkv->bhqv', scores, vi)
        b_last = b[:, :, -1:, :]
        new_state = state * jnp.exp(b_last)[:, :, 0, :, None] + jnp.einsum(
            'bhkd,bhkv->bhdv', ki * jnp.exp(b_last - b), vi)
        return new_state, (inner + cross).astype(jnp.float32)

    state0 = jnp.zeros((B, H, dk, dv), jnp.float32)
    _, out = lax.scan(step, state0, (qc, kc, vc, gc))
    return _from_chunks(out)


def _hybrid_mixer(h, pos, w_in, w_gate_up, b_gate, g_ret, g_gla, w_out):
    B, L, _ = h.shape
    proj = h @ w_in
    splits = np.cumsum(IN_WIDTHS)[:-1].tolist()
    q_r, k_r, v_r, o_r, q_g, k_g, v_g, o_g, z_g, m_r, m_g = jnp.split(proj, splits, axis=-1)

    q = _rope(q_r.reshape(B, L, RET_HEADS, RET_DK), pos)
    k = _rope(k_r.reshape(B, L, RET_HEADS, RET_DK), pos) * (RET_DK ** -0.5)
    v = v_r.reshape(B, L, RET_HEADS, RET_DV)
    y = _retention_chunked(q, k, v)
    mu = jnp.mean(y, axis=-1, keepdims=True)
    var = jnp.mean(jnp.square(y - mu), axis=-1, keepdims=True)
    y = ((y - mu) * lax.rsqrt(var + EPS)).reshape(B, L, RET_V) * g_ret.astype(jnp.float32)
    y_ret = y.astype(h.dtype) * jax.nn.silu(o_r)

    log_a = jax.nn.log_sigmoid((z_g @ w_gate_up + b_gate).astype(jnp.float32)) / GLA_GATE_TAU
    q = q_g.reshape(B, L, GLA_HEADS, GLA_DK) * (GLA_DK ** -0.5)
    k = k_g.reshape(B, L, GLA_HEADS, GLA_DK)
    v = v_g.reshape(B, L, GLA_HEADS, GLA_DV)
    y = _gla_chunked(q, k, v, log_a.reshape(B, L, GLA_HEADS, GLA_DK))
    y = (y * lax.rsqrt(jnp.mean(y * y, axis=-1, keepdims=True) + EPS)).reshape(B, L, GLA_V)
    y_gla = (y * g_gla.astype(jnp.float32)).astype(h.dtype) * jax.nn.silu(o_g)

    merged = jax.nn.sigmoid(m_r) * y_ret + jax.nn.sigmoid(m_g) * y_gla
    return (merged @ w_out).astype(h.dtype)


def _conv_ffn(h, w_ffn_in, conv_w, conv_b, w_ffn_out):
    L = h.shape[1]
    up, gate = jnp.split(h @ w_ffn_in, 2, axis=-1)
    up_p = jnp.pad(up, ((0, 0), (CONV_W - 1, 0), (0, 0)))
    c = conv_b
    for j in range(CONV_W):
        c = c + conv_w[j] * up_p[:, j:j + L]
    return ((jax.nn.silu(c) * gate) @ w_ffn_out).astype(h.dtype)


def setup_inputs(seed: int = 0) -> dict:
    key = jax.random.key(seed)
    ks = jax.random.split(key, 16)
    f32 = jnp.float32
    nrm = lambda k, s, sc: jax.random.normal(k, s, f32) * sc
    return {
        "x": nrm(ks[0], (BATCH, SEQ, D_MODEL), 1.0),
        "positions": jnp.broadcast_to(jnp.arange(SEQ, dtype=jnp.int32), (BATCH, SEQ)),
        "meta_tokens": nrm(ks[1], (N_META, D_MODEL), 1.0),
        "attn_norm": 1.0 + nrm(ks[2], (DEPTH, D_MODEL), 0.02),
        "w_in": nrm(ks[3], (DEPTH, D_MODEL, W_IN_COLS), D_MODEL ** -0.5),
        "w_gate_up": nrm(ks[4], (DEPTH, GLA_RANK, GLA_QK), GLA_RANK ** -0.5),
        "b_gate": nrm(ks[5], (DEPTH, GLA_QK), 0.01),
        "ret_norm": 1.0 + nrm(ks[6], (DEPTH, RET_V), 0.02),
        "gla_norm": 1.0 + nrm(ks[7], (DEPTH, GLA_V), 0.02),
        "w_out": nrm(ks[8], (DEPTH, D_MODEL, D_MODEL), D_MODEL ** -0.5),
        "ffn_norm": 1.0 + nrm(ks[9], (DEPTH, D_MODEL), 0.02),
        "w_ffn_in": nrm(ks[10], (DEPTH, D_MODEL, 2 * D_FF), D_MODEL ** -0.5),
        "conv_w": nrm(ks[11], (DEPTH, CONV_W, D_FF), CONV_W ** -0.5),
        "conv_b": nrm(ks[12], (DEPTH, D_FF), 0.01),
        "w_ffn_out": nrm(ks[13], (DEPTH, D_FF, D_MODEL), D_FF ** -0.5),
        "final_norm": 1.0 + nrm(ks[14], (D_MODEL,), 0.02),
    }


def reference(x, positions, meta_tokens, attn_norm, w_in, w_gate_up, b_gate, ret_norm, gla_norm,
              w_out, ffn_norm, w_ffn_in, conv_w, conv_b, w_ffn_out, final_norm):
    B = x.shape[0]
    meta = jnp.broadcast_to(meta_tokens[None].astype(x.dtype), (B, N_META, D_MODEL))
    h = jnp.concatenate([meta, x], axis=1)
    pos = jnp.concatenate([jnp.broadcast_to(jnp.arange(N_META, dtype=jnp.int32), (B, N_META)),
                           positions.astype(jnp.int32) + N_META], axis=1)
    for i in range(DEPTH):
        h = h + _hybrid_mixer(_rmsnorm(h, attn_norm[i]), pos, w_in[i], w_gate_up[i], b_gate[i],
                              ret_norm[i], gla_norm[i], w_out[i])
        h = h + _conv_ffn(_rmsnorm(h, ffn_norm[i]), w_ffn_in[i], conv_w[i], conv_b[i], w_ffn_out[i])
    h = _rmsnorm(h, final_norm)
    return h[:, N_META:]
```

```python
import math
from contextlib import ExitStack
import numpy as np
import ml_dtypes
import concourse.bass as bass
import concourse.mybir as mybir
from concourse.bass_utils import run_bass_kernel_spmd

F32 = mybir.dt.float32
BF16 = mybir.dt.bfloat16
I32 = mybir.dt.int32
AF = mybir.ActivationFunctionType
ALU = mybir.AluOpType
AX = mybir.AxisListType

ENGS = ('pe', 'act', 'dve', 'pool', 'sp')
NDMA_SEM = 12
EPS = 1e-6


class Prog:
    def __init__(self, nc):
        self.nc = nc
        self.q = {e: [] for e in ENGS}
        self.seq = {e: 0 for e in ENGS}
        self.ndma = {e: 0 for e in ENGS}
        self.waited = {e: {} for e in ENGS}
        self.res = {}
        self.sems = {}
        self.latest = {}
        self.trace = {e: [] for e in ENGS}

    def _deps(self, reads, writes):
        deps = {}

        def add(k, v):
            if deps.get(k, 0) < v:
                deps[k] = v
        for r in reads:
            st = self.res.get(r)
            if st:
                if st[0] is not None:
                    add(*st[0])
        for w in writes:
            st = self.res.get(w)
            if st:
                if st[0] is not None:
                    add(*st[0])
                for k, v in st[1].items():
                    add(k, v)
        return deps

    def _update(self, tok, reads, writes):
        k, v = tok
        if self.latest.get(k, 0) < v:
            self.latest[k] = v
        for r in reads:
            st = self.res.setdefault(r, [None, {}])
            if st[1].get(k, 0) < v:
                st[1][k] = v
        for w in writes:
            self.res[w] = [tok, {}]

    def _waits(self, eng, deps, skip_self=False):
        ws = []
        for k, v in deps.items():
            if skip_self and k == eng:
                continue
            if self.waited[eng].get(k, 0) < v:
                self.waited[eng][k] = v
                ws.append((k, v))
        return ws

    def op(self, eng, fn, reads=(), writes=(), signal=True):
        deps = self._deps(reads, writes)
        ws = self._waits(eng, deps, skip_self=(eng == 'pe'))
        if signal:
            self.seq[eng] += 1
            tok = (eng, self.seq[eng])
        else:
            tok = (eng, self.seq[eng] + 1)
        sems = self.sems

        def run(e, fn=fn, ws=ws, signal=signal, eng=eng):
            for k, v in ws:
                e.wait_ge(sems[k], v)
            ins = fn(e)
            if signal:
                ins.then_inc(sems[eng], 1)
        self.q[eng].append(run)
        self.trace[eng].append((list(ws), (eng, 1) if signal else None))
        self._update(tok, reads, writes)
        return tok

    def dma(self, queue, fn, reads=(), writes=()):
        n = self.ndma[queue]
        self.ndma[queue] += 1
        idx = n % NDMA_SEM
        cnt = n // NDMA_SEM + 1
        key = ('dma', queue, idx)
        deps = self._deps(reads, writes)
        if cnt > 1 and deps.get(key, 0) < 16 * (cnt - 1):
            deps[key] = 16 * (cnt - 1)
        ws = self._waits(queue, deps)
        tok = (key, 16 * cnt)
        sems = self.sems

        def run(e, fn=fn, ws=ws, key=key):
            for k, v in ws:
                e.wait_ge(sems[k], v)
            fn(e).then_inc(sems[key], 16)
        self.q[queue].append(run)
        self.trace[queue].append((list(ws), (key, 16)))
        self._update(tok, reads, writes)
        return tok

    def barrier(self):
        lat = dict(self.latest)
        sems = self.sems
        for eng in ENGS:
            ws = self._waits(eng, lat, skip_self=False)
            ws = [(k, v) for (k, v) in ws if k != eng or eng != 'pe']

            def run(e, ws=ws):
                for k, v in ws:
                    e.wait_ge(sems[k], v)
            self.q[eng].append(run)
            self.trace[eng].append((list(ws), None))
        self.res = {}

    def check_deadlock(self):
        val = {}
        pos = {e: 0 for e in ENGS}
        tr = self.trace
        while True:
            prog = False
            for e in ENGS:
                while pos[e] < len(tr[e]):
                    ws, inc = tr[e][pos[e]]
                    if all(val.get(k, 0) >= v for k, v in ws):
                        if inc:
                            val[inc[0]] = val.get(inc[0], 0) + inc[1]
                        pos[e] += 1
                        prog = True
                    else:
                        break
            if not prog:
                break
        stuck = {e: (pos[e], len(tr[e])) for e in ENGS if pos[e] < len(tr[e])}
        if stuck:
            msg = {e: (p, n, tr[e][p][0], {k: val.get(k, 0) for k, _ in tr[e][p][0]}) for e, (p, n) in stuck.items()}
            raise RuntimeError(f"semaphore deadlock: {msg}")

    def alloc_sems(self, es):
        nc = self.nc
        for e in ('pe', 'act', 'dve', 'pool'):
            self.sems[e] = es.enter_context(nc.semaphore('s_' + e))
        for qn in ('sp', 'act', 'pool'):
            for i in range(NDMA_SEM):
                self.sems[('dma', qn, i)] = es.enter_context(nc.semaphore(f'd_{qn}_{i}'))

    def flush(self):
        nc = self.nc
        q = self.q
        with nc.Block() as block:
            @block.tensor
            def _(e):
                for f in q['pe']:
                    f(e)

            @block.scalar
            def _(e):
                for f in q['act']:
                    f(e)

            @block.vector
            def _(e):
                for f in q['dve']:
                    f(e)

            @block.gpsimd
            def _(e):
                for f in q['pool']:
                    f(e)

            @block.sync
            def _(e):
                for f in q['sp']:
                    f(e)
        self.q = {e: [] for e in ENGS}


class Cfg:
    def __init__(self, D=4096, DFF=11008, NPRE=8, NMAIN=9, pre_blocks=(4, 4), main_blocks=(5, 4)):
        self.D = D
        self.KT = D // 128
        self.RH = D // 512
        self.GH = D // 1024
        self.RDK, self.RDV, self.GDK, self.GDV, self.RANK = 256, 512, 512, 1024, 16
        self.DFF = DFF
        self.FT = DFF // 128
        self.NPRE, self.NMAIN = NPRE, NMAIN
        self.NCH = NPRE + NMAIN
        self.NS = self.NCH * 128
        self.NREAL = (NMAIN - 1) * 128
        self.pre_blocks, self.main_blocks = pre_blocks, main_blocks
        assert sum(pre_blocks) == NPRE and sum(main_blocks) == NMAIN
        RQK, RV, GQK, GV = self.RH * 256, D, self.GH * 512, D
        self.c_qr = 0
        self.c_kr = RQK
        self.c_vr = 2 * RQK
        self.c_or = self.c_vr + RV
        self.c_qg = self.c_or + RV
        self.c_kg = self.c_qg + GQK
        self.c_vg = self.c_kg + GQK
        self.c_og = self.c_vg + GV
        self.c_zg = self.c_og + GV
        self.c_mr = self.c_zg + 16
        self.c_mg = self.c_mr + D
        self.WIN = self.c_mg + D
        self.GQK = GQK
        self.MAXB = max(max(pre_blocks), max(main_blocks)) * 128


def build_nc(cfg):
    D, KT, NS, DFF, FT = cfg.D, cfg.KT, cfg.NS, cfg.DFF, cfg.FT
    NMS = cfg.NMAIN * 128
    NREAL = cfg.NREAL
    nc = bass.Bass("TRN2", target_bir_lowering=False)

    def din(name, shape, dt=F32):
        return nc.dram_tensor(name, list(shape), dt, kind="ExternalInput").ap()

    def dscr(name, shape, dt):
        return nc.dram_tensor(name, list(shape), dt, kind="Internal").ap()
    xs = din("xs", [NS, D])
    posr = din("posr", [1, NS], I32)
    attn_norm = din("attn_norm", [1, D])
    w_in = din("w_in", [D, cfg.WIN])
    w_gate = din("w_gate", [17, cfg.GQK])
    ret_norm = din("ret_norm", [1, D])
    gla_norm = din("gla_norm", [1, D])
    w_out = din("w_out", [D, D])
    ffn_norm = din("ffn_norm", [1, D])
    w_ffn_in = din("w_ffn_in", [D, 2 * DFF])
    conv_w = din("conv_w", [3, DFF])
    conv_b = din("conv_b", [1, DFF])
    w_ffn_out = din("w_ffn_out", [DFF, D])
    final_norm = din("final_norm", [1, D])
    c_ident_bf = din("c_ident_bf", [128, 128], BF16)
    c_ident_f = din("c_ident_f", [128, 128])
    c_invf = din("c_invf", [128, 1])
    c_rdk = din("c_rdk", [cfg.RH, 256])
    c_rdq = din("c_rdq", [cfg.RH, 256])
    c_rmask = din("c_rmask", [cfg.RH, 128, 128])
    c_gmask = din("c_gmask", [128, 128])
    c_tri = din("c_tri", [128, 128])
    out = nc.dram_tensor("out", [NREAL, D], F32, kind="ExternalOutput").ap()

    hnT = dscr("hnT", [KT, 128, NS], BF16)
    MR = dscr("MR", [NMS, D], BF16)
    MG = dscr("MG", [NMS, D], BF16)
    H1 = dscr("H1", [NMS, D], F32)
    FO = dscr("FO", [NREAL, D], F32)
    ACTS = dscr("ACTS", [FT, 128, NREAL], BF16)
    cosD = dscr("cosD", [128, NS], F32)
    sinD = dscr("sinD", [128, NS], F32)

    P = Prog(nc)
    gam = [1.0 - 2.0 ** (-5.0 - h) for h in range(cfg.RH)]
    PRE0 = cfg.NPRE * 128

    with ExitStack() as top:
        P.alloc_sems(top)

        def bcast_row(ap_row, n):
            return ap_row.partition_broadcast(n)

        with ExitStack() as es:
            def sb(name, shape, dt):
                return es.enter_context(nc.sbuf_tensor(name, list(shape), dt))

            def psm(name, shape, dt):
                return es.enter_context(nc.psum_tensor(name, list(shape), dt))
            ident = sb("s0_ident", [128, 128], BF16)
            gB = sb("s0_gB", [128, D], F32)
            xt = [sb(f"s0_xt{i}", [128, D], F32) for i in range(2)]
            yb = [sb(f"s0_yb{i}", [128, D], BF16) for i in range(2)]
            junk = sb("s0_junk", [128, D], BF16)
            ss = sb("s0_ss", [128, 2 * cfg.NCH], F32)
            hT = [sb(f"s0_hT{i}", [128, KT, 128], BF16) for i in range(2)]
            pt = [psm(f"s0_pt{i}", [128, 8, 128], BF16) for i in range(4)]
            posi = sb("s0_posi", [128, NS], I32)
            sinT = sb("s0_sin", [128, NS], F32)
            cosT = sb("s0_cos", [128, NS], F32)
            invf = sb("s0_invf", [128, 1], F32)
            P.dma('sp', lambda e: e.dma_start(out=ident[:], in_=c_ident_bf), writes=['ident'])
            P.dma('sp', lambda e: e.dma_start(out=gB[:], in_=bcast_row(attn_norm[0:1, :], 128)), writes=['gB'])
            P.dma('sp', lambda e: e.dma_start(out=invf[:], in_=c_invf), writes=['invf'])
            P.dma('sp', lambda e: e.dma_start(out=posi[:], in_=bcast_row(posr[0:1, :], 128)), writes=['posi'])
            P.op('dve', lambda e: e.tensor_scalar(out=sinT[:], in0=posi[:], scalar1=16.0, scalar2=None, op0=ALU.add),
                 reads=['posi'], writes=['sinT'])
            P.op('dve', lambda e: e.tensor_scalar(out=sinT[:], in0=sinT[:], scalar1=invf[:, 0:1], scalar2=None, op0=ALU.mult),
                 reads=['sinT', 'invf'], writes=['sinT'])
            ki = sb("s0_ki", [128, NS], I32)
            kf = sb("s0_kf", [128, NS], F32)
            P.op('dve', lambda e: e.tensor_scalar(out=cosT[:], in0=sinT[:], scalar1=0.5 * math.pi, scalar2=None, op0=ALU.add),
                 reads=['sinT'], writes=['cosT'])
            for nm, tt_ in (('sinT', sinT), ('cosT', cosT)):
                P.op('dve', lambda e, tt_=tt_: e.tensor_scalar(out=kf[:], in0=tt_[:], scalar1=1.0 / (2 * math.pi), scalar2=None, op0=ALU.mult),
                     reads=[nm], writes=['kf'])
                P.op('dve', lambda e: e.tensor_copy(out=ki[:], in_=kf[:]), reads=['kf'], writes=['ki'])
                P.op('dve', lambda e: e.tensor_copy(out=kf[:], in_=ki[:]), reads=['ki'], writes=['kf'])
                P.op('dve', lambda e, tt_=tt_: e.scalar_tensor_tensor(out=tt_[:], in0=kf[:], scalar=-2 * math.pi, in1=tt_[:], op0=ALU.mult, op1=ALU.add),
                     reads=['kf', nm], writes=[nm])
                P.op('dve', lambda e, tt_=tt_: e.tensor_scalar(out=kf[:], in0=tt_[:], scalar1=math.pi, scalar2=2 * math.pi, op0=ALU.is_gt, op1=ALU.mult),
                     reads=[nm], writes=['kf'])
                P.op('dve', lambda e, tt_=tt_: e.tensor_tensor(out=tt_[:], in0=tt_[:], in1=kf[:], op=ALU.subtract),
                     reads=[nm, 'kf'], writes=[nm])
            P.op('act', lambda e: e.activation(out=sinT[:], in_=sinT[:], func=AF.Sin), reads=['sinT'], writes=['sinT'])
            P.op('act', lambda e: e.activation(out=cosT[:], in_=cosT[:], func=AF.Sin), reads=['cosT'], writes=['cosT'])
            P.dma('sp', lambda e: e.dma_start(out=sinD, in_=sinT[:]), reads=['sinT'], writes=['sinD'])
            P.dma('sp', lambda e: e.dma_start(out=cosD, in_=cosT[:]), reads=['cosT'], writes=['cosD'])
            P.op('dve', lambda e: e.memset(ss[:], 0.0), writes=['ss'])
            epsT = sb("s0_eps", [128, 1], F32)
            P.op('dve', lambda e: e.memset(epsT[:], EPS), writes=['epsT'])
            for t in range(cfg.NCH):
                b = t % 2
                P.dma('sp', lambda e, t=t, b=b: e.dma_start(out=xt[b][:], in_=xs[t * 128:(t + 1) * 128, :]),
                      writes=[('xt', b)])
                P.op('act', lambda e, t=t, b=b: e.activation(out=junk[:], in_=xt[b][:], func=AF.Square,
                                                             accum_out=ss[:, 2 * t:2 * t + 1]),
                     reads=[('xt', b)], writes=['junk', 'ss'])
                P.op('act', lambda e, t=t: e.activation(out=ss[:, 2 * t + 1:2 * t + 2], in_=ss[:, 2 * t:2 * t + 1], func=AF.Sqrt,
                                                        bias=epsT[:, 0:1], scale=1.0 / D), reads=['ss', 'epsT'], writes=['ss'])
                P.op('dve', lambda e, t=t: e.reciprocal(out=ss[:, 2 * t + 1:2 * t + 2], in_=ss[:, 2 * t + 1:2 * t + 2]),
                     reads=['ss'], writes=['ss'])
                P.op('dve', lambda e, t=t, b=b: e.scalar_tensor_tensor(out=yb[b][:], in0=xt[b][:],
                                                                       scalar=ss[:, 2 * t + 1:2 * t + 2], in1=gB[:],
                                                                       op0=ALU.mult, op1=ALU.mult),
                     reads=[('xt', b), 'ss', 'gB'], writes=[('yb', b)])
                for g4 in range(KT // 8):
                    pi = (t * (KT // 8) + g4) % 4
                    for j in range(8):
                        kt = g4 * 8 + j
                        P.op('pe', lambda e, kt=kt, j=j, pi=pi, b=b: e.transpose(pt[pi][:, j, :], yb[b][:, kt * 128:(kt + 1) * 128], ident[:]),
                             reads=[('yb', b), 'ident'], writes=[('pt', pi)], signal=(j == 7))
                    P.op('act', lambda e, g4=g4, pi=pi, b=b: e.activation(out=hT[b][:, g4 * 8:(g4 + 1) * 8, :], in_=pt[pi][:], func=AF.Copy),
                         reads=[('pt', pi)], writes=[('hT', b)])
                for g2 in range(0, KT, 8):
                    P.dma('sp', lambda e, t=t, b=b, g2=g2: e.dma_start(
                        out=hnT[g2:g2 + 8, :, t * 128:(t + 1) * 128].rearrange("k p n -> p k n"),
                        in_=hT[b][:, g2:g2 + 8, :]), reads=[('hT', b)], writes=[('hnT', t, g2)])
            P.barrier()
            P.flush()

        with ExitStack() as es:
            def sb(name, shape, dt):
                return es.enter_context(nc.sbuf_tensor(name, list(shape), dt))

            def psm(name, shape, dt):
                return es.enter_context(nc.psum_tensor(name, list(shape), dt))
            MAXB = cfg.MAXB
            NTB = MAXB // 128
            ident = sb("p1_ident", [128, 128], BF16)
            cosT = sb("p1_cos", [128, MAXB], F32)
            sinT = sb("p1_sin", [128, MAXB], F32)
            yr = sb("p1_yr", [128, NTB, 1024], BF16)
            tg = sb("p1_tg", [128, 256], F32)
            NW = 3
            WCOLS = 256
            wb = [sb(f"p1_w{i}", [128, KT * WCOLS], BF16) for i in range(NW)]
            hnb = sb("p1_hnb", [128, KT, MAXB], BF16)
            qT = sb("p1_qT", [128, 4, MAXB], BF16)
            PREB = max(cfg.pre_blocks)
            kTs = [sb("p1_kT", [128, 4, MAXB], BF16), sb("p1_kT1", [128, 4, PREB * 128], BF16)]
            vtoks = [sb("p1_vtok", [128, NTB, 1024], BF16), sb("p1_vtok1", [128, PREB, 1024], BF16)]
            yn = sb("p1_yn", [128, NTB, 1024], BF16)
            S = sb("p1_S", [128, 4, 1024], F32)
            Sbf = sb("p1_Sbf", [128, 4, 1024], BF16)
            tabk = sb("p1_tabk", [128, 256], F32)
            tabq = sb("p1_tabq", [128, 256], F32)
            maskT = sb("p1_mask", [128, 128], F32)
            gmask = sb("p1_gmask", [128, 128], F32)
            tri = sb("p1_tri", [128, 128], F32)
            gnB = sb("p1_gnB", [128, 1024], F32)
            wg = sb("p1_wg", [17, 512], BF16)
            zTs = [sb("p1_zT", [17, MAXB], BF16), sb("p1_zT1", [17, PREB * 128], BF16)]
            t1 = sb("p1_t1", [128, 512], F32)
            t2 = sb("p1_t2", [128, 512], F32)
            t3 = t1
            t4 = t2
            khT = [sb(f"p1_khT{i}", [128, 4, 128], BF16) for i in range(2)]
            khat = [sb(f"p1_khat{i}", [128, 512], BF16) for i in range(2)]
            qd = [sb(f"p1_qd{i}", [128, 4, 128], BF16) for i in range(2)]
            kd = [sb(f"p1_kd{i}", [128, 4, 128], BF16) for i in range(2)]
            sT = [sb(f"p1_sT{i}", [128, 128], BF16) for i in range(2)]
            spf = [sb(f"p1_spf{i}", [128, 512], F32) for i in range(2)]
            eb = [sb(f"p1_eb{i}", [128, 4, 128], F32) for i in range(2)]
            enb = [sb(f"p1_enb{i}", [128, 4, 128], F32) for i in range(2)]
            st = sb("p1_st", [128, 8], F32)
            pA = [psm(f"p1_pA{i}", [128, 512], F32) for i in range(4)]
            pB = [psm(f"p1_pB{i}", [128, 512], F32) for i in range(2)]
            pC = psm("p1_pC", [128, 512], F32)
            pT = psm("p1_pT", [128, 4, 128], BF16)
            pa_ctr = [0]
            epsT = sb("p1_eps", [128, 1], F32)
            oneT = sb("p1_one", [128, 1], F32)
            P.op('dve', lambda e: e.memset(epsT[:], EPS), writes=['epsT'])
            P.op('dve', lambda e: e.memset(oneT[:], 1.0), writes=['oneT'])

            def next_pA():
                i = pa_ctr[0] % 4
                pa_ctr[0] += 1
                return i

            P.dma('sp', lambda e: e.dma_start(out=ident[:], in_=c_ident_bf), writes=['ident'])
            P.dma('sp', lambda e: e.dma_start(out=gmask[:], in_=c_gmask), writes=['gmask'])
            P.dma('sp', lambda e: e.dma_start(out=tri[:], in_=c_tri), writes=['tri'])

            wctr = [0]

            def wload(c0, ncols):
                s = wctr[0] % NW
                wctr[0] += 1
                view = wb[s][:, 0:KT * ncols].rearrange("p (k c) -> p k c", k=KT)
                step = 8
                for k0 in range(0, KT, step):
                    P.dma('pool', lambda e, k0=k0, view=view: e.dma_start(
                        out=view[:, k0:k0 + step, :],
                        in_=w_in[k0 * 128:(k0 + step) * 128, c0:c0 + ncols].rearrange("(k p) c -> p k c", p=128)),
                        writes=[('w', s, k0 // step)])
                return view, (lambda kt, s=s: ('w', s, kt // 8))

            def blocks():
                s0 = 0
                for nb in cfg.pre_blocks:
                    yield (False, s0, nb)
                    s0 += nb * 128
                for nb in cfg.main_blocks:
                    yield (True, s0, nb)
                    s0 += nb * 128

            def pieces(nb):
                res, c = [], 0
                while c < nb:
                    n = min(4, nb - c)
                    if nb - c > 4 and nb - c < 8:
                        n = (nb - c + 1) // 2
                    res.append((c * 128, n * 128))
                    c += n
                return res

            def load_hn(s0, ntok):
                for k0 in range(0, KT, 8):
                    P.dma('act', lambda e, k0=k0: e.dma_start(
                        out=hnb[:, k0:k0 + 8, 0:ntok],
                        in_=hnT[k0:k0 + 8, :, s0:s0 + ntok].rearrange("k p n -> p k n")),
                        writes=[('hnb', k0 // 8)])

            def fproj(c0, ncols, ntok, evac):
                for cc in range(0, ncols, WCOLS):
                    nc_ = min(WCOLS, ncols - cc)
                    view, wres = wload(c0 + cc, nc_)
                    for j in range((nc_ + 127) // 128):
                        m = min(128, nc_ - j * 128)
                        for (po, pn) in pieces(ntok // 128):
                            pi = next_pA()
                            for kt in range(KT):
                                P.op('pe', lambda e, pi=pi, view=view, j=j, m=m, kt=kt, po=po, pn=pn: e.matmul(
                                    pA[pi][0:m, 0:pn], lhsT=view[:, kt, j * 128:j * 128 + m], rhs=hnb[:, kt, po:po + pn],
                                    start=(kt == 0), stop=(kt == KT - 1)),
                                    reads=[wres(kt), ('hnb', kt // 8)], writes=[('pA', pi)], signal=(kt == KT - 1))
                                if kt == KT // 2 - 1:
                                    yield
                            evac(cc // 128 + j, po, pn, pi)
                            yield

            def tproj(c0, ncols, ntok, evac):
                for cc in range(0, ncols, WCOLS):
                    nv = min(WCOLS, ncols - cc)
                    view, wres = wload(c0 + cc, nv)
                    for tt in range(ntok // 128):
                        pi = next_pA()
                        for kt in range(KT):
                            P.op('pe', lambda e, pi=pi, view=view, nv=nv, kt=kt, tt=tt: e.matmul(
                                pA[pi][:, 0:nv], lhsT=hnb[:, kt, tt * 128:(tt + 1) * 128], rhs=view[:, kt, :],
                                start=(kt == 0), stop=(kt == KT - 1)),
                                reads=[wres(kt), ('hnb', kt // 8)], writes=[('pA', pi)], signal=(kt == KT - 1))
                            if kt == KT // 2 - 1:
                                yield
                        evac(tt, cc, nv, pi)
                        yield

            def drain(gen):
                for _ in gen:
                    pass

            def interleave(ga, gb):
                a_done = b_done = False
                while not (a_done and b_done):
                    if not a_done:
                        try:
                            next(ga)
                        except StopIteration:
                            a_done = True
                    if not b_done:
                        try:
                            next(gb)
                        except StopIteration:
                            b_done = True

            def gates(u, ntok, dv, c_o, c_m):
                ch0 = u * dv

                def ev_o(tt, co, ncp, pi):
                    P.op('act', lambda e: e.activation(out=tg[:, 0:ncp], in_=pA[pi][:, 0:ncp], func=AF.Exp, scale=-1.0),
                         reads=[('pA', pi)], writes=['tg'])
                    P.op('act', lambda e: e.activation(out=tg[:, 0:ncp], in_=tg[:, 0:ncp], func=AF.Ln, bias=oneT[:, 0:1]), reads=['tg', 'oneT'], writes=['tg'])
                    P.op('act', lambda e: e.activation(out=tg[:, 0:ncp], in_=tg[:, 0:ncp], func=AF.Exp, scale=-1.0), reads=['tg'], writes=['tg'])
                    P.op('dve', lambda e: e.tensor_tensor(out=tg[:, 0:ncp], in0=pA[pi][:, 0:ncp], in1=tg[:, 0:ncp], op=ALU.mult),
                         reads=[('pA', pi), 'tg'], writes=['tg'])
                    P.op('dve', lambda e: e.tensor_tensor(out=yn[:, tt, co:co + ncp], in0=tg[:, 0:ncp], in1=gnB[:, co:co + ncp], op=ALU.mult),
                         reads=['tg', 'gnB'], writes=[('yn', tt)])
                yield from tproj(c_o + ch0, dv, ntok, ev_o)

                def ev_m(tt, co, ncp, pi):
                    P.op('act', lambda e: e.activation(out=tg[:, 0:ncp], in_=pA[pi][:, 0:ncp], func=AF.Exp, scale=-1.0),
                         reads=[('pA', pi)], writes=['tg'])
                    P.op('act', lambda e: e.activation(out=tg[:, 0:ncp], in_=tg[:, 0:ncp], func=AF.Ln, bias=oneT[:, 0:1]), reads=['tg', 'oneT'], writes=['tg'])
                    P.op('act', lambda e: e.activation(out=tg[:, 0:ncp], in_=tg[:, 0:ncp], func=AF.Exp, scale=-1.0), reads=['tg'], writes=['tg'])
                    P.op('dve', lambda e: e.tensor_tensor(out=yn[:, tt, co:co + ncp], in0=yn[:, tt, co:co + ncp], in1=tg[:, 0:ncp], op=ALU.mult),
                         reads=[('yn', tt), 'tg'], writes=[('yn', tt)])
                yield from tproj(c_m + ch0, dv, ntok, ev_m)

            def store_y(is_ret, u, s0, c, dv, dst):
                m0 = s0 - PRE0
                ch0 = u * dv
                P.op('dve', lambda e: e.tensor_tensor(out=yn[:, c, 0:dv], in0=yr[:, c, 0:dv], in1=yn[:, c, 0:dv], op=ALU.mult),
                     reads=[('yr', c), ('yn', c)], writes=[('yn', c)])
                P.dma('sp', lambda e: e.dma_start(out=dst[m0 + c * 128:m0 + (c + 1) * 128, ch0:ch0 + dv], in_=yn[:, c, 0:dv]),
                      reads=[('yn', c)], writes=[('dst', is_ret, u, s0, c)])

            def copy_evac(dstT, nm, scale=1.0):
                def ev(j, po, pn, pi):
                    P.op('act', lambda e: e.activation(out=dstT[:, j, po:po + pn], in_=pA[pi][:, 0:pn], func=AF.Copy, scale=scale),
                         reads=[('pA', pi)], writes=[nm])
                return ev

            def v_evac_to(vt, bs):
                def v_evac(tt, co, ncp, pi):
                    P.op('act', lambda e: e.activation(out=vt[:, tt, co:co + ncp], in_=pA[pi][:, 0:ncp], func=AF.Copy),
                         reads=[('pA', pi)], writes=[('vtok', bs)])
                return v_evac

            def skewed(nb, pre, post):
                yield from pre(0)
                for c in range(nb):
                    if c + 1 < nb:
                        yield from pre(c + 1)
                    yield from post(c)

            blks = list(blocks())

            def bufset(bi):
                is_main, _, _ = blks[bi]
                return 0 if is_main else bi % 2

            def run_unit(gemm, scan, gates_fn, finalize):
                drain(gemm(0))
                for bi in range(len(blks)):
                    is_main = blks[bi][0]
                    nxt = bi + 1 < len(blks)
                    if not is_main:
                        if nxt:
                            interleave(scan(bi), gemm(bi + 1))
                        else:
                            drain(scan(bi))
                    else:
                        interleave(scan(bi), gates_fn(bi))
                        finalize(bi)
                        if nxt:
                            drain(gemm(bi + 1))

            def ret_unit(h):
                P.dma('sp', lambda e: e.dma_start(out=tabk[:], in_=bcast_row(c_rdk[h:h + 1, :], 128)), writes=['tabk'])
                P.dma('sp', lambda e: e.dma_start(out=tabq[:], in_=bcast_row(c_rdq[h:h + 1, :], 128)), writes=['tabq'])
                P.dma('sp', lambda e: e.dma_start(out=maskT[:], in_=c_rmask[h]), writes=['maskT'])
                P.dma('sp', lambda e: e.dma_start(out=gnB[:, 0:512], in_=bcast_row(ret_norm[0:1, h * 512:(h + 1) * 512], 128)), writes=['gnB'])
                P.op('dve', lambda e: e.memset(S[:, 0:2, 0:512], 0.0), writes=['S'])
                P.op('dve', lambda e: e.memset(Sbf[:, 0:2, 0:512], 0.0), writes=['Sbf'])
                gC = gam[h] ** 128

                def rope_evac(dstT, nm):
                    hold = {}

                    def ev(j, po, pn, pi):
                        hold[(j, po)] = pi
                        if j == 0:
                            return
                        p1, p2 = hold[(0, po)], pi
                        cs, sn = cosT[:, po:po + pn], sinT[:, po:po + pn]
                        P.op('dve', lambda e: e.tensor_tensor(out=t1[:, 0:pn], in0=pA[p1][:, 0:pn], in1=cs, op=ALU.mult),
                             reads=[('pA', p1), 'cosT'], writes=['t1'])
                        P.op('dve', lambda e: e.tensor_tensor(out=t2[:, 0:pn], in0=pA[p2][:, 0:pn], in1=sn, op=ALU.mult),
                             reads=[('pA', p2), 'sinT'], writes=['t2'])
                        P.op('dve', lambda e: e.tensor_tensor(out=dstT[:, 0, po:po + pn], in0=t1[:, 0:pn], in1=t2[:, 0:pn], op=ALU.subtract),
                             reads=['t1', 't2'], writes=[nm])
                        P.op('dve', lambda e: e.tensor_tensor(out=t1[:, 0:pn], in0=pA[p2][:, 0:pn], in1=cs, op=ALU.mult),
                             reads=[('pA', p2), 'cosT'], writes=['t1'])
                        P.op('dve', lambda e: e.tensor_tensor(out=t2[:, 0:pn], in0=pA[p1][:, 0:pn], in1=sn, op=ALU.mult),
                             reads=[('pA', p1), 'sinT'], writes=['t2'])
                        P.op('dve', lambda e: e.tensor_tensor(out=dstT[:, 1, po:po + pn], in0=t1[:, 0:pn], in1=t2[:, 0:pn], op=ALU.add),
                             reads=['t1', 't2'], writes=[nm])
                    return ev

                def gemm(bi):
                    is_main, s0, nb = blks[bi]
                    bs = bufset(bi)
                    ntok = nb * 128
                    load_hn(s0, ntok)
                    P.dma('act', lambda e: e.dma_start(out=sinT[:, 0:ntok], in_=sinD[:, s0:s0 + ntok]), writes=['sinT'])
                    P.dma('act', lambda e: e.dma_start(out=cosT[:, 0:ntok], in_=cosD[:, s0:s0 + ntok]), writes=['cosT'])
                    if is_main:
                        yield from fproj(cfg.c_qr + h * 256, 256, ntok, rope_evac(qT, 'qT'))
                    yield from fproj(cfg.c_kr + h * 256, 256, ntok, rope_evac(kTs[bs], ('kT', bs)))
                    yield from tproj(cfg.c_vr + h * 512, 512, ntok, v_evac_to(vtoks[bs], bs))

                def scan(bi):
                    is_main, s0, nb = blks[bi]
                    bs = bufset(bi)
                    kT, vtok = kTs[bs], vtoks[bs]

                    def pre(c):
                        o = c * 128
                        b = c % 2
                        P.op('dve', lambda e: e.tensor_tensor(out=khT[b][:, 0:2, :], in0=kT[:, 0:2, o:o + 128],
                                                              in1=tabk[:].rearrange("p (j n) -> p j n", j=2), op=ALU.mult),
                             reads=[('kT', bs), 'tabk'], writes=[('khT', b)])
                        for j in range(2):
                            P.op('pe', lambda e, j=j: e.transpose(pT[:, j, :], khT[b][:, j, :], ident[:]),
                                 reads=[('khT', b), 'ident'], writes=['pT'], signal=(j == 1))
                        P.op('act', lambda e: e.activation(out=khat[b][:, 0:256], in_=pT[:, 0:2, :].rearrange("p j n -> p (j n)"), func=AF.Copy),
                             reads=['pT'], writes=[('khat', b)])
                        yield
                        if is_main:
                            for j in range(2):
                                P.op('pe', lambda e, j=j: e.matmul(pC[:, 0:128], lhsT=kT[:, j, o:o + 128], rhs=qT[:, j, o:o + 128],
                                                                   start=(j == 0), stop=(j == 1)),
                                     reads=[('kT', bs), 'qT'], writes=['pC'], signal=(j == 1))
                            P.op('dve', lambda e: e.tensor_tensor(out=sT[b][:], in0=pC[:, 0:128], in1=maskT[:], op=ALU.mult),
                                 reads=['pC', 'maskT'], writes=[('sT', b)])
                            P.op('dve', lambda e: e.tensor_tensor(out=qd[b][:, 0:2, :], in0=qT[:, 0:2, o:o + 128],
                                                                  in1=tabq[:].rearrange("p (j n) -> p j n", j=2), op=ALU.mult),
                                 reads=['qT', 'tabq'], writes=[('qd', b)])
                            yield

                    def post(c):
                        b = c % 2
                        if is_main:
                            P.op('pe', lambda e: e.matmul(pB[0][:, :], lhsT=sT[b][:], rhs=vtok[:, c, 0:512], start=True, stop=False),
                                 reads=[('sT', b), ('vtok', bs)], writes=[('pB', 0)], signal=False)
                            for j in range(2):
                                P.op('pe', lambda e, j=j: e.matmul(pB[0][:, :], lhsT=qd[b][:, j, :], rhs=Sbf[:, j, 0:512], start=False, stop=(j == 1)),
                                     reads=[('qd', b), 'Sbf'], writes=[('pB', 0)], signal=(j == 1))
                            P.op('dve', lambda e: e.bn_stats(out=st[:, 0:6], in_=pB[0][:, :]), reads=[('pB', 0)], writes=['st'])
                            P.op('dve', lambda e: e.bn_aggr(out=st[:, 6:8], in_=st[:, 0:6]), reads=['st'], writes=['st'])
                            P.op('act', lambda e: e.activation(out=st[:, 7:8], in_=st[:, 7:8], func=AF.Ln, bias=epsT[:, 0:1], scale=1.0),
                                 reads=['st', 'epsT'], writes=['st'])
                            P.op('act', lambda e: e.activation(out=st[:, 7:8], in_=st[:, 7:8], func=AF.Exp, scale=-0.5), reads=['st'], writes=['st'])
                            P.op('dve', lambda e: e.tensor_scalar(out=yr[:, c, 0:512], in0=pB[0][:, :], scalar1=st[:, 6:7], scalar2=st[:, 7:8],
                                                                  op0=ALU.subtract, op1=ALU.mult),
                                 reads=[('pB', 0), 'st'], writes=[('yr', c)])
                            yield
                        for j in range(2):
                            P.op('pe', lambda e, j=j: e.matmul(pB[1][:, :], lhsT=khat[b][:, j * 128:(j + 1) * 128], rhs=vtok[:, c, 0:512], start=True, stop=True),
                                 reads=[('khat', b), ('vtok', bs)], writes=[('pB', 1)])
                            P.op('dve', lambda e, j=j: e.scalar_tensor_tensor(out=S[:, j, 0:512], in0=S[:, j, 0:512], scalar=gC, in1=pB[1][:, :],
                                                                              op0=ALU.mult, op1=ALU.add),
                                 reads=['S', ('pB', 1)], writes=['S'])
                            P.op('pool', lambda e, j=j: e.tensor_copy(out=Sbf[:, j, 0:512], in_=S[:, j, 0:512]),
                                 reads=['S'], writes=['Sbf'])
                            yield
                    yield from skewed(nb, pre, post)

                def gates_fn(bi):
                    _, s0, nb = blks[bi]
                    yield from gates(h, nb * 128, 512, cfg.c_or, cfg.c_mr)

                def finalize(bi):
                    _, s0, nb = blks[bi]
                    for c in range(nb):
                        store_y(True, h, s0, c, 512, MR)
                run_unit(gemm, scan, gates_fn, finalize)

            def gla_unit(g):
                P.dma('sp', lambda e: e.dma_start(out=gnB[:, 0:1024], in_=bcast_row(gla_norm[0:1, g * 1024:(g + 1) * 1024], 128)), writes=['gnB'])
                P.dma('pool', lambda e: e.dma_start(out=wg[:], in_=w_gate[:, g * 512:(g + 1) * 512]), writes=['wg'])
                P.op('dve', lambda e: e.memset(S[:], 0.0), writes=['S'])
                P.op('dve', lambda e: e.memset(Sbf[:], 0.0), writes=['Sbf'])
                for i_ in range(2):
                    P.op('dve', lambda e, i_=i_: e.memset(zTs[i_][:], 1.0), writes=[('zT', i_)])

                def gemm(bi):
                    is_main, s0, nb = blks[bi]
                    bs = bufset(bi)
                    ntok = nb * 128
                    load_hn(s0, ntok)
                    if is_main:
                        yield from fproj(cfg.c_qg + g * 512, 512, ntok, copy_evac(qT, 'qT', scale=512 ** -0.5))
                    yield from fproj(cfg.c_kg + g * 512, 512, ntok, copy_evac(kTs[bs], ('kT', bs)))

                    def z_evac(j, po, pn, pi):
                        P.op('act', lambda e: e.activation(out=zTs[bs][0:16, po:po + pn], in_=pA[pi][0:16, 0:pn], func=AF.Copy),
                             reads=[('pA', pi)], writes=[('zT', bs)])
                    yield from fproj(cfg.c_zg, 16, ntok, z_evac)
                    yield from tproj(cfg.c_vg + g * 1024, 1024, ntok, v_evac_to(vtoks[bs], bs))

                def scan(bi):
                    is_main, s0, nb = blks[bi]
                    bs = bufset(bi)
                    kT, vtok, zT = kTs[bs], vtoks[bs], zTs[bs]

                    def pre(c):
                        o = c * 128
                        b = c % 2
                        P.op('pe', lambda e: e.matmul(pC[:, :], lhsT=zT[:, o:o + 128], rhs=wg[:, :], start=True, stop=True),
                             reads=[('zT', bs), 'wg'], writes=['pC'])
                        yield
                        P.op('act', lambda e: e.activation(out=spf[b][:], in_=pC[:, :], func=AF.Exp, scale=-1.0), reads=['pC'], writes=[('spf', b)])
                        P.op('act', lambda e: e.activation(out=spf[b][:], in_=spf[b][:], func=AF.Ln, bias=oneT[:, 0:1]), reads=[('spf', b), 'oneT'], writes=[('spf', b)])
                        for j in range(4):
                            P.op('pe', lambda e, j=j: e.matmul(pC[:, j * 128:(j + 1) * 128], lhsT=spf[b][:, j * 128:(j + 1) * 128], rhs=tri[:, :], start=True, stop=True),
                                 reads=[('spf', b), 'tri'], writes=['pC'], signal=(j == 3))
                        yield
                        P.op('act', lambda e: e.activation(out=eb[b][:].rearrange("p j n -> p (j n)"), in_=pC[:, :], func=AF.Exp), reads=['pC'], writes=[('eb', b)])
                        P.op('act', lambda e: e.activation(out=enb[b][:].rearrange("p j n -> p (j n)"), in_=pC[:, :], func=AF.Exp, scale=-1.0), reads=['pC'], writes=[('enb', b)])
                        P.op('dve', lambda e: e.tensor_tensor(out=kd[b][:], in0=kT[:, :, o:o + 128], in1=enb[b][:], op=ALU.mult),
                             reads=[('kT', bs), ('enb', b)], writes=[('kd', b)])
                        for j in range(4):
                            P.op('dve', lambda e, j=j: e.tensor_scalar(out=khT[b][:, j, :], in0=kd[b][:, j, :], scalar1=eb[b][:, j, 127:128], scalar2=None, op0=ALU.mult),
                                 reads=[('kd', b), ('eb', b)], writes=[('khT', b)])
                        for j in range(4):
                            P.op('pe', lambda e, j=j: e.transpose(pT[:, j, :], khT[b][:, j, :], ident[:]),
                                 reads=[('khT', b), 'ident'], writes=['pT'], signal=(j == 3))
                        P.op('act', lambda e: e.activation(out=khat[b][:], in_=pT[:].rearrange("p j n -> p (j n)"), func=AF.Copy),
                             reads=['pT'], writes=[('khat', b)])
                        yield
                        if is_main:
                            P.op('dve', lambda e: e.tensor_tensor(out=qd[b][:], in0=qT[:, :, o:o + 128], in1=eb[b][:], op=ALU.mult),
                                 reads=['qT', ('eb', b)], writes=[('qd', b)])
                            for j in range(4):
                                P.op('pe', lambda e, j=j: e.matmul(pC[:, 0:128], lhsT=kd[b][:, j, :], rhs=qd[b][:, j, :], start=(j == 0), stop=(j == 3)),
                                     reads=[('kd', b), ('qd', b)], writes=['pC'], signal=(j == 3))
                            P.op('dve', lambda e: e.tensor_tensor(out=sT[b][:], in0=pC[:, 0:128], in1=gmask[:], op=ALU.mult),
                                 reads=['pC', 'gmask'], writes=[('sT', b)])
                            yield

                    def post(c):
                        b = c % 2
                        if is_main:
                            P.op('dve', lambda e: e.memset(st[:, 0:2], 0.0), writes=['st'])
                            for vh in range(2):
                                P.op('pe', lambda e, vh=vh: e.matmul(pB[vh][:, :], lhsT=sT[b][:], rhs=vtok[:, c, vh * 512:(vh + 1) * 512], start=True, stop=False),
                                     reads=[('sT', b), ('vtok', bs)], writes=[('pB', vh)], signal=False)
                                for j in range(4):
                                    P.op('pe', lambda e, j=j, vh=vh: e.matmul(pB[vh][:, :], lhsT=qd[b][:, j, :], rhs=Sbf[:, j, vh * 512:(vh + 1) * 512], start=False, stop=(j == 3)),
                                         reads=[('qd', b), 'Sbf'], writes=[('pB', vh)], signal=(j == 3))
                                P.op('act', lambda e, vh=vh: e.activation(out=t1[:], in_=pB[vh][:, :], func=AF.Square, accum_out=st[:, vh:vh + 1]),
                                     reads=[('pB', vh)], writes=['t1', 'st'])
                            P.op('dve', lambda e: e.tensor_tensor(out=st[:, 2:3], in0=st[:, 0:1], in1=st[:, 1:2], op=ALU.add), reads=['st'], writes=['st'])
                            P.op('act', lambda e: e.activation(out=st[:, 2:3], in_=st[:, 2:3], func=AF.Ln, bias=epsT[:, 0:1], scale=1.0 / 1024),
                                 reads=['st', 'epsT'], writes=['st'])
                            P.op('act', lambda e: e.activation(out=st[:, 2:3], in_=st[:, 2:3], func=AF.Exp, scale=-0.5), reads=['st'], writes=['st'])
                            for vh in range(2):
                                P.op('act', lambda e, vh=vh: e.activation(out=yr[:, c, vh * 512:(vh + 1) * 512], in_=pB[vh][:, :], func=AF.Copy, scale=st[:, 2:3]),
                                     reads=[('pB', vh), 'st'], writes=[('yr', c)])
                            yield
                        for j in range(4):
                            for vh in range(2):
                                P.op('pe', lambda e, j=j, vh=vh: e.matmul(pB[vh][:, :], lhsT=khat[b][:, j * 128:(j + 1) * 128], rhs=vtok[:, c, vh * 512:(vh + 1) * 512], start=True, stop=True),
                                     reads=[('khat', b), ('vtok', bs)], writes=[('pB', vh)])
                                P.op('dve', lambda e, j=j, vh=vh: e.scalar_tensor_tensor(out=S[:, j, vh * 512:(vh + 1) * 512], in0=S[:, j, vh * 512:(vh + 1) * 512],
                                                                                          scalar=eb[b][:, j, 127:128], in1=pB[vh][:, :], op0=ALU.mult, op1=ALU.add),
                                     reads=['S', ('pB', vh), ('eb', b)], writes=['S'])
                            P.op('pool', lambda e, j=j: e.tensor_copy(out=Sbf[:, j, :], in_=S[:, j, :]), reads=['S'], writes=['Sbf'])
                            yield
                    yield from skewed(nb, pre, post)

                def gates_fn(bi):
                    _, s0, nb = blks[bi]
                    yield from gates(g, nb * 128, 1024, cfg.c_og, cfg.c_mg)

                def finalize(bi):
                    _, s0, nb = blks[bi]
                    for c in range(nb):
                        store_y(False, g, s0, c, 1024, MG)
                run_unit(gemm, scan, gates_fn, finalize)

            for g in range(cfg.GH):
                gla_unit(g)
            for h in range(cfg.RH):
                ret_unit(h)
            P.barrier()
            P.flush()

        NT = NREAL // 128
        with ExitStack() as es:
            def sb(name, shape, dt):
                return es.enter_context(nc.sbuf_tensor(name, list(shape), dt))

            def psm(name, shape, dt):
                return es.enter_context(nc.psum_tensor(name, list(shape), dt))
            ident = sb("p2_ident", [128, 128], BF16)
            aT = sb("p2_aT", [128, KT, NREAL], BF16)
            aTh = sb("p2_aTh", [128, KT, 2], BF16)
            m1 = [sb(f"p2_m1{i}", [128, D], BF16) for i in range(2)]
            m2s = sb("p2_m2", [128, D], BF16)
            m2 = [m2s, m2s]
            OC = 256
            warena = sb("p2_warena", [128, KT * OC * 3], BF16)
            xp = [sb(f"p2_xp{i}", [128, OC], F32) for i in range(2)]
            gB = sb("p2_gB", [128, D], F32)
            hrow = sb("p2_hrow", [128, D], F32)
            ss = sb("p2_ss", [128, 4], F32)
            cw = sb("p2_cw", [128, 4, FT], F32)
            upb = sb("p2_upb", [128, NREAL + 2], F32)
            cc = sb("p2_cc", [128, NREAL], F32)
            sg = sb("p2_sg", [128, NREAL], F32)
            ab = [sb(f"p2_ab{i}", [128, NREAL], BF16) for i in range(2)]
            pA = [psm(f"p2_pA{i}", [128, 512], F32) for i in range(6)]
            pH = psm("p2_pH", [128, 512], F32)
            pT = psm("p2_pT", [128, 8, 128], BF16)
            epsT = sb("p2_eps", [128, 1], F32)
            P.op('dve', lambda e: e.memset(epsT[:], EPS), writes=['epsT'])
            P.dma('sp', lambda e: e.dma_start(out=ident[:], in_=c_ident_bf), writes=['ident'])
            P.dma('sp', lambda e: e.dma_start(out=gB[:], in_=bcast_row(ffn_norm[0:1, :], 128)), writes=['gB'])
            for i in range(3):
                P.dma('sp', lambda e, i=i: e.dma_start(out=cw[:, i, :], in_=conv_w[i:i + 1, :].rearrange("o (f p) -> p (o f)", p=128), allow_slow_non_contiguous=True), writes=['cw'])
            P.dma('sp', lambda e: e.dma_start(out=cw[:, 3, :], in_=conv_b[0:1, :].rearrange("o (f p) -> p (o f)", p=128), allow_slow_non_contiguous=True), writes=['cw'])

            def transposes_to(src_bf, srcres, dst_fn, rows=128, keep=None):
                for g4 in range(KT // 8):
                    for j in range(8):
                        kt = g4 * 8 + j
                        P.op('pe', lambda e, kt=kt, j=j: e.transpose(pT[:, j, 0:rows], src_bf[0:rows, kt * 128:(kt + 1) * 128], ident[0:rows, 0:rows]),
                             reads=[srcres, 'ident'], writes=['pT'], signal=(j == 7))
                    dst, dres = dst_fn(g4 * 8)
                    k0_, k1_ = keep if keep else (0, rows)
                    P.op('act', lambda e, dst=dst, k0_=k0_, k1_=k1_: e.activation(out=dst, in_=pT[:, :, k0_:k1_], func=AF.Copy),
                         reads=['pT'], writes=[dres])

            for ti, t in enumerate([0] + list(range(1, NT + 1))):
                b = ti % 2
                P.dma('sp', lambda e, t=t, b=b: e.dma_start(out=m1[b][:], in_=MR[t * 128:(t + 1) * 128, :]), writes=[('m1', b)])
                P.dma('sp', lambda e, t=t, b=b: e.dma_start(out=m2[b][:], in_=MG[t * 128:(t + 1) * 128, :]), writes=['m2'])
                P.op('dve', lambda e, b=b: e.tensor_tensor(out=m1[b][:], in0=m1[b][:], in1=m2[b][:], op=ALU.add),
                     reads=[('m1', b), 'm2'], writes=[('m1', b)])
                if t == 0:
                    transposes_to(m1[b], ('m1', b), lambda k0: (aTh[:, k0:k0 + 8, :], 'aTh'), rows=128, keep=(126, 128))
                else:
                    transposes_to(m1[b], ('m1', b), lambda k0, t=t: (aT[:, k0:k0 + 8, (t - 1) * 128:t * 128], 'aT'))

            w2ctr = [0]

            def wload2(src, r0, nrows_t, c0, ncols, nslots, tag):
                s = w2ctr[0] % nslots
                w2ctr[0] += 1
                sz = nrows_t * ncols
                view = warena[:, s * sz:(s + 1) * sz].rearrange("p (k c) -> p k c", k=nrows_t)
                step = 8
                for k0 in range(0, nrows_t, step):
                    k1 = min(nrows_t, k0 + step)
                    P.dma('pool', lambda e, k0=k0, k1=k1, view=view: e.dma_start(
                        out=view[:, k0:k1, :],
                        in_=src[r0 + k0 * 128:r0 + k1 * 128, c0:c0 + ncols].rearrange("(k p) c -> p k c", p=128)),
                        writes=[(tag, s, k0 // step)])
                return view, (lambda kt, s=s, tag=tag: (tag, s, kt // 8))

            pa2 = [0]
            for cg in range(D // OC):
                view, wres = wload2(w_out, 0, KT, cg * OC, OC, 3, 'wo')
                for t in range(0, NT + 1):
                    pi = pa2[0] % 6
                    pa2[0] += 1
                    b = pa2[0] % 2
                    if t == 0:
                        lhs = lambda kt: aTh[:, kt, :]
                        rows, r0, lres = 2, 126, 'aTh'
                    else:
                        lhs = lambda kt, t=t: aT[:, kt, (t - 1) * 128:t * 128]
                        rows, r0, lres = 128, t * 128, 'aT'
                    for kt in range(KT):
                        P.op('pe', lambda e, kt=kt, pi=pi, lhs=lhs, rows=rows, view=view: e.matmul(pA[pi][0:rows, 0:OC], lhsT=lhs(kt), rhs=view[:, kt, :],
                                                                                                  start=(kt == 0), stop=(kt == KT - 1)),
                             reads=[wres(kt), lres], writes=[('pA', pi)], signal=(kt == KT - 1))
                    P.dma('sp', lambda e, b=b, rows=rows, r0=r0, cg=cg: e.dma_start(out=xp[b][0:rows, :], in_=xs[PRE0 + r0:PRE0 + r0 + rows, cg * OC:(cg + 1) * OC]),
                          writes=[('xp', b)])
                    P.op('dve', lambda e, b=b, rows=rows, pi=pi: e.tensor_tensor(out=xp[b][0:rows, :], in0=pA[pi][0:rows, 0:OC], in1=xp[b][0:rows, :], op=ALU.add),
                         reads=[('pA', pi), ('xp', b)], writes=[('xp', b)])
                    P.dma('sp', lambda e, b=b, rows=rows, r0=r0, cg=cg: e.dma_start(out=H1[r0:r0 + rows, cg * OC:(cg + 1) * OC], in_=xp[b][0:rows, :]),
                          reads=[('xp', b)], writes=[('H1', t, cg)])

            for ti, t in enumerate(range(0, NT + 1)):
                b = ti % 2
                rows, r0 = (2, 126) if t == 0 else (128, t * 128)
                P.dma('sp', lambda e, rows=rows, r0=r0: e.dma_start(out=hrow[0:rows, :], in_=H1[r0:r0 + rows, :]), reads=[('H1', t, cg_) for cg_ in range(D // OC)], writes=['hrow'])
                P.op('dve', lambda e: e.memset(ss[:, 0:1], 0.0), writes=['ss'])
                P.op('act', lambda e, b=b, rows=rows: e.activation(out=m2[b][0:rows, :], in_=hrow[0:rows, :], func=AF.Square, accum_out=ss[0:rows, 0:1]),
                     reads=['hrow'], writes=['m2', 'ss'])
                P.op('act', lambda e, rows=rows: e.activation(out=ss[0:rows, 1:2], in_=ss[0:rows, 0:1], func=AF.Sqrt, bias=epsT[0:rows, 0:1], scale=1.0 / D),
                     reads=['ss', 'epsT'], writes=['ss'])
                P.op('dve', lambda e, rows=rows: e.reciprocal(out=ss[0:rows, 1:2], in_=ss[0:rows, 1:2]), reads=['ss'], writes=['ss'])
                P.op('dve', lambda e, b=b, rows=rows: e.scalar_tensor_tensor(out=m1[b][0:rows, :], in0=hrow[0:rows, :], scalar=ss[0:rows, 1:2], in1=gB[0:rows, :],
                                                                           op0=ALU.mult, op1=ALU.mult),
                     reads=['hrow', 'ss', 'gB'], writes=[('m1', b)])
                if t == 0:
                    transposes_to(m1[b], ('m1', b), lambda k0: (aTh[:, k0:k0 + 8, :], 'aTh'), rows=2, keep=(0, 2))
                else:
                    transposes_to(m1[b], ('m1', b), lambda k0, t=t: (aT[:, k0:k0 + 8, (t - 1) * 128:t * 128], 'aT'))

            P.barrier()
            NP = [(o, min(512, NREAL - o)) for o in range(0, NREAL, 512)]
            assert len(NP) <= 2
            for m in range(FT):
                view, wres = wload2(w_ffn_in, 0, KT, m * 128, 128, 6, 'wf')
                viewg, wresg = wload2(w_ffn_in, 0, KT, DFF + m * 128, 128, 6, 'wf')
                for kt in range(KT):
                    P.op('pe', lambda e, kt=kt, view=view: e.matmul(pH[:, 0:2], lhsT=view[:, kt, :], rhs=aTh[:, kt, :], start=(kt == 0), stop=(kt == KT - 1)),
                         reads=[wres(kt), 'aTh'], writes=['pH'], signal=(kt == KT - 1))
                P.op('act', lambda e: e.activation(out=upb[:, 0:2], in_=pH[:, 0:2], func=AF.Copy), reads=['pH'], writes=['upb'])
                ups, gts = [], []
                for (po, pn) in NP:
                    pi = pa2[0] % 6
                    pa2[0] += 1
                    for kt in range(KT):
                        P.op('pe', lambda e, kt=kt, pi=pi, po=po, pn=pn, view=view: e.matmul(pA[pi][:, 0:pn], lhsT=view[:, kt, :], rhs=aT[:, kt, po:po + pn],
                                                                                             start=(kt == 0), stop=(kt == KT - 1)),
                             reads=[wres(kt), 'aT'], writes=[('pA', pi)], signal=(kt == KT - 1))
                    P.op('act', lambda e, pi=pi, po=po, pn=pn: e.activation(out=upb[:, 2 + po:2 + po + pn], in_=pA[pi][:, 0:pn], func=AF.Copy),
                         reads=[('pA', pi)], writes=['upb'])
                for (po, pn) in NP:
                    pi = pa2[0] % 6
                    pa2[0] += 1
                    for kt in range(KT):
                        P.op('pe', lambda e, kt=kt, pi=pi, po=po, pn=pn, viewg=viewg: e.matmul(pA[pi][:, 0:pn], lhsT=viewg[:, kt, :], rhs=aT[:, kt, po:po + pn],
                                                                                               start=(kt == 0), stop=(kt == KT - 1)),
                             reads=[wresg(kt), 'aT'], writes=[('pA', pi)], signal=(kt == KT - 1))
                    gts.append((pi, po, pn))
                P.op('dve', lambda e, m=m: e.tensor_scalar(out=cc[:], in0=upb[:, 2:2 + NREAL], scalar1=cw[:, 2, m:m + 1], scalar2=cw[:, 3, m:m + 1], op0=ALU.mult, op1=ALU.add),
                     reads=['upb', 'cw'], writes=['cc'])
                P.op('dve', lambda e, m=m: e.scalar_tensor_tensor(out=cc[:], in0=upb[:, 1:1 + NREAL], scalar=cw[:, 1, m:m + 1], in1=cc[:], op0=ALU.mult, op1=ALU.add),
                     reads=['upb', 'cw', 'cc'], writes=['cc'])
                P.op('dve', lambda e, m=m: e.scalar_tensor_tensor(out=cc[:], in0=upb[:, 0:NREAL], scalar=cw[:, 0, m:m + 1], in1=cc[:], op0=ALU.mult, op1=ALU.add),
                     reads=['upb', 'cw', 'cc'], writes=['cc'])
                P.op('act', lambda e: e.activation(out=sg[:], in_=cc[:], func=AF.Sigmoid), reads=['cc'], writes=['sg'])
                P.op('dve', lambda e: e.tensor_tensor(out=cc[:], in0=cc[:], in1=sg[:], op=ALU.mult), reads=['cc', 'sg'], writes=['cc'])
                ba = m % 2
                for (pi, po, pn) in gts:
                    P.op('dve', lambda e, pi=pi, po=po, pn=pn, ba=ba: e.tensor_tensor(out=ab[ba][:, po:po + pn], in0=pA[pi][:, 0:pn], in1=cc[:, po:po + pn], op=ALU.mult),
                         reads=[('pA', pi), 'cc'], writes=[('ab', ba)])
                P.dma('sp', lambda e, m=m, ba=ba: e.dma_start(out=ACTS[m, :, :], in_=ab[ba][:]), reads=[('ab', ba)], writes=[('ACTS', m)])
            P.barrier()
            P.flush()

        with ExitStack() as es:
            def sb(name, shape, dt):
                return es.enter_context(nc.sbuf_tensor(name, list(shape), dt))

            def psm(name, shape, dt):
                return es.enter_context(nc.psum_tensor(name, list(shape), dt))
            identf = sb("p3_identf", [128, 128], F32)
            actT = sb("p3_actT", [128, FT, 512], BF16)
            wb3 = [sb(f"p3_w{i}", [128, 8 * 512], BF16) for i in range(4)]
            o2 = [sb(f"p3_o2{i}", [128, 512], F32) for i in range(2)]
            fo = [sb(f"p3_fo{i}", [128, 4, 128], F32) for i in range(2)]
            gB = sb("p3_gB", [128, D], F32)
            r1 = sb("p3_r1", [128, D], F32)
            r2 = sb("p3_r2", [128, D], F32)
            ss = sb("p3_ss", [128, 4], F32)
            pA = [psm(f"p3_pA{i}", [128, 512], F32) for i in range(4)]
            pT = [psm(f"p3_pT{i}", [128, 4, 128], F32) for i in range(2)]
            epsT = sb("p3_eps", [128, 1], F32)
            P.op('dve', lambda e: e.memset(epsT[:], EPS), writes=['epsT'])
            P.dma('sp', lambda e: e.dma_start(out=identf[:], in_=c_ident_f), writes=['identf'])
            P.dma('sp', lambda e: e.dma_start(out=gB[:], in_=bcast_row(final_norm[0:1, :], 128)), writes=['gB'])
            w3ctr = [0]
            cnt = 0
            JG = 8
            NW3 = 4
            CG = min(512, D)
            for (po, pn) in [(o, min(512, NREAL - o)) for o in range(0, NREAL, 512)]:
                for f0 in range(0, FT, 8):
                    f1 = min(FT, f0 + 8)
                    P.dma('sp', lambda e, f0=f0, f1=f1, po=po, pn=pn: e.dma_start(out=actT[:, f0:f1, 0:pn],
                                                                              in_=ACTS[f0:f1, :, po:po + pn].rearrange("f p n -> p f n")),
                          writes=[('actT', f0 // 8)])
                nsub = pn // 128
                for cg in range(D // CG):
                    nct = CG // 128
                    for j0 in range(0, FT, JG):
                        j1 = min(FT, j0 + JG)
                        s_ = w3ctr[0] % NW3
                        w3ctr[0] += 1
                        view = wb3[s_][:, 0:(j1 - j0) * CG].rearrange("p (k c) -> p k c", k=j1 - j0)
                        P.dma('pool', lambda e, j0=j0, j1=j1, view=view, cg=cg: e.dma_start(
                            out=view,
                            in_=w_ffn_out[j0 * 128:j1 * 128, cg * CG:(cg + 1) * CG].rearrange("(k p) c -> p k c", p=128)),
                            writes=[('w3', s_)])
                        for ct in range(nct):
                            for jt in range(j0, j1):
                                last = (ct == nct - 1 and jt == j1 - 1)
                                P.op('pe', lambda e, jt=jt, j0=j0, ct=ct, view=view, pn=pn: e.matmul(
                                    pA[ct][:, 0:pn], lhsT=view[:, jt - j0, ct * 128:(ct + 1) * 128], rhs=actT[:, jt, 0:pn],
                                    start=(jt == 0), stop=(jt == FT - 1)),
                                    reads=[('w3', s_), ('actT', jt // 8)], writes=[('pA', ct)], signal=(last or jt == FT - 1))
                    for ct in range(nct):
                        f = cg * nct + ct
                        b = cnt % 2
                        cnt += 1
                        P.op('act', lambda e, ct=ct, b=b, pn=pn: e.activation(out=o2[b][:, 0:pn], in_=pA[ct][:, 0:pn], func=AF.Copy),
                             reads=[('pA', ct)], writes=[('o2', b)])
                        for sidx in range(nsub):
                            P.op('pe', lambda e, sidx=sidx, b=b: e.transpose(pT[b][:, sidx, :], o2[b][:, sidx * 128:(sidx + 1) * 128], identf[:]),
                                 reads=[('o2', b), 'identf'], writes=[('pT', b)], signal=(sidx == nsub - 1))
                        P.op('dve', lambda e, b=b, nsub=nsub: e.tensor_copy(out=fo[b][:, 0:nsub, :], in_=pT[b][:, 0:nsub, :]),
                             reads=[('pT', b)], writes=[('fo', b)])
                        P.dma('sp', lambda e, b=b, nsub=nsub, po=po, f=f: e.dma_start(
                            out=FO[po:po + nsub * 128, f * 128:(f + 1) * 128].rearrange("(s p) c -> p s c", p=128),
                            in_=fo[b][:, 0:nsub, :]), reads=[('fo', b)], writes=[('FO', po, f)])
            P.barrier()
            for t in range(NT):
                P.dma('sp', lambda e, t=t: e.dma_start(out=r1[:], in_=FO[t * 128:(t + 1) * 128, :]), writes=['r1'])
                P.dma('sp', lambda e, t=t: e.dma_start(out=r2[:], in_=H1[(t + 1) * 128:(t + 2) * 128, :]), writes=['r2'])
                P.op('dve', lambda e: e.tensor_tensor(out=r1[:], in0=r1[:], in1=r2[:], op=ALU.add), reads=['r1', 'r2'], writes=['r1'])
                P.op('dve', lambda e: e.memset(ss[:, 0:1], 0.0), writes=['ss'])
                P.op('act', lambda e: e.activation(out=r2[:], in_=r1[:], func=AF.Square, accum_out=ss[:, 0:1]), reads=['r1'], writes=['r2', 'ss'])
                P.op('act', lambda e: e.activation(out=ss[:, 1:2], in_=ss[:, 0:1], func=AF.Sqrt, bias=epsT[:, 0:1], scale=1.0 / D), reads=['ss', 'epsT'], writes=['ss'])
                P.op('dve', lambda e: e.reciprocal(out=ss[:, 1:2], in_=ss[:, 1:2]), reads=['ss'], writes=['ss'])
                P.op('dve', lambda e: e.scalar_tensor_tensor(out=r2[:], in0=r1[:], scalar=ss[:, 1:2], in1=gB[:], op0=ALU.mult, op1=ALU.mult),
                     reads=['r1', 'ss', 'gB'], writes=['r2'])
                P.dma('sp', lambda e, t=t: e.dma_start(out=out[t * 128:(t + 1) * 128, :], in_=r2[:]), reads=['r2'], writes=[('out', t)])
            P.barrier()
            P.flush()
    P.check_deadlock()
    return nc


def host_constants(cfg):
    n = np.arange(128)
    c = {}
    c["c_ident_bf"] = np.eye(128, dtype=np.float32).astype(ml_dtypes.bfloat16)
    c["c_ident_f"] = np.eye(128, dtype=np.float32)
    c["c_invf"] = (10000.0 ** (-np.arange(128, dtype=np.float32) / 128)).astype(np.float32)[:, None]
    lg = np.log1p(-np.exp2(-5.0 - np.arange(cfg.RH, dtype=np.float64)))
    rdk = np.exp(lg[:, None] * (127 - n)[None, :]) / 16.0
    rdq = np.exp(lg[:, None] * (n + 1)[None, :])
    c["c_rdk"] = np.concatenate([rdk, rdk], axis=1).astype(np.float32)
    c["c_rdq"] = np.concatenate([rdq, rdq], axis=1).astype(np.float32)
    rel = n[None, :] - n[:, None]
    c["c_rmask"] = np.where(rel[None] >= 0, np.exp(lg[:, None, None] * np.maximum(rel, 0)[None]) / 16.0, 0.0).astype(np.float32)
    c["c_gmask"] = (rel >= 0).astype(np.float32)
    c["c_tri"] = np.where(rel >= 0, -1.0 / 16.0, 0.0).astype(np.float32)
    return c


def make_in_maps(cfg, x, positions, meta_tokens, attn_norm, w_in, w_gate_up, b_gate, ret_norm, gla_norm,
                 w_out, ffn_norm, w_ffn_in, conv_w, conv_b, w_ffn_out, final_norm):
    B, SEQ, D = x.shape
    f32 = np.float32
    consts = host_constants(cfg)
    shared = {
        "attn_norm": np.ascontiguousarray(attn_norm[0][None], f32),
        "w_in": np.ascontiguousarray(w_in[0], f32),
        "w_gate": np.ascontiguousarray(np.concatenate([w_gate_up[0], b_gate[0][None]], axis=0), f32),
        "ret_norm": np.ascontiguousarray(ret_norm[0][None], f32),
        "gla_norm": np.ascontiguousarray(gla_norm[0][None], f32),
        "w_out": np.ascontiguousarray(w_out[0], f32),
        "ffn_norm": np.ascontiguousarray(ffn_norm[0][None], f32),
        "w_ffn_in": np.ascontiguousarray(w_ffn_in[0], f32),
        "conv_w": np.ascontiguousarray(conv_w[0], f32),
        "conv_b": np.ascontiguousarray(conv_b[0][None], f32),
        "w_ffn_out": np.ascontiguousarray(w_ffn_out[0], f32),
        "final_norm": np.ascontiguousarray(final_norm[None], f32),
    }
    shared.update(consts)
    NPAD = 112
    in_maps = []
    half = SEQ // 2
    assert half == cfg.NREAL and NPAD + 16 + SEQ == cfg.NS
    metapos = (np.arange(16) - 16).astype(np.int32)
    for b in range(B):
        seq = np.concatenate([np.zeros((NPAD, D), f32), np.asarray(meta_tokens, f32), np.asarray(x[b], f32)], axis=0)
        pos = np.concatenate([np.zeros(NPAD, np.int32), metapos, np.asarray(positions[b], np.int32)])
        for s in range(2):
            if s == 0:
                xs = np.concatenate([np.zeros((cfg.NPRE * 128, D), f32), seq[0:cfg.NMAIN * 128]], axis=0)
                ps = np.concatenate([np.zeros(cfg.NPRE * 128, np.int32), pos[0:cfg.NMAIN * 128]])
            else:
                xs, ps = seq, pos
            m = dict(shared)
            m["xs"] = np.ascontiguousarray(xs)
            m["posr"] = np.ascontiguousarray(ps[None])
            in_maps.append(m)
    return in_maps


def run(cfg, inputs, trace=False):
    inputs = {k: np.asarray(v) for k, v in inputs.items()}
    nc = build_nc(cfg)
    in_maps = make_in_maps(cfg, **inputs)
    res = run_bass_kernel_spmd(nc, in_maps, core_ids=list(range(8)))
    B, SEQ, D = inputs["x"].shape
    out = np.zeros((B, SEQ, D), np.float32)
    for b in range(B):
        for s in range(2):
            out[b, s * cfg.NREAL:(s + 1) * cfg.NREAL] = res.results[2 * b + s]["out"]
    return out


def kernel(**inputs):
    return run(Cfg(), inputs)
```

```python
import math
from contextlib import ExitStack
import numpy as np
import ml_dtypes
import concourse.bass as bass
import concourse.mybir as mybir
from concourse.bass_utils import run_bass_kernel_spmd

F32 = mybir.dt.float32
BF16 = mybir.dt.bfloat16
I32 = mybir.dt.int32
AF = mybir.ActivationFunctionType
ALU = mybir.AluOpType
AX = mybir.AxisListType

ENGS = ('pe', 'act', 'dve', 'pool', 'sp')
NDMA_SEM = 12
EPS = 1e-6


class Prog:
    def __init__(self, nc):
        self.nc = nc
        self.q = {e: [] for e in ENGS}
        self.seq = {e: 0 for e in ENGS}
        self.ndma = {e: 0 for e in ENGS}
        self.waited = {e: {} for e in ENGS}
        self.res = {}
        self.sems = {}
        self.latest = {}
        self.trace = {e: [] for e in ENGS}

    def _deps(self, reads, writes):
        deps = {}

        def add(k, v):
            if deps.get(k, 0) < v:
                deps[k] = v
        for r in reads:
            st = self.res.get(r)
            if st:
                if st[0] is not None:
                    add(*st[0])
        for w in writes:
            st = self.res.get(w)
            if st:
                if st[0] is not None:
                    add(*st[0])
                for k, v in st[1].items():
                    add(k, v)
        return deps

    def _update(self, tok, reads, writes):
        k, v = tok
        if self.latest.get(k, 0) < v:
            self.latest[k] = v
        for r in reads:
            st = self.res.setdefault(r, [None, {}])
            if st[1].get(k, 0) < v:
                st[1][k] = v
        for w in writes:
            self.res[w] = [tok, {}]

    def _waits(self, eng, deps, skip_self=False):
        ws = []
        for k, v in deps.items():
            if skip_self and k == eng:
                continue
            if self.waited[eng].get(k, 0) < v:
                self.waited[eng][k] = v
                ws.append((k, v))
        return ws

    def op(self, eng, fn, reads=(), writes=(), signal=True):
        deps = self._deps(reads, writes)
        ws = self._waits(eng, deps, skip_self=(eng == 'pe'))
        if signal:
            self.seq[eng] += 1
            tok = (eng, self.seq[eng])
        else:
            tok = (eng, self.seq[eng] + 1)
        sems = self.sems

        def run(e, fn=fn, ws=ws, signal=signal, eng=eng):
            for k, v in ws:
                e.wait_ge(sems[k], v)
            ins = fn(e)
            if signal:
                ins.then_inc(sems[eng], 1)
        self.q[eng].append(run)
        self.trace[eng].append((list(ws), (eng, 1) if signal else None))
        self._update(tok, reads, writes)
        return tok

    def dma(self, queue, fn, reads=(), writes=()):
        n = self.ndma[queue]
        self.ndma[queue] += 1
        idx = n % NDMA_SEM
        cnt = n // NDMA_SEM + 1
        key = ('dma', queue, idx)
        deps = self._deps(reads, writes)
        if cnt > 1 and deps.get(key, 0) < 16 * (cnt - 1):
            deps[key] = 16 * (cnt - 1)
        ws = self._waits(queue, deps)
        tok = (key, 16 * cnt)
        sems = self.sems

        def run(e, fn=fn, ws=ws, key=key):
            for k, v in ws:
                e.wait_ge(sems[k], v)
            fn(e).then_inc(sems[key], 16)
        self.q[queue].append(run)
        self.trace[queue].append((list(ws), (key, 16)))
        self._update(tok, reads, writes)
        return tok

    def barrier(self):
        lat = dict(self.latest)
        sems = self.sems
        for eng in ENGS:
            ws = self._waits(eng, lat, skip_self=False)
            ws = [(k, v) for (k, v) in ws if k != eng or eng != 'pe']

            def run(e, ws=ws):
                for k, v in ws:
                    e.wait_ge(sems[k], v)
            self.q[eng].append(run)
            self.trace[eng].append((list(ws), None))
        self.res = {}

    def check_deadlock(self):
        val = {}
        pos = {e: 0 for e in ENGS}
        tr = self.trace
        while True:
            prog = False
            for e in ENGS:
                while pos[e] < len(tr[e]):
                    ws, inc = tr[e][pos[e]]
                    if all(val.get(k, 0) >= v for k, v in ws):
                        if inc:
                            val[inc[0]] = val.get(inc[0], 0) + inc[1]
                        pos[e] += 1
                        prog = True
                    else:
                        break
            if not prog:
                break
        stuck = {e: (pos[e], len(tr[e])) for e in ENGS if pos[e] < len(tr[e])}
        if stuck:
            msg = {e: (p, n, tr[e][p][0], {k: val.get(k, 0) for k, _ in tr[e][p][0]}) for e, (p, n) in stuck.items()}
            raise RuntimeError(f"semaphore deadlock: {msg}")

    def alloc_sems(self, es):
        nc = self.nc
        for e in ('pe', 'act', 'dve', 'pool'):
            self.sems[e] = es.enter_context(nc.semaphore('s_' + e))
        for qn in ('sp', 'act', 'pool'):
            for i in range(NDMA_SEM):
                self.sems[('dma', qn, i)] = es.enter_context(nc.semaphore(f'd_{qn}_{i}'))

    def flush(self):
        nc = self.nc
        q = self.q
        with nc.Block() as block:
            @block.tensor
            def _(e):
                for f in q['pe']:
                    f(e)

            @block.scalar
            def _(e):
                for f in q['act']:
                    f(e)

            @block.vector
            def _(e):
                for f in q['dve']:
                    f(e)

            @block.gpsimd
            def _(e):
                for f in q['pool']:
                    f(e)

            @block.sync
            def _(e):
                for f in q['sp']:
                    f(e)
        self.q = {e: [] for e in ENGS}


class Cfg:
    def __init__(self, D=4096, DFF=11008, NPRE=8, NMAIN=9, pre_blocks=(4, 4), main_blocks=(5, 4)):
        self.D = D
        self.KT = D // 128
        self.RH = D // 512
        self.GH = D // 1024
        self.RDK, self.RDV, self.GDK, self.GDV, self.RANK = 256, 512, 512, 1024, 16
        self.DFF = DFF
        self.FT = DFF // 128
        self.NPRE, self.NMAIN = NPRE, NMAIN
        self.NCH = NPRE + NMAIN
        self.NS = self.NCH * 128
        self.NREAL = (NMAIN - 1) * 128
        self.pre_blocks, self.main_blocks = pre_blocks, main_blocks
        assert sum(pre_blocks) == NPRE and sum(main_blocks) == NMAIN
        RQK, RV, GQK, GV = self.RH * 256, D, self.GH * 512, D
        self.c_qr = 0
        self.c_kr = RQK
        self.c_vr = 2 * RQK
        self.c_or = self.c_vr + RV
        self.c_qg = self.c_or + RV
        self.c_kg = self.c_qg + GQK
        self.c_vg = self.c_kg + GQK
        self.c_og = self.c_vg + GV
        self.c_zg = self.c_og + GV
        self.c_mr = self.c_zg + 16
        self.c_mg = self.c_mr + D
        self.WIN = self.c_mg + D
        self.GQK = GQK
        self.MAXB = max(max(pre_blocks), max(main_blocks)) * 128


def build_nc(cfg):
    D, KT, NS, DFF, FT = cfg.D, cfg.KT, cfg.NS, cfg.DFF, cfg.FT
    NMS = cfg.NMAIN * 128
    NREAL = cfg.NREAL
    nc = bass.Bass("TRN2", target_bir_lowering=False)

    def din(name, shape, dt=F32):
        return nc.dram_tensor(name, list(shape), dt, kind="ExternalInput").ap()

    def dscr(name, shape, dt):
        return nc.dram_tensor(name, list(shape), dt, kind="Internal").ap()
    xs = din("xs", [NS, D])
    posr = din("posr", [1, NS], I32)
    attn_norm = din("attn_norm", [1, D])
    w_in = din("w_in", [D, cfg.WIN])
    w_gate = din("w_gate", [17, cfg.GQK])
    ret_norm = din("ret_norm", [1, D])
    gla_norm = din("gla_norm", [1, D])
    w_out = din("w_out", [D, D])
    ffn_norm = din("ffn_norm", [1, D])
    w_ffn_in = din("w_ffn_in", [D, 2 * DFF])
    conv_w = din("conv_w", [3, DFF])
    conv_b = din("conv_b", [1, DFF])
    w_ffn_out = din("w_ffn_out", [DFF, D])
    final_norm = din("final_norm", [1, D])
    c_ident_bf = din("c_ident_bf", [128, 128], BF16)
    c_ident_f = din("c_ident_f", [128, 128])
    c_invf = din("c_invf", [128, 1])
    c_rdk = din("c_rdk", [cfg.RH, 256])
    c_rdq = din("c_rdq", [cfg.RH, 256])
    c_rmask = din("c_rmask", [cfg.RH, 128, 128])
    c_gmask = din("c_gmask", [128, 128])
    c_tri = din("c_tri", [128, 128])
    out = nc.dram_tensor("out", [NREAL, D], F32, kind="ExternalOutput").ap()

    hnT = dscr("hnT", [KT, 128, NS], BF16)
    MR = dscr("MR", [NMS, D], BF16)
    MG = dscr("MG", [NMS, D], BF16)
    H1 = dscr("H1", [NMS, D], F32)
    FO = dscr("FO", [NREAL, D], F32)
    ACTS = dscr("ACTS", [FT, 128, NREAL], BF16)
    cosD = dscr("cosD", [128, NS], F32)
    sinD = dscr("sinD", [128, NS], F32)

    P = Prog(nc)
    gam = [1.0 - 2.0 ** (-5.0 - h) for h in range(cfg.RH)]
    PRE0 = cfg.NPRE * 128

    with ExitStack() as top:
        P.alloc_sems(top)

        def bcast_row(ap_row, n):
            return ap_row.partition_broadcast(n)

        with ExitStack() as es:
            def sb(name, shape, dt):
                return es.enter_context(nc.sbuf_tensor(name, list(shape), dt))

            def psm(name, shape, dt):
                return es.enter_context(nc.psum_tensor(name, list(shape), dt))
            ident = sb("s0_ident", [128, 128], BF16)
            gB = sb("s0_gB", [128, D], F32)
            xt = [sb(f"s0_xt{i}", [128, D], F32) for i in range(2)]
            yb = [sb(f"s0_yb{i}", [128, D], BF16) for i in range(2)]
            junk = [sb(f"s0_junk{i}", [128, D], BF16) for i in range(2)]
            ss = sb("s0_ss", [128, 2 * cfg.NCH], F32)
            hT = [sb(f"s0_hT{i}", [128, KT, 128], BF16) for i in range(2)]
            pt = [psm(f"s0_pt{i}", [128, 8, 128], BF16) for i in range(4)]
            posi = sb("s0_posi", [128, NS], I32)
            sinT = sb("s0_sin", [128, NS], F32)
            cosT = sb("s0_cos", [128, NS], F32)
            invf = sb("s0_invf", [128, 1], F32)
            P.dma('sp', lambda e: e.dma_start(out=ident[:], in_=c_ident_bf), writes=['ident'])
            P.dma('sp', lambda e: e.dma_start(out=gB[:], in_=bcast_row(attn_norm[0:1, :], 128)), writes=['gB'])
            P.dma('sp', lambda e: e.dma_start(out=invf[:], in_=c_invf), writes=['invf'])
            P.dma('sp', lambda e: e.dma_start(out=posi[:], in_=bcast_row(posr[0:1, :], 128)), writes=['posi'])
            P.op('dve', lambda e: e.tensor_scalar(out=sinT[:], in0=posi[:], scalar1=16.0, scalar2=None, op0=ALU.add),
                 reads=['posi'], writes=['sinT'])
            P.op('dve', lambda e: e.tensor_scalar(out=sinT[:], in0=sinT[:], scalar1=invf[:, 0:1], scalar2=None, op0=ALU.mult),
                 reads=['sinT', 'invf'], writes=['sinT'])
            ki = sb("s0_ki", [128, NS], I32)
            kf = sb("s0_kf", [128, NS], F32)
            P.op('dve', lambda e: e.tensor_scalar(out=cosT[:], in0=sinT[:], scalar1=0.5 * math.pi, scalar2=None, op0=ALU.add),
                 reads=['sinT'], writes=['cosT'])
            for nm, tt_ in (('sinT', sinT), ('cosT', cosT)):
                P.op('dve', lambda e, tt_=tt_: e.tensor_scalar(out=kf[:], in0=tt_[:], scalar1=1.0 / (2 * math.pi), scalar2=None, op0=ALU.mult),
                     reads=[nm], writes=['kf'])
                P.op('dve', lambda e: e.tensor_copy(out=ki[:], in_=kf[:]), reads=['kf'], writes=['ki'])
                P.op('dve', lambda e: e.tensor_copy(out=kf[:], in_=ki[:]), reads=['ki'], writes=['kf'])
                P.op('dve', lambda e, tt_=tt_: e.scalar_tensor_tensor(out=tt_[:], in0=kf[:], scalar=-2 * math.pi, in1=tt_[:], op0=ALU.mult, op1=ALU.add),
                     reads=['kf', nm], writes=[nm])
                P.op('dve', lambda e, tt_=tt_: e.tensor_scalar(out=kf[:], in0=tt_[:], scalar1=math.pi, scalar2=2 * math.pi, op0=ALU.is_gt, op1=ALU.mult),
                     reads=[nm], writes=['kf'])
                P.op('dve', lambda e, tt_=tt_: e.tensor_tensor(out=tt_[:], in0=tt_[:], in1=kf[:], op=ALU.subtract),
                     reads=[nm, 'kf'], writes=[nm])
            P.op('act', lambda e: e.activation(out=sinT[:], in_=sinT[:], func=AF.Sin), reads=['sinT'], writes=['sinT'])
            P.op('act', lambda e: e.activation(out=cosT[:], in_=cosT[:], func=AF.Sin), reads=['cosT'], writes=['cosT'])
            P.dma('sp', lambda e: e.dma_start(out=sinD, in_=sinT[:]), reads=['sinT'], writes=['sinD'])
            P.dma('sp', lambda e: e.dma_start(out=cosD, in_=cosT[:]), reads=['cosT'], writes=['cosD'])
            P.op('dve', lambda e: e.memset(ss[:], 0.0), writes=[('ss', t_) for t_ in range(cfg.NCH)])
            epsT = sb("s0_eps", [128, 1], F32)
            P.op('dve', lambda e: e.memset(epsT[:], EPS), writes=['epsT'])
            for t in range(cfg.NCH):
                b = t % 2
                P.dma('sp', lambda e, t=t, b=b: e.dma_start(out=xt[b][:], in_=xs[t * 128:(t + 1) * 128, :]),
                      writes=[('xt', b)])
                P.op('act', lambda e, t=t, b=b: e.activation(out=junk[b][:], in_=xt[b][:], func=AF.Square,
                                                             accum_out=ss[:, 2 * t:2 * t + 1]),
                     reads=[('xt', b)], writes=[('junk', b), ('ss', t)])
                P.op('act', lambda e, t=t: e.activation(out=ss[:, 2 * t + 1:2 * t + 2], in_=ss[:, 2 * t:2 * t + 1], func=AF.Sqrt,
                                                        bias=epsT[:, 0:1], scale=1.0 / D), reads=[('ss', t), 'epsT'], writes=[('ss', t)])
                P.op('dve', lambda e, t=t: e.reciprocal(out=ss[:, 2 * t + 1:2 * t + 2], in_=ss[:, 2 * t + 1:2 * t + 2]),
                     reads=[('ss', t)], writes=[('ss', t)])
                P.op('dve', lambda e, t=t, b=b: e.scalar_tensor_tensor(out=yb[b][:], in0=xt[b][:],
                                                                       scalar=ss[:, 2 * t + 1:2 * t + 2], in1=gB[:],
                                                                       op0=ALU.mult, op1=ALU.mult),
                     reads=[('xt', b), ('ss', t), 'gB'], writes=[('yb', b)])
                for g4 in range(KT // 8):
                    pi = (t * (KT // 8) + g4) % 4
                    for j in range(8):
                        kt = g4 * 8 + j
                        P.op('pe', lambda e, kt=kt, j=j, pi=pi, b=b: e.transpose(pt[pi][:, j, :], yb[b][:, kt * 128:(kt + 1) * 128], ident[:]),
                             reads=[('yb', b), 'ident'], writes=[('pt', pi)], signal=(j == 7))
                    P.op('act', lambda e, g4=g4, pi=pi, b=b: e.activation(out=hT[b][:, g4 * 8:(g4 + 1) * 8, :], in_=pt[pi][:], func=AF.Copy),
                         reads=[('pt', pi)], writes=[('hT', b)])
                for g2 in range(0, KT, 8):
                    P.dma('sp', lambda e, t=t, b=b, g2=g2: e.dma_start(
                        out=hnT[g2:g2 + 8, :, t * 128:(t + 1) * 128].rearrange("k p n -> p k n"),
                        in_=hT[b][:, g2:g2 + 8, :]), reads=[('hT', b)], writes=[('hnT', t, g2)])
            P.barrier()
            P.flush()

        with ExitStack() as es:
            def sb(name, shape, dt):
                return es.enter_context(nc.sbuf_tensor(name, list(shape), dt))

            def psm(name, shape, dt):
                return es.enter_context(nc.psum_tensor(name, list(shape), dt))
            MAXB = cfg.MAXB
            NTB = MAXB // 128
            ident = sb("p1_ident", [128, 128], BF16)
            cosT = sb("p1_cos", [128, MAXB], F32)
            sinT = sb("p1_sin", [128, MAXB], F32)
            yr = sb("p1_yr", [128, NTB, 1024], BF16)
            tg = sb("p1_tg", [128, 256], F32)
            NW = 3
            WCOLS = 256
            wb = [sb(f"p1_w{i}", [128, KT * WCOLS], BF16) for i in range(NW)]
            hnb = sb("p1_hnb", [128, KT, MAXB], BF16)
            qT = sb("p1_qT", [128, 4, MAXB], BF16)
            PREB = max(cfg.pre_blocks)
            kTs = [sb("p1_kT", [128, 4, MAXB], BF16), sb("p1_kT1", [128, 4, PREB * 128], BF16)]
            vtoks = [sb("p1_vtok", [128, NTB, 1024], BF16), sb("p1_vtok1", [128, PREB, 1024], BF16)]
            yn = sb("p1_yn", [128, NTB, 1024], BF16)
            S = sb("p1_S", [128, 4, 1024], F32)
            Sbf = sb("p1_Sbf", [128, 4, 1024], BF16)
            tabk = sb("p1_tabk", [128, 256], F32)
            tabq = sb("p1_tabq", [128, 256], F32)
            maskT = sb("p1_mask", [128, 128], F32)
            gmask = sb("p1_gmask", [128, 128], F32)
            tri = sb("p1_tri", [128, 128], F32)
            gnB = sb("p1_gnB", [128, 1024], F32)
            wg = sb("p1_wg", [17, 512], BF16)
            zTs = [sb("p1_zT", [17, MAXB], BF16), sb("p1_zT1", [17, PREB * 128], BF16)]
            t1 = sb("p1_t1", [128, 512], F32)
            t2 = sb("p1_t2", [128, 512], F32)
            t3 = t1
            t4 = t2
            khT = [sb(f"p1_khT{i}", [128, 4, 128], BF16) for i in range(2)]
            khat = [sb(f"p1_khat{i}", [128, 512], BF16) for i in range(2)]
            qd = [sb(f"p1_qd{i}", [128, 4, 128], BF16) for i in range(2)]
            kd = [sb(f"p1_kd{i}", [128, 4, 128], BF16) for i in range(2)]
            sT = [sb(f"p1_sT{i}", [128, 128], BF16) for i in range(2)]
            spf = [sb(f"p1_spf{i}", [128, 512], F32) for i in range(2)]
            eb = [sb(f"p1_eb{i}", [128, 4, 128], F32) for i in range(2)]
            enb = [sb(f"p1_enb{i}", [128, 4, 128], F32) for i in range(2)]
            st = sb("p1_st", [128, 8], F32)
            pA = [psm(f"p1_pA{i}", [128, 512], F32) for i in range(4)]
            pB = [psm(f"p1_pB{i}", [128, 512], F32) for i in range(2)]
            pC = psm("p1_pC", [128, 512], F32)
            pT = psm("p1_pT", [128, 4, 128], BF16)
            pa_ctr = [0]
            epsT = sb("p1_eps", [128, 1], F32)
            oneT = sb("p1_one", [128, 1], F32)
            P.op('dve', lambda e: e.memset(epsT[:], EPS), writes=['epsT'])
            P.op('dve', lambda e: e.memset(oneT[:], 1.0), writes=['oneT'])

            def next_pA():
                i = pa_ctr[0] % 4
                pa_ctr[0] += 1
                return i

            P.dma('sp', lambda e: e.dma_start(out=ident[:], in_=c_ident_bf), writes=['ident'])
            P.dma('sp', lambda e: e.dma_start(out=gmask[:], in_=c_gmask), writes=['gmask'])
            P.dma('sp', lambda e: e.dma_start(out=tri[:], in_=c_tri), writes=['tri'])

            wctr = [0]

            def wload(c0, ncols):
                s = wctr[0] % NW
                wctr[0] += 1
                view = wb[s][:, 0:KT * ncols].rearrange("p (k c) -> p k c", k=KT)
                step = 8
                for k0 in range(0, KT, step):
                    P.dma('pool', lambda e, k0=k0, view=view: e.dma_start(
                        out=view[:, k0:k0 + step, :],
                        in_=w_in[k0 * 128:(k0 + step) * 128, c0:c0 + ncols].rearrange("(k p) c -> p k c", p=128)),
                        writes=[('w', s, k0 // step)])
                return view, (lambda kt, s=s: ('w', s, kt // 8))

            def blocks():
                s0 = 0
                for nb in cfg.pre_blocks:
                    yield (False, s0, nb)
                    s0 += nb * 128
                for nb in cfg.main_blocks:
                    yield (True, s0, nb)
                    s0 += nb * 128

            def pieces(nb):
                res, c = [], 0
                while c < nb:
                    n = min(4, nb - c)
                    if nb - c > 4 and nb - c < 8:
                        n = (nb - c + 1) // 2
                    res.append((c * 128, n * 128))
                    c += n
                return res

            def load_hn(s0, ntok):
                for k0 in range(0, KT, 8):
                    P.dma('act', lambda e, k0=k0: e.dma_start(
                        out=hnb[:, k0:k0 + 8, 0:ntok],
                        in_=hnT[k0:k0 + 8, :, s0:s0 + ntok].rearrange("k p n -> p k n")),
                        writes=[('hnb', k0 // 8)])

            def fproj(c0, ncols, ntok, evac):
                for cc in range(0, ncols, WCOLS):
                    nc_ = min(WCOLS, ncols - cc)
                    view, wres = wload(c0 + cc, nc_)
                    for j in range((nc_ + 127) // 128):
                        m = min(128, nc_ - j * 128)
                        for (po, pn) in pieces(ntok // 128):
                            pi = next_pA()
                            for kt in range(KT):
                                P.op('pe', lambda e, pi=pi, view=view, j=j, m=m, kt=kt, po=po, pn=pn: e.matmul(
                                    pA[pi][0:m, 0:pn], lhsT=view[:, kt, j * 128:j * 128 + m], rhs=hnb[:, kt, po:po + pn],
                                    start=(kt == 0), stop=(kt == KT - 1)),
                                    reads=[wres(kt), ('hnb', kt // 8)], writes=[('pA', pi)], signal=(kt == KT - 1))
                                if kt == KT // 2 - 1:
                                    yield
                            evac(cc // 128 + j, po, pn, pi)
                            yield

            def tproj(c0, ncols, ntok, evac):
                for cc in range(0, ncols, WCOLS):
                    nv = min(WCOLS, ncols - cc)
                    view, wres = wload(c0 + cc, nv)
                    for tt in range(ntok // 128):
                        pi = next_pA()
                        for kt in range(KT):
                            P.op('pe', lambda e, pi=pi, view=view, nv=nv, kt=kt, tt=tt: e.matmul(
                                pA[pi][:, 0:nv], lhsT=hnb[:, kt, tt * 128:(tt + 1) * 128], rhs=view[:, kt, :],
                                start=(kt == 0), stop=(kt == KT - 1)),
                                reads=[wres(kt), ('hnb', kt // 8)], writes=[('pA', pi)], signal=(kt == KT - 1))
                            if kt == KT // 2 - 1:
                                yield
                        evac(tt, cc, nv, pi)
                        yield

            def drain(gen):
                for _ in gen:
                    pass

            def interleave(ga, gb):
                a_done = b_done = False
                while not (a_done and b_done):
                    if not a_done:
                        try:
                            next(ga)
                        except StopIteration:
                            a_done = True
                    if not b_done:
                        try:
                            next(gb)
                        except StopIteration:
                            b_done = True

            def gates(u, ntok, dv, c_o, c_m):
                ch0 = u * dv

                def ev_o(tt, co, ncp, pi):
                    P.op('act', lambda e: e.activation(out=tg[:, 0:ncp], in_=pA[pi][:, 0:ncp], func=AF.Exp, scale=-1.0),
                         reads=[('pA', pi)], writes=['tg'])
                    P.op('act', lambda e: e.activation(out=tg[:, 0:ncp], in_=tg[:, 0:ncp], func=AF.Ln, bias=oneT[:, 0:1]), reads=['tg', 'oneT'], writes=['tg'])
                    P.op('act', lambda e: e.activation(out=tg[:, 0:ncp], in_=tg[:, 0:ncp], func=AF.Exp, scale=-1.0), reads=['tg'], writes=['tg'])
                    P.op('dve', lambda e: e.tensor_tensor(out=tg[:, 0:ncp], in0=pA[pi][:, 0:ncp], in1=tg[:, 0:ncp], op=ALU.mult),
                         reads=[('pA', pi), 'tg'], writes=['tg'])
                    P.op('dve', lambda e: e.tensor_tensor(out=yn[:, tt, co:co + ncp], in0=tg[:, 0:ncp], in1=gnB[:, co:co + ncp], op=ALU.mult),
                         reads=['tg', 'gnB'], writes=[('yn', tt)])
                yield from tproj(c_o + ch0, dv, ntok, ev_o)

                def ev_m(tt, co, ncp, pi):
                    P.op('act', lambda e: e.activation(out=tg[:, 0:ncp], in_=pA[pi][:, 0:ncp], func=AF.Exp, scale=-1.0),
                         reads=[('pA', pi)], writes=['tg'])
                    P.op('act', lambda e: e.activation(out=tg[:, 0:ncp], in_=tg[:, 0:ncp], func=AF.Ln, bias=oneT[:, 0:1]), reads=['tg', 'oneT'], writes=['tg'])
                    P.op('act', lambda e: e.activation(out=tg[:, 0:ncp], in_=tg[:, 0:ncp], func=AF.Exp, scale=-1.0), reads=['tg'], writes=['tg'])
                    P.op('dve', lambda e: e.tensor_tensor(out=yn[:, tt, co:co + ncp], in0=yn[:, tt, co:co + ncp], in1=tg[:, 0:ncp], op=ALU.mult),
                         reads=[('yn', tt), 'tg'], writes=[('yn', tt)])
                yield from tproj(c_m + ch0, dv, ntok, ev_m)

            def store_y(is_ret, u, s0, c, dv, dst):
                m0 = s0 - PRE0
                ch0 = u * dv
                P.op('dve', lambda e: e.tensor_tensor(out=yn[:, c, 0:dv], in0=yr[:, c, 0:dv], in1=yn[:, c, 0:dv], op=ALU.mult),
                     reads=[('yr', c), ('yn', c)], writes=[('yn', c)])
                P.dma('sp', lambda e: e.dma_start(out=dst[m0 + c * 128:m0 + (c + 1) * 128, ch0:ch0 + dv], in_=yn[:, c, 0:dv]),
                      reads=[('yn', c)], writes=[('dst', is_ret, u, s0, c)])

            def copy_evac(dstT, nm, scale=1.0):
                def ev(j, po, pn, pi):
                    P.op('act', lambda e: e.activation(out=dstT[:, j, po:po + pn], in_=pA[pi][:, 0:pn], func=AF.Copy, scale=scale),
                         reads=[('pA', pi)], writes=[nm])
                return ev

            def v_evac_to(vt, bs):
                def v_evac(tt, co, ncp, pi):
                    P.op('act', lambda e: e.activation(out=vt[:, tt, co:co + ncp], in_=pA[pi][:, 0:ncp], func=AF.Copy),
                         reads=[('pA', pi)], writes=[('vtok', bs)])
                return v_evac

            def skewed(nb, pre, post):
                yield from pre(0)
                for c in range(nb):
                    if c + 1 < nb:
                        yield from pre(c + 1)
                    yield from post(c)

            blks = list(blocks())

            def bufset(bi):
                is_main, _, _ = blks[bi]
                return 0 if is_main else bi % 2

            def run_unit(gemm, scan, gates_fn, finalize):
                drain(gemm(0))
                for bi in range(len(blks)):
                    is_main = blks[bi][0]
                    nxt = bi + 1 < len(blks)
                    if not is_main:
                        if nxt:
                            interleave(scan(bi), gemm(bi + 1))
                        else:
                            drain(scan(bi))
                    else:
                        interleave(scan(bi), gates_fn(bi))
                        finalize(bi)
                        if nxt:
                            drain(gemm(bi + 1))

            def ret_unit(h):
                P.dma('sp', lambda e: e.dma_start(out=tabk[:], in_=bcast_row(c_rdk[h:h + 1, :], 128)), writes=['tabk'])
                P.dma('sp', lambda e: e.dma_start(out=tabq[:], in_=bcast_row(c_rdq[h:h + 1, :], 128)), writes=['tabq'])
                P.dma('sp', lambda e: e.dma_start(out=maskT[:], in_=c_rmask[h]), writes=['maskT'])
                P.dma('sp', lambda e: e.dma_start(out=gnB[:, 0:512], in_=bcast_row(ret_norm[0:1, h * 512:(h + 1) * 512], 128)), writes=['gnB'])
                P.op('dve', lambda e: e.memset(S[:, 0:2, 0:512], 0.0), writes=['S'])
                P.op('dve', lambda e: e.memset(Sbf[:, 0:2, 0:512], 0.0), writes=['Sbf'])
                gC = gam[h] ** 128

                def rope_evac(dstT, nm):
                    hold = {}

                    def ev(j, po, pn, pi):
                        hold[(j, po)] = pi
                        if j == 0:
                            return
                        p1, p2 = hold[(0, po)], pi
                        cs, sn = cosT[:, po:po + pn], sinT[:, po:po + pn]
                        P.op('dve', lambda e: e.tensor_tensor(out=t1[:, 0:pn], in0=pA[p1][:, 0:pn], in1=cs, op=ALU.mult),
                             reads=[('pA', p1), 'cosT'], writes=['t1'])
                        P.op('dve', lambda e: e.tensor_tensor(out=t2[:, 0:pn], in0=pA[p2][:, 0:pn], in1=sn, op=ALU.mult),
                             reads=[('pA', p2), 'sinT'], writes=['t2'])
                        P.op('dve', lambda e: e.tensor_tensor(out=dstT[:, 0, po:po + pn], in0=t1[:, 0:pn], in1=t2[:, 0:pn], op=ALU.subtract),
                             reads=['t1', 't2'], writes=[nm])
                        P.op('dve', lambda e: e.tensor_tensor(out=t1[:, 0:pn], in0=pA[p2][:, 0:pn], in1=cs, op=ALU.mult),
                             reads=[('pA', p2), 'cosT'], writes=['t1'])
                        P.op('dve', lambda e: e.tensor_tensor(out=t2[:, 0:pn], in0=pA[p1][:, 0:pn], in1=sn, op=ALU.mult),
                             reads=[('pA', p1), 'sinT'], writes=['t2'])
                        P.op('dve', lambda e: e.tensor_tensor(out=dstT[:, 1, po:po + pn], in0=t1[:, 0:pn], in1=t2[:, 0:pn], op=ALU.add),
                             reads=['t1', 't2'], writes=[nm])
                    return ev

                def gemm(bi):
                    is_main, s0, nb = blks[bi]
                    bs = bufset(bi)
                    ntok = nb * 128
                    load_hn(s0, ntok)
                    P.dma('act', lambda e: e.dma_start(out=sinT[:, 0:ntok], in_=sinD[:, s0:s0 + ntok]), writes=['sinT'])
                    P.dma('act', lambda e: e.dma_start(out=cosT[:, 0:ntok], in_=cosD[:, s0:s0 + ntok]), writes=['cosT'])
                    if is_main:
                        yield from fproj(cfg.c_qr + h * 256, 256, ntok, rope_evac(qT, 'qT'))
                    yield from fproj(cfg.c_kr + h * 256, 256, ntok, rope_evac(kTs[bs], ('kT', bs)))
                    yield from tproj(cfg.c_vr + h * 512, 512, ntok, v_evac_to(vtoks[bs], bs))

                def scan(bi):
                    is_main, s0, nb = blks[bi]
                    bs = bufset(bi)
                    kT, vtok = kTs[bs], vtoks[bs]

                    def pre(c):
                        o = c * 128
                        b = c % 2
                        P.op('dve', lambda e: e.tensor_tensor(out=khT[b][:, 0:2, :], in0=kT[:, 0:2, o:o + 128],
                                                              in1=tabk[:].rearrange("p (j n) -> p j n", j=2), op=ALU.mult),
                             reads=[('kT', bs), 'tabk'], writes=[('khT', b)])
                        for j in range(2):
                            P.op('pe', lambda e, j=j: e.transpose(pT[:, j, :], khT[b][:, j, :], ident[:]),
                                 reads=[('khT', b), 'ident'], writes=['pT'], signal=(j == 1))
                        P.op('act', lambda e: e.activation(out=khat[b][:, 0:256], in_=pT[:, 0:2, :].rearrange("p j n -> p (j n)"), func=AF.Copy),
                             reads=['pT'], writes=[('khat', b)])
                        yield
                        if is_main:
                            for j in range(2):
                                P.op('pe', lambda e, j=j: e.matmul(pC[:, 0:128], lhsT=kT[:, j, o:o + 128], rhs=qT[:, j, o:o + 128],
                                                                   start=(j == 0), stop=(j == 1)),
                                     reads=[('kT', bs), 'qT'], writes=['pC'], signal=(j == 1))
                            P.op('dve', lambda e: e.tensor_tensor(out=sT[b][:], in0=pC[:, 0:128], in1=maskT[:], op=ALU.mult),
                                 reads=['pC', 'maskT'], writes=[('sT', b)])
                            P.op('dve', lambda e: e.tensor_tensor(out=qd[b][:, 0:2, :], in0=qT[:, 0:2, o:o + 128],
                                                                  in1=tabq[:].rearrange("p (j n) -> p j n", j=2), op=ALU.mult),
                                 reads=['qT', 'tabq'], writes=[('qd', b)])
                            yield

                    def post(c):
                        b = c % 2
                        if is_main:
                            P.op('pe', lambda e: e.matmul(pB[0][:, :], lhsT=sT[b][:], rhs=vtok[:, c, 0:512], start=True, stop=False),
                                 reads=[('sT', b), ('vtok', bs)], writes=[('pB', 0)], signal=False)
                            for j in range(2):
                                P.op('pe', lambda e, j=j: e.matmul(pB[0][:, :], lhsT=qd[b][:, j, :], rhs=Sbf[:, j, 0:512], start=False, stop=(j == 1)),
                                     reads=[('qd', b), 'Sbf'], writes=[('pB', 0)], signal=(j == 1))
                            P.op('dve', lambda e: e.bn_stats(out=st[:, 0:6], in_=pB[0][:, :]), reads=[('pB', 0)], writes=['st'])
                            P.op('dve', lambda e: e.bn_aggr(out=st[:, 6:8], in_=st[:, 0:6]), reads=['st'], writes=['st'])
                            P.op('act', lambda e: e.activation(out=st[:, 7:8], in_=st[:, 7:8], func=AF.Ln, bias=epsT[:, 0:1], scale=1.0),
                                 reads=['st', 'epsT'], writes=['st'])
                            P.op('act', lambda e: e.activation(out=st[:, 7:8], in_=st[:, 7:8], func=AF.Exp, scale=-0.5), reads=['st'], writes=['st'])
                            P.op('dve', lambda e: e.tensor_scalar(out=yr[:, c, 0:512], in0=pB[0][:, :], scalar1=st[:, 6:7], scalar2=st[:, 7:8],
                                                                  op0=ALU.subtract, op1=ALU.mult),
                                 reads=[('pB', 0), 'st'], writes=[('yr', c)])
                            yield
                        for j in range(2):
                            P.op('pe', lambda e, j=j: e.matmul(pB[1][:, :], lhsT=khat[b][:, j * 128:(j + 1) * 128], rhs=vtok[:, c, 0:512], start=True, stop=True),
                                 reads=[('khat', b), ('vtok', bs)], writes=[('pB', 1)])
                            P.op('dve', lambda e, j=j: e.scalar_tensor_tensor(out=S[:, j, 0:512], in0=S[:, j, 0:512], scalar=gC, in1=pB[1][:, :],
                                                                              op0=ALU.mult, op1=ALU.add),
                                 reads=['S', ('pB', 1)], writes=['S'])
                            P.op('act', lambda e, j=j: e.activation(out=Sbf[:, j, 0:512], in_=S[:, j, 0:512], func=AF.Copy),
                                 reads=['S'], writes=['Sbf'])
                            yield
                    yield from skewed(nb, pre, post)

                def gates_fn(bi):
                    _, s0, nb = blks[bi]
                    yield from gates(h, nb * 128, 512, cfg.c_or, cfg.c_mr)

                def finalize(bi):
                    _, s0, nb = blks[bi]
                    for c in range(nb):
                        store_y(True, h, s0, c, 512, MR)
                run_unit(gemm, scan, gates_fn, finalize)

            def gla_unit(g):
                P.dma('sp', lambda e: e.dma_start(out=gnB[:, 0:1024], in_=bcast_row(gla_norm[0:1, g * 1024:(g + 1) * 1024], 128)), writes=['gnB'])
                P.dma('pool', lambda e: e.dma_start(out=wg[:], in_=w_gate[:, g * 512:(g + 1) * 512]), writes=['wg'])
                P.op('dve', lambda e: e.memset(S[:], 0.0), writes=['S'])
                P.op('dve', lambda e: e.memset(Sbf[:], 0.0), writes=['Sbf'])
                for i_ in range(2):
                    P.op('dve', lambda e, i_=i_: e.memset(zTs[i_][:], 1.0), writes=[('zT', i_)])

                def gemm(bi):
                    is_main, s0, nb = blks[bi]
                    bs = bufset(bi)
                    ntok = nb * 128
                    load_hn(s0, ntok)
                    if is_main:
                        yield from fproj(cfg.c_qg + g * 512, 512, ntok, copy_evac(qT, 'qT', scale=512 ** -0.5))
                    yield from fproj(cfg.c_kg + g * 512, 512, ntok, copy_evac(kTs[bs], ('kT', bs)))

                    def z_evac(j, po, pn, pi):
                        P.op('act', lambda e: e.activation(out=zTs[bs][0:16, po:po + pn], in_=pA[pi][0:16, 0:pn], func=AF.Copy),
                             reads=[('pA', pi)], writes=[('zT', bs)])
                    yield from fproj(cfg.c_zg, 16, ntok, z_evac)
                    yield from tproj(cfg.c_vg + g * 1024, 1024, ntok, v_evac_to(vtoks[bs], bs))

                def scan(bi):
                    is_main, s0, nb = blks[bi]
                    bs = bufset(bi)
                    kT, vtok, zT = kTs[bs], vtoks[bs], zTs[bs]

                    def pre(c):
                        o = c * 128
                        b = c % 2
                        P.op('pe', lambda e: e.matmul(pC[:, :], lhsT=zT[:, o:o + 128], rhs=wg[:, :], start=True, stop=True),
                             reads=[('zT', bs), 'wg'], writes=['pC'])
                        yield
                        P.op('act', lambda e: e.activation(out=spf[b][:], in_=pC[:, :], func=AF.Exp, scale=-1.0), reads=['pC'], writes=[('spf', b)])
                        P.op('act', lambda e: e.activation(out=spf[b][:], in_=spf[b][:], func=AF.Ln, bias=oneT[:, 0:1]), reads=[('spf', b), 'oneT'], writes=[('spf', b)])
                        for j in range(4):
                            P.op('pe', lambda e, j=j: e.matmul(pC[:, j * 128:(j + 1) * 128], lhsT=spf[b][:, j * 128:(j + 1) * 128], rhs=tri[:, :], start=True, stop=True),
                                 reads=[('spf', b), 'tri'], writes=['pC'], signal=(j == 3))
                        yield
                        P.op('act', lambda e: e.activation(out=eb[b][:].rearrange("p j n -> p (j n)"), in_=pC[:, :], func=AF.Exp), reads=['pC'], writes=[('eb', b)])
                        P.op('act', lambda e: e.activation(out=enb[b][:].rearrange("p j n -> p (j n)"), in_=pC[:, :], func=AF.Exp, scale=-1.0), reads=['pC'], writes=[('enb', b)])
                        P.op('dve', lambda e: e.tensor_tensor(out=kd[b][:], in0=kT[:, :, o:o + 128], in1=enb[b][:], op=ALU.mult),
                             reads=[('kT', bs), ('enb', b)], writes=[('kd', b)])
                        for j in range(4):
                            P.op('dve', lambda e, j=j: e.tensor_scalar(out=khT[b][:, j, :], in0=kd[b][:, j, :], scalar1=eb[b][:, j, 127:128], scalar2=None, op0=ALU.mult),
                                 reads=[('kd', b), ('eb', b)], writes=[('khT', b)])
                        for j in range(4):
                            P.op('pe', lambda e, j=j: e.transpose(pT[:, j, :], khT[b][:, j, :], ident[:]),
                                 reads=[('khT', b), 'ident'], writes=['pT'], signal=(j == 3))
                        P.op('act', lambda e: e.activation(out=khat[b][:], in_=pT[:].rearrange("p j n -> p (j n)"), func=AF.Copy),
                             reads=['pT'], writes=[('khat', b)])
                        yield
                        if is_main:
                            P.op('dve', lambda e: e.tensor_tensor(out=qd[b][:], in0=qT[:, :, o:o + 128], in1=eb[b][:], op=ALU.mult),
                                 reads=['qT', ('eb', b)], writes=[('qd', b)])
                            for j in range(4):
                                P.op('pe', lambda e, j=j: e.matmul(pC[:, 0:128], lhsT=kd[b][:, j, :], rhs=qd[b][:, j, :], start=(j == 0), stop=(j == 3)),
                                     reads=[('kd', b), ('qd', b)], writes=['pC'], signal=(j == 3))
                            P.op('dve', lambda e: e.tensor_tensor(out=sT[b][:], in0=pC[:, 0:128], in1=gmask[:], op=ALU.mult),
                                 reads=['pC', 'gmask'], writes=[('sT', b)])
                            yield

                    def post(c):
                        b = c % 2
                        if is_main:
                            P.op('dve', lambda e: e.memset(st[:, 0:2], 0.0), writes=['st'])
                            for vh in range(2):
                                P.op('pe', lambda e, vh=vh: e.matmul(pB[vh][:, :], lhsT=sT[b][:], rhs=vtok[:, c, vh * 512:(vh + 1) * 512], start=True, stop=False),
                                     reads=[('sT', b), ('vtok', bs)], writes=[('pB', vh)], signal=False)
                                for j in range(4):
                                    P.op('pe', lambda e, j=j, vh=vh: e.matmul(pB[vh][:, :], lhsT=qd[b][:, j, :], rhs=Sbf[:, j, vh * 512:(vh + 1) * 512], start=False, stop=(j == 3)),
                                         reads=[('qd', b), 'Sbf'], writes=[('pB', vh)], signal=(j == 3))
                                P.op('act', lambda e, vh=vh: e.activation(out=t1[:], in_=pB[vh][:, :], func=AF.Square, accum_out=st[:, vh:vh + 1]),
                                     reads=[('pB', vh)], writes=['t1', 'st'])
                            P.op('dve', lambda e: e.tensor_tensor(out=st[:, 2:3], in0=st[:, 0:1], in1=st[:, 1:2], op=ALU.add), reads=['st'], writes=['st'])
                            P.op('act', lambda e: e.activation(out=st[:, 2:3], in_=st[:, 2:3], func=AF.Ln, bias=epsT[:, 0:1], scale=1.0 / 1024),
                                 reads=['st', 'epsT'], writes=['st'])
                            P.op('act', lambda e: e.activation(out=st[:, 2:3], in_=st[:, 2:3], func=AF.Exp, scale=-0.5), reads=['st'], writes=['st'])
                            for vh in range(2):
                                P.op('act', lambda e, vh=vh: e.activation(out=yr[:, c, vh * 512:(vh + 1) * 512], in_=pB[vh][:, :], func=AF.Copy, scale=st[:, 2:3]),
                                     reads=[('pB', vh), 'st'], writes=[('yr', c)])
                            yield
                        for j in range(4):
                            for vh in range(2):
                                P.op('pe', lambda e, j=j, vh=vh: e.matmul(pB[vh][:, :], lhsT=khat[b][:, j * 128:(j + 1) * 128], rhs=vtok[:, c, vh * 512:(vh + 1) * 512], start=True, stop=True),
                                     reads=[('khat', b), ('vtok', bs)], writes=[('pB', vh)])
                                P.op('dve', lambda e, j=j, vh=vh: e.scalar_tensor_tensor(out=S[:, j, vh * 512:(vh + 1) * 512], in0=S[:, j, vh * 512:(vh + 1) * 512],
                                                                                          scalar=eb[b][:, j, 127:128], in1=pB[vh][:, :], op0=ALU.mult, op1=ALU.add),
                                     reads=['S', ('pB', vh), ('eb', b)], writes=['S'])
                            P.op('act', lambda e, j=j: e.activation(out=Sbf[:, j, :], in_=S[:, j, :], func=AF.Copy), reads=['S'], writes=['Sbf'])
                            yield
                    yield from skewed(nb, pre, post)

                def gates_fn(bi):
                    _, s0, nb = blks[bi]
                    yield from gates(g, nb * 128, 1024, cfg.c_og, cfg.c_mg)

                def finalize(bi):
                    _, s0, nb = blks[bi]
                    for c in range(nb):
                        store_y(False, g, s0, c, 1024, MG)
                run_unit(gemm, scan, gates_fn, finalize)

            for g in range(cfg.GH):
                gla_unit(g)
            for h in range(cfg.RH):
                ret_unit(h)
            P.barrier()
            P.flush()

        NT = NREAL // 128
        with ExitStack() as es:
            def sb(name, shape, dt):
                return es.enter_context(nc.sbuf_tensor(name, list(shape), dt))

            def psm(name, shape, dt):
                return es.enter_context(nc.psum_tensor(name, list(shape), dt))
            ident = sb("p2_ident", [128, 128], BF16)
            aT = sb("p2_aT", [128, KT, NREAL], BF16)
            aTh = sb("p2_aTh", [128, KT, 2], BF16)
            m1 = [sb(f"p2_m1{i}", [128, D], BF16) for i in range(2)]
            m2s = sb("p2_m2", [128, D], BF16)
            m2 = [m2s, m2s]
            OC = 256
            warena = sb("p2_warena", [128, KT * OC * 3], BF16)
            xp = [sb(f"p2_xp{i}", [128, OC], F32) for i in range(2)]
            gB = sb("p2_gB", [128, D], F32)
            hrow = sb("p2_hrow", [128, D], F32)
            ss = sb("p2_ss", [128, 4], F32)
            cw = sb("p2_cw", [128, 4, FT], F32)
            upb = sb("p2_upb", [128, NREAL + 2], F32)
            cc = sb("p2_cc", [128, NREAL], F32)
            sg = sb("p2_sg", [128, NREAL], F32)
            ab = [sb(f"p2_ab{i}", [128, NREAL], BF16) for i in range(2)]
            pA = [psm(f"p2_pA{i}", [128, 512], F32) for i in range(6)]
            pH = psm("p2_pH", [128, 512], F32)
            pT = psm("p2_pT", [128, 8, 128], BF16)
            epsT = sb("p2_eps", [128, 1], F32)
            P.op('dve', lambda e: e.memset(epsT[:], EPS), writes=['epsT'])
            P.dma('sp', lambda e: e.dma_start(out=ident[:], in_=c_ident_bf), writes=['ident'])
            P.dma('sp', lambda e: e.dma_start(out=gB[:], in_=bcast_row(ffn_norm[0:1, :], 128)), writes=['gB'])
            for i in range(3):
                P.dma('sp', lambda e, i=i: e.dma_start(out=cw[:, i, :], in_=conv_w[i:i + 1, :].rearrange("o (f p) -> p (o f)", p=128), allow_slow_non_contiguous=True), writes=['cw'])
            P.dma('sp', lambda e: e.dma_start(out=cw[:, 3, :], in_=conv_b[0:1, :].rearrange("o (f p) -> p (o f)", p=128), allow_slow_non_contiguous=True), writes=['cw'])

            def transposes_to(src_bf, srcres, dst_fn, rows=128, keep=None):
                for g4 in range(KT // 8):
                    for j in range(8):
                        kt = g4 * 8 + j
                        P.op('pe', lambda e, kt=kt, j=j: e.transpose(pT[:, j, 0:rows], src_bf[0:rows, kt * 128:(kt + 1) * 128], ident[0:rows, 0:rows]),
                             reads=[srcres, 'ident'], writes=['pT'], signal=(j == 7))
                    dst, dres = dst_fn(g4 * 8)
                    k0_, k1_ = keep if keep else (0, rows)
                    P.op('act', lambda e, dst=dst, k0_=k0_, k1_=k1_: e.activation(out=dst, in_=pT[:, :, k0_:k1_], func=AF.Copy),
                         reads=['pT'], writes=[dres])

            for ti, t in enumerate([0] + list(range(1, NT + 1))):
                b = ti % 2
                P.dma('sp', lambda e, t=t, b=b: e.dma_start(out=m1[b][:], in_=MR[t * 128:(t + 1) * 128, :]), writes=[('m1', b)])
                P.dma('sp', lambda e, t=t, b=b: e.dma_start(out=m2[b][:], in_=MG[t * 128:(t + 1) * 128, :]), writes=['m2'])
                P.op('dve', lambda e, b=b: e.tensor_tensor(out=m1[b][:], in0=m1[b][:], in1=m2[b][:], op=ALU.add),
                     reads=[('m1', b), 'm2'], writes=[('m1', b)])
                if t == 0:
                    transposes_to(m1[b], ('m1', b), lambda k0: (aTh[:, k0:k0 + 8, :], 'aTh'), rows=128, keep=(126, 128))
                else:
                    transposes_to(m1[b], ('m1', b), lambda k0, t=t: (aT[:, k0:k0 + 8, (t - 1) * 128:t * 128], 'aT'))

            w2ctr = [0]

            def wload2(src, r0, nrows_t, c0, ncols, nslots, tag):
                s = w2ctr[0] % nslots
                w2ctr[0] += 1
                sz = nrows_t * ncols
                view = warena[:, s * sz:(s + 1) * sz].rearrange("p (k c) -> p k c", k=nrows_t)
                step = 8
                for k0 in range(0, nrows_t, step):
                    k1 = min(nrows_t, k0 + step)
                    P.dma('pool', lambda e, k0=k0, k1=k1, view=view: e.dma_start(
                        out=view[:, k0:k1, :],
                        in_=src[r0 + k0 * 128:r0 + k1 * 128, c0:c0 + ncols].rearrange("(k p) c -> p k c", p=128)),
                        writes=[(tag, s, k0 // step)])
                return view, (lambda kt, s=s, tag=tag: (tag, s, kt // 8))

            pa2 = [0]
            for cg in range(D // OC):
                view, wres = wload2(w_out, 0, KT, cg * OC, OC, 3, 'wo')
                for t in range(0, NT + 1):
                    pi = pa2[0] % 6
                    pa2[0] += 1
                    b = pa2[0] % 2
                    if t == 0:
                        lhs = lambda kt: aTh[:, kt, :]
                        rows, r0, lres = 2, 126, 'aTh'
                    else:
                        lhs = lambda kt, t=t: aT[:, kt, (t - 1) * 128:t * 128]
                        rows, r0, lres = 128, t * 128, 'aT'
                    for kt in range(KT):
                        P.op('pe', lambda e, kt=kt, pi=pi, lhs=lhs, rows=rows, view=view: e.matmul(pA[pi][0:rows, 0:OC], lhsT=lhs(kt), rhs=view[:, kt, :],
                                                                                                  start=(kt == 0), stop=(kt == KT - 1)),
                             reads=[wres(kt), lres], writes=[('pA', pi)], signal=(kt == KT - 1))
                    P.dma('sp', lambda e, b=b, rows=rows, r0=r0, cg=cg: e.dma_start(out=xp[b][0:rows, :], in_=xs[PRE0 + r0:PRE0 + r0 + rows, cg * OC:(cg + 1) * OC]),
                          writes=[('xp', b)])
                    P.op('dve', lambda e, b=b, rows=rows, pi=pi: e.tensor_tensor(out=xp[b][0:rows, :], in0=pA[pi][0:rows, 0:OC], in1=xp[b][0:rows, :], op=ALU.add),
                         reads=[('pA', pi), ('xp', b)], writes=[('xp', b)])
                    P.dma('sp', lambda e, b=b, rows=rows, r0=r0, cg=cg: e.dma_start(out=H1[r0:r0 + rows, cg * OC:(cg + 1) * OC], in_=xp[b][0:rows, :]),
                          reads=[('xp', b)], writes=[('H1', t, cg)])

            for ti, t in enumerate(range(0, NT + 1)):
                b = ti % 2
                rows, r0 = (2, 126) if t == 0 else (128, t * 128)
                P.dma('sp', lambda e, rows=rows, r0=r0: e.dma_start(out=hrow[0:rows, :], in_=H1[r0:r0 + rows, :]), reads=[('H1', t, cg_) for cg_ in range(D // OC)], writes=['hrow'])
                P.op('dve', lambda e: e.memset(ss[:, 0:1], 0.0), writes=['ss'])
                P.op('act', lambda e, b=b, rows=rows: e.activation(out=m2[b][0:rows, :], in_=hrow[0:rows, :], func=AF.Square, accum_out=ss[0:rows, 0:1]),
                     reads=['hrow'], writes=['m2', 'ss'])
                P.op('act', lambda e, rows=rows: e.activation(out=ss[0:rows, 1:2], in_=ss[0:rows, 0:1], func=AF.Sqrt, bias=epsT[0:rows, 0:1], scale=1.0 / D),
                     reads=['ss', 'epsT'], writes=['ss'])
                P.op('dve', lambda e, rows=rows: e.reciprocal(out=ss[0:rows, 1:2], in_=ss[0:rows, 1:2]), reads=['ss'], writes=['ss'])
                P.op('dve', lambda e, b=b, rows=rows: e.scalar_tensor_tensor(out=m1[b][0:rows, :], in0=hrow[0:rows, :], scalar=ss[0:rows, 1:2], in1=gB[0:rows, :],
                                                                           op0=ALU.mult, op1=ALU.mult),
                     reads=['hrow', 'ss', 'gB'], writes=[('m1', b)])
                if t == 0:
                    transposes_to(m1[b], ('m1', b), lambda k0: (aTh[:, k0:k0 + 8, :], 'aTh'), rows=2, keep=(0, 2))
                else:
                    transposes_to(m1[b], ('m1', b), lambda k0, t=t: (aT[:, k0:k0 + 8, (t - 1) * 128:t * 128], 'aT'))

            P.barrier()
            NP = [(o, min(512, NREAL - o)) for o in range(0, NREAL, 512)]
            assert len(NP) <= 2
            for m in range(FT):
                view, wres = wload2(w_ffn_in, 0, KT, m * 128, 128, 6, 'wf')
                viewg, wresg = wload2(w_ffn_in, 0, KT, DFF + m * 128, 128, 6, 'wf')
                for kt in range(KT):
                    P.op('pe', lambda e, kt=kt, view=view: e.matmul(pH[:, 0:2], lhsT=view[:, kt, :], rhs=aTh[:, kt, :], start=(kt == 0), stop=(kt == KT - 1)),
                         reads=[wres(kt), 'aTh'], writes=['pH'], signal=(kt == KT - 1))
                P.op('act', lambda e: e.activation(out=upb[:, 0:2], in_=pH[:, 0:2], func=AF.Copy), reads=['pH'], writes=['upb'])
                ups, gts = [], []
                for (po, pn) in NP:
                    pi = pa2[0] % 6
                    pa2[0] += 1
                    for kt in range(KT):
                        P.op('pe', lambda e, kt=kt, pi=pi, po=po, pn=pn, view=view: e.matmul(pA[pi][:, 0:pn], lhsT=view[:, kt, :], rhs=aT[:, kt, po:po + pn],
                                                                                             start=(kt == 0), stop=(kt == KT - 1)),
                             reads=[wres(kt), 'aT'], writes=[('pA', pi)], signal=(kt == KT - 1))
                    P.op('act', lambda e, pi=pi, po=po, pn=pn: e.activation(out=upb[:, 2 + po:2 + po + pn], in_=pA[pi][:, 0:pn], func=AF.Copy),
                         reads=[('pA', pi)], writes=['upb'])
                for (po, pn) in NP:
                    pi = pa2[0] % 6
                    pa2[0] += 1
                    for kt in range(KT):
                        P.op('pe', lambda e, kt=kt, pi=pi, po=po, pn=pn, viewg=viewg: e.matmul(pA[pi][:, 0:pn], lhsT=viewg[:, kt, :], rhs=aT[:, kt, po:po + pn],
                                                                                               start=(kt == 0), stop=(kt == KT - 1)),
                             reads=[wresg(kt), 'aT'], writes=[('pA', pi)], signal=(kt == KT - 1))
                    gts.append((pi, po, pn))
                P.op('dve', lambda e, m=m: e.tensor_scalar(out=cc[:], in0=upb[:, 2:2 + NREAL], scalar1=cw[:, 2, m:m + 1], scalar2=cw[:, 3, m:m + 1], op0=ALU.mult, op1=ALU.add),
                     reads=['upb', 'cw'], writes=['cc'])
                P.op('dve', lambda e, m=m: e.scalar_tensor_tensor(out=cc[:], in0=upb[:, 1:1 + NREAL], scalar=cw[:, 1, m:m + 1], in1=cc[:], op0=ALU.mult, op1=ALU.add),
                     reads=['upb', 'cw', 'cc'], writes=['cc'])
                P.op('dve', lambda e, m=m: e.scalar_tensor_tensor(out=cc[:], in0=upb[:, 0:NREAL], scalar=cw[:, 0, m:m + 1], in1=cc[:], op0=ALU.mult, op1=ALU.add),
                     reads=['upb', 'cw', 'cc'], writes=['cc'])
                P.op('act', lambda e: e.activation(out=sg[:], in_=cc[:], func=AF.Sigmoid), reads=['cc'], writes=['sg'])
                P.op('dve', lambda e: e.tensor_tensor(out=cc[:], in0=cc[:], in1=sg[:], op=ALU.mult), reads=['cc', 'sg'], writes=['cc'])
                ba = m % 2
                for (pi, po, pn) in gts:
                    P.op('dve', lambda e, pi=pi, po=po, pn=pn, ba=ba: e.tensor_tensor(out=ab[ba][:, po:po + pn], in0=pA[pi][:, 0:pn], in1=cc[:, po:po + pn], op=ALU.mult),
                         reads=[('pA', pi), 'cc'], writes=[('ab', ba)])
                P.dma('sp', lambda e, m=m, ba=ba: e.dma_start(out=ACTS[m, :, :], in_=ab[ba][:]), reads=[('ab', ba)], writes=[('ACTS', m)])
            P.barrier()
            P.flush()

        with ExitStack() as es:
            def sb(name, shape, dt):
                return es.enter_context(nc.sbuf_tensor(name, list(shape), dt))

            def psm(name, shape, dt):
                return es.enter_context(nc.psum_tensor(name, list(shape), dt))
            identf = sb("p3_identf", [128, 128], F32)
            actT = sb("p3_actT", [128, FT, 512], BF16)
            wb3 = [sb(f"p3_w{i}", [128, 8 * 512], BF16) for i in range(4)]
            o2 = [sb(f"p3_o2{i}", [128, 512], F32) for i in range(2)]
            fo = [sb(f"p3_fo{i}", [128, 4, 128], F32) for i in range(2)]
            gB = sb("p3_gB", [128, D], F32)
            r1 = sb("p3_r1", [128, D], F32)
            r2 = sb("p3_r2", [128, D], F32)
            ss = sb("p3_ss", [128, 4], F32)
            pA = [psm(f"p3_pA{i}", [128, 512], F32) for i in range(4)]
            pT = [psm(f"p3_pT{i}", [128, 4, 128], F32) for i in range(2)]
            epsT = sb("p3_eps", [128, 1], F32)
            P.op('dve', lambda e: e.memset(epsT[:], EPS), writes=['epsT'])
            P.dma('sp', lambda e: e.dma_start(out=identf[:], in_=c_ident_f), writes=['identf'])
            P.dma('sp', lambda e: e.dma_start(out=gB[:], in_=bcast_row(final_norm[0:1, :], 128)), writes=['gB'])
            w3ctr = [0]
            cnt = 0
            JG = 8
            NW3 = 4
            CG = min(512, D)
            for (po, pn) in [(o, min(512, NREAL - o)) for o in range(0, NREAL, 512)]:
                for f0 in range(0, FT, 8):
                    f1 = min(FT, f0 + 8)
                    P.dma('sp', lambda e, f0=f0, f1=f1, po=po, pn=pn: e.dma_start(out=actT[:, f0:f1, 0:pn],
                                                                              in_=ACTS[f0:f1, :, po:po + pn].rearrange("f p n -> p f n")),
                          writes=[('actT', f0 // 8)])
                nsub = pn // 128
                for cg in range(D // CG):
                    nct = CG // 128
                    for j0 in range(0, FT, JG):
                        j1 = min(FT, j0 + JG)
                        s_ = w3ctr[0] % NW3
                        w3ctr[0] += 1
                        view = wb3[s_][:, 0:(j1 - j0) * CG].rearrange("p (k c) -> p k c", k=j1 - j0)
                        P.dma('pool', lambda e, j0=j0, j1=j1, view=view, cg=cg: e.dma_start(
                            out=view,
                            in_=w_ffn_out[j0 * 128:j1 * 128, cg * CG:(cg + 1) * CG].rearrange("(k p) c -> p k c", p=128)),
                            writes=[('w3', s_)])
                        for ct in range(nct):
                            for jt in range(j0, j1):
                                last = (ct == nct - 1 and jt == j1 - 1)
                                P.op('pe', lambda e, jt=jt, j0=j0, ct=ct, view=view, pn=pn: e.matmul(
                                    pA[ct][:, 0:pn], lhsT=view[:, jt - j0, ct * 128:(ct + 1) * 128], rhs=actT[:, jt, 0:pn],
                                    start=(jt == 0), stop=(jt == FT - 1)),
                                    reads=[('w3', s_), ('actT', jt // 8)], writes=[('pA', ct)], signal=(last or jt == FT - 1))
                    for ct in range(nct):
                        f = cg * nct + ct
                        b = cnt % 2
                        cnt += 1
                        P.op('act', lambda e, ct=ct, b=b, pn=pn: e.activation(out=o2[b][:, 0:pn], in_=pA[ct][:, 0:pn], func=AF.Copy),
                             reads=[('pA', ct)], writes=[('o2', b)])
                        for sidx in range(nsub):
                            P.op('pe', lambda e, sidx=sidx, b=b: e.transpose(pT[b][:, sidx, :], o2[b][:, sidx * 128:(sidx + 1) * 128], identf[:]),
                                 reads=[('o2', b), 'identf'], writes=[('pT', b)], signal=(sidx == nsub - 1))
                        P.op('dve', lambda e, b=b, nsub=nsub: e.tensor_copy(out=fo[b][:, 0:nsub, :], in_=pT[b][:, 0:nsub, :]),
                             reads=[('pT', b)], writes=[('fo', b)])
                        P.dma('sp', lambda e, b=b, nsub=nsub, po=po, f=f: e.dma_start(
                            out=FO[po:po + nsub * 128, f * 128:(f + 1) * 128].rearrange("(s p) c -> p s c", p=128),
                            in_=fo[b][:, 0:nsub, :]), reads=[('fo', b)], writes=[('FO', po, f)])
                for t in range(po // 128, (po + pn) // 128):
                    P.dma('sp', lambda e, t=t: e.dma_start(out=r1[:], in_=FO[t * 128:(t + 1) * 128, :]),
                          reads=[('FO', po, f_) for f_ in range(D // 128)], writes=['r1'])
                    P.dma('sp', lambda e, t=t: e.dma_start(out=r2[:], in_=H1[(t + 1) * 128:(t + 2) * 128, :]), writes=['r2'])
                    P.op('dve', lambda e: e.tensor_tensor(out=r1[:], in0=r1[:], in1=r2[:], op=ALU.add), reads=['r1', 'r2'], writes=['r1'])
                    P.op('dve', lambda e: e.memset(ss[:, 0:1], 0.0), writes=['ss'])
                    P.op('act', lambda e: e.activation(out=r2[:], in_=r1[:], func=AF.Square, accum_out=ss[:, 0:1]), reads=['r1'], writes=['r2', 'ss'])
                    P.op('act', lambda e: e.activation(out=ss[:, 1:2], in_=ss[:, 0:1], func=AF.Sqrt, bias=epsT[:, 0:1], scale=1.0 / D), reads=['ss', 'epsT'], writes=['ss'])
                    P.op('dve', lambda e: e.reciprocal(out=ss[:, 1:2], in_=ss[:, 1:2]), reads=['ss'], writes=['ss'])
                    P.op('dve', lambda e: e.scalar_tensor_tensor(out=r2[:], in0=r1[:], scalar=ss[:, 1:2], in1=gB[:], op0=ALU.mult, op1=ALU.mult),
                         reads=['r1', 'ss', 'gB'], writes=['r2'])
                    P.dma('sp', lambda e, t=t: e.dma_start(out=out[t * 128:(t + 1) * 128, :], in_=r2[:]), reads=['r2'], writes=[('out', t)])
            P.barrier()
            P.flush()
    P.check_deadlock()
    return nc


def host_constants(cfg):
    n = np.arange(128)
    c = {}
    c["c_ident_bf"] = np.eye(128, dtype=np.float32).astype(ml_dtypes.bfloat16)
    c["c_ident_f"] = np.eye(128, dtype=np.float32)
    c["c_invf"] = (10000.0 ** (-np.arange(128, dtype=np.float32) / 128)).astype(np.float32)[:, None]
    lg = np.log1p(-np.exp2(-5.0 - np.arange(cfg.RH, dtype=np.float64)))
    rdk = np.exp(lg[:, None] * (127 - n)[None, :]) / 16.0
    rdq = np.exp(lg[:, None] * (n + 1)[None, :])
    c["c_rdk"] = np.concatenate([rdk, rdk], axis=1).astype(np.float32)
    c["c_rdq"] = np.concatenate([rdq, rdq], axis=1).astype(np.float32)
    rel = n[None, :] - n[:, None]
    c["c_rmask"] = np.where(rel[None] >= 0, np.exp(lg[:, None, None] * np.maximum(rel, 0)[None]) / 16.0, 0.0).astype(np.float32)
    c["c_gmask"] = (rel >= 0).astype(np.float32)
    c["c_tri"] = np.where(rel >= 0, -1.0 / 16.0, 0.0).astype(np.float32)
    return c


def make_in_maps(cfg, x, positions, meta_tokens, attn_norm, w_in, w_gate_up, b_gate, ret_norm, gla_norm,
                 w_out, ffn_norm, w_ffn_in, conv_w, conv_b, w_ffn_out, final_norm):
    B, SEQ, D = x.shape
    f32 = np.float32
    consts = host_constants(cfg)
    shared = {
        "attn_norm": np.ascontiguousarray(attn_norm[0][None], f32),
        "w_in": np.ascontiguousarray(w_in[0], f32),
        "w_gate": np.ascontiguousarray(np.concatenate([w_gate_up[0], b_gate[0][None]], axis=0), f32),
        "ret_norm": np.ascontiguousarray(ret_norm[0][None], f32),
        "gla_norm": np.ascontiguousarray(gla_norm[0][None], f32),
        "w_out": np.ascontiguousarray(w_out[0], f32),
        "ffn_norm": np.ascontiguousarray(ffn_norm[0][None], f32),
        "w_ffn_in": np.ascontiguousarray(w_ffn_in[0], f32),
        "conv_w": np.ascontiguousarray(conv_w[0], f32),
        "conv_b": np.ascontiguousarray(conv_b[0][None], f32),
        "w_ffn_out": np.ascontiguousarray(w_ffn_out[0], f32),
        "final_norm": np.ascontiguousarray(final_norm[None], f32),
    }
    shared.update(consts)
    NPAD = 112
    in_maps = []
    half = SEQ // 2
    assert half == cfg.NREAL and NPAD + 16 + SEQ == cfg.NS
    metapos = (np.arange(16) - 16).astype(np.int32)
    for b in range(B):
        seq = np.concatenate([np.zeros((NPAD, D), f32), np.asarray(meta_tokens, f32), np.asarray(x[b], f32)], axis=0)
        pos = np.concatenate([np.zeros(NPAD, np.int32), metapos, np.asarray(positions[b], np.int32)])
        for s in range(2):
            if s == 0:
                xs = np.concatenate([np.zeros((cfg.NPRE * 128, D), f32), seq[0:cfg.NMAIN * 128]], axis=0)
                ps = np.concatenate([np.zeros(cfg.NPRE * 128, np.int32), pos[0:cfg.NMAIN * 128]])
            else:
                xs, ps = seq, pos
            m = dict(shared)
            m["xs"] = np.ascontiguousarray(xs)
            m["posr"] = np.ascontiguousarray(ps[None])
            in_maps.append(m)
    return in_maps


def run(cfg, inputs, trace=False):
    inputs = {k: np.asarray(v) for k, v in inputs.items()}
    nc = build_nc(cfg)
    in_maps = make_in_maps(cfg, **inputs)
    res = run_bass_kernel_spmd(nc, in_maps, core_ids=list(range(8)))
    B, SEQ, D = inputs["x"].shape
    out = np.zeros((B, SEQ, D), np.float32)
    for b in range(B):
        for s in range(2):
            out[b, s * cfg.NREAL:(s + 1) * cfg.NREAL] = res.results[2 * b + s]["out"]
    return out


def kernel(**inputs):
    return run(Cfg(), inputs)
```

```python
import math
from contextlib import ExitStack
import numpy as np
import ml_dtypes
import concourse.bass as bass
import concourse.mybir as mybir
from concourse.bass_utils import run_bass_kernel_spmd

F32 = mybir.dt.float32
BF16 = mybir.dt.bfloat16
I32 = mybir.dt.int32
AF = mybir.ActivationFunctionType
ALU = mybir.AluOpType
AX = mybir.AxisListType

ENGS = ('pe', 'act', 'dve', 'pool', 'sp')
NDMA_SEM = 12
EPS = 1e-6


class Prog:
    def __init__(self, nc):
        self.nc = nc
        self.q = {e: [] for e in ENGS}
        self.seq = {e: 0 for e in ENGS}
        self.ndma = {e: 0 for e in ENGS}
        self.waited = {e: {} for e in ENGS}
        self.res = {}
        self.sems = {}
        self.latest = {}
        self.trace = {e: [] for e in ENGS}

    def _deps(self, reads, writes):
        deps = {}

        def add(k, v):
            if deps.get(k, 0) < v:
                deps[k] = v
        for r in reads:
            st = self.res.get(r)
            if st:
                if st[0] is not None:
                    add(*st[0])
        for w in writes:
            st = self.res.get(w)
            if st:
                if st[0] is not None:
                    add(*st[0])
                for k, v in st[1].items():
                    add(k, v)
        return deps

    def _update(self, tok, reads, writes):
        k, v = tok
        if self.latest.get(k, 0) < v:
            self.latest[k] = v
        for r in reads:
            st = self.res.setdefault(r, [None, {}])
            if st[1].get(k, 0) < v:
                st[1][k] = v
        for w in writes:
            self.res[w] = [tok, {}]

    def _waits(self, eng, deps, skip_self=False):
        ws = []
        for k, v in deps.items():
            if skip_self and k == eng:
                continue
            if self.waited[eng].get(k, 0) < v:
                self.waited[eng][k] = v
                ws.append((k, v))
        return ws

    def op(self, eng, fn, reads=(), writes=(), signal=True):
        deps = self._deps(reads, writes)
        ws = self._waits(eng, deps, skip_self=(eng == 'pe'))
        if signal:
            self.seq[eng] += 1
            tok = (eng, self.seq[eng])
        else:
            tok = (eng, self.seq[eng] + 1)
        sems = self.sems

        def run(e, fn=fn, ws=ws, signal=signal, eng=eng):
            for k, v in ws:
                e.wait_ge(sems[k], v)
            ins = fn(e)
            if signal:
                ins.then_inc(sems[eng], 1)
        self.q[eng].append(run)
        self.trace[eng].append((list(ws), (eng, 1) if signal else None))
        self._update(tok, reads, writes)
        return tok

    def dma(self, queue, fn, reads=(), writes=()):
        n = self.ndma[queue]
        self.ndma[queue] += 1
        idx = n % NDMA_SEM
        cnt = n // NDMA_SEM + 1
        key = ('dma', queue, idx)
        deps = self._deps(reads, writes)
        if cnt > 1 and deps.get(key, 0) < 16 * (cnt - 1):
            deps[key] = 16 * (cnt - 1)
        ws = self._waits(queue, deps)
        tok = (key, 16 * cnt)
        sems = self.sems

        def run(e, fn=fn, ws=ws, key=key):
            for k, v in ws:
                e.wait_ge(sems[k], v)
            fn(e).then_inc(sems[key], 16)
        self.q[queue].append(run)
        self.trace[queue].append((list(ws), (key, 16)))
        self._update(tok, reads, writes)
        return tok

    def barrier(self):
        lat = dict(self.latest)
        sems = self.sems
        for eng in ENGS:
            ws = self._waits(eng, lat, skip_self=False)
            ws = [(k, v) for (k, v) in ws if k != eng or eng != 'pe']

            def run(e, ws=ws):
                for k, v in ws:
                    e.wait_ge(sems[k], v)
            self.q[eng].append(run)
            self.trace[eng].append((list(ws), None))
        self.res = {}

    def check_deadlock(self):
        val = {}
        pos = {e: 0 for e in ENGS}
        tr = self.trace
        while True:
            prog = False
            for e in ENGS:
                while pos[e] < len(tr[e]):
                    ws, inc = tr[e][pos[e]]
                    if all(val.get(k, 0) >= v for k, v in ws):
                        if inc:
                            val[inc[0]] = val.get(inc[0], 0) + inc[1]
                        pos[e] += 1
                        prog = True
                    else:
                        break
            if not prog:
                break
        stuck = {e: (pos[e], len(tr[e])) for e in ENGS if pos[e] < len(tr[e])}
        if stuck:
            msg = {e: (p, n, tr[e][p][0], {k: val.get(k, 0) for k, _ in tr[e][p][0]}) for e, (p, n) in stuck.items()}
            raise RuntimeError(f"semaphore deadlock: {msg}")

    def alloc_sems(self, es):
        nc = self.nc
        for e in ('pe', 'act', 'dve', 'pool'):
            self.sems[e] = es.enter_context(nc.semaphore('s_' + e))
        for qn in ('sp', 'act', 'pool'):
            for i in range(NDMA_SEM):
                self.sems[('dma', qn, i)] = es.enter_context(nc.semaphore(f'd_{qn}_{i}'))

    def flush(self):
        nc = self.nc
        q = self.q
        with nc.Block() as block:
            @block.tensor
            def _(e):
                for f in q['pe']:
                    f(e)

            @block.scalar
            def _(e):
                for f in q['act']:
                    f(e)

            @block.vector
            def _(e):
                for f in q['dve']:
                    f(e)

            @block.gpsimd
            def _(e):
                for f in q['pool']:
                    f(e)

            @block.sync
            def _(e):
                for f in q['sp']:
                    f(e)
        self.q = {e: [] for e in ENGS}


class Cfg:
    def __init__(self, D=4096, DFF=11008, NPRE=8, NMAIN=9, pre_blocks=(4, 4), main_blocks=(5, 4)):
        self.D = D
        self.KT = D // 128
        self.RH = D // 512
        self.GH = D // 1024
        self.RDK, self.RDV, self.GDK, self.GDV, self.RANK = 256, 512, 512, 1024, 16
        self.DFF = DFF
        self.FT = DFF // 128
        self.NPRE, self.NMAIN = NPRE, NMAIN
        self.NCH = NPRE + NMAIN
        self.NS = self.NCH * 128
        self.NREAL = (NMAIN - 1) * 128
        self.pre_blocks, self.main_blocks = pre_blocks, main_blocks
        assert sum(pre_blocks) == NPRE and sum(main_blocks) == NMAIN
        RQK, RV, GQK, GV = self.RH * 256, D, self.GH * 512, D
        self.c_qr = 0
        self.c_kr = RQK
        self.c_vr = 2 * RQK
        self.c_or = self.c_vr + RV
        self.c_qg = self.c_or + RV
        self.c_kg = self.c_qg + GQK
        self.c_vg = self.c_kg + GQK
        self.c_og = self.c_vg + GV
        self.c_zg = self.c_og + GV
        self.c_mr = self.c_zg + 16
        self.c_mg = self.c_mr + D
        self.WIN = self.c_mg + D
        self.GQK = GQK
        self.MAXB = max(max(pre_blocks), max(main_blocks)) * 128


def build_nc(cfg):
    D, KT, NS, DFF, FT = cfg.D, cfg.KT, cfg.NS, cfg.DFF, cfg.FT
    NMS = cfg.NMAIN * 128
    NREAL = cfg.NREAL
    nc = bass.Bass("TRN2", target_bir_lowering=False)

    def din(name, shape, dt=F32):
        return nc.dram_tensor(name, list(shape), dt, kind="ExternalInput").ap()

    def dscr(name, shape, dt):
        return nc.dram_tensor(name, list(shape), dt, kind="Internal").ap()
    xs = din("xs", [NS, D])
    posr = din("posr", [1, NS], I32)
    attn_norm = din("attn_norm", [1, D])
    w_in = din("w_in", [D, cfg.WIN])
    w_gate = din("w_gate", [17, cfg.GQK])
    ret_norm = din("ret_norm", [1, D])
    gla_norm = din("gla_norm", [1, D])
    w_out = din("w_out", [D, D])
    ffn_norm = din("ffn_norm", [1, D])
    w_ffn_in = din("w_ffn_in", [D, 2 * DFF])
    conv_w = din("conv_w", [3, DFF])
    conv_b = din("conv_b", [1, DFF])
    w_ffn_out = din("w_ffn_out", [DFF, D])
    final_norm = din("final_norm", [1, D])
    c_ident_bf = din("c_ident_bf", [128, 128], BF16)
    c_ident_f = din("c_ident_f", [128, 128])
    c_invf = din("c_invf", [128, 1])
    c_rdk = din("c_rdk", [cfg.RH, 256])
    c_rdq = din("c_rdq", [cfg.RH, 256])
    c_rmask = din("c_rmask", [cfg.RH, 128, 128])
    c_gmask = din("c_gmask", [128, 128])
    c_tri = din("c_tri", [128, 128])
    out = nc.dram_tensor("out", [NREAL, D], F32, kind="ExternalOutput").ap()

    hnT = dscr("hnT", [KT, 128, NS], BF16)
    MR = dscr("MR", [NMS, D], BF16)
    MG = dscr("MG", [NMS, D], BF16)
    H1 = dscr("H1", [NMS, D], F32)
    FO = dscr("FO", [NREAL, D], F32)
    ACTS = dscr("ACTS", [FT, 128, NREAL], BF16)
    cosD = dscr("cosD", [128, NS], F32)
    sinD = dscr("sinD", [128, NS], F32)

    P = Prog(nc)
    gam = [1.0 - 2.0 ** (-5.0 - h) for h in range(cfg.RH)]
    PRE0 = cfg.NPRE * 128

    with ExitStack() as top:
        P.alloc_sems(top)

        def bcast_row(ap_row, n):
            return ap_row.partition_broadcast(n)

        with ExitStack() as es:
            def sb(name, shape, dt):
                return es.enter_context(nc.sbuf_tensor(name, list(shape), dt))

            def psm(name, shape, dt):
                return es.enter_context(nc.psum_tensor(name, list(shape), dt))
            ident = sb("s0_ident", [128, 128], BF16)
            gB = sb("s0_gB", [128, D], F32)
            xt = [sb(f"s0_xt{i}", [128, D], F32) for i in range(2)]
            yb = [sb(f"s0_yb{i}", [128, D], BF16) for i in range(2)]
            junk = [sb(f"s0_junk{i}", [128, D], BF16) for i in range(2)]
            ss = sb("s0_ss", [128, 2 * cfg.NCH], F32)
            hT = [sb(f"s0_hT{i}", [128, KT, 128], BF16) for i in range(2)]
            pt = [psm(f"s0_pt{i}", [128, 8, 128], BF16) for i in range(4)]
            posi = sb("s0_posi", [128, NS], I32)
            sinT = sb("s0_sin", [128, NS], F32)
            cosT = sb("s0_cos", [128, NS], F32)
            invf = sb("s0_invf", [128, 1], F32)
            P.dma('sp', lambda e: e.dma_start(out=ident[:], in_=c_ident_bf), writes=['ident'])
            P.dma('sp', lambda e: e.dma_start(out=gB[:], in_=bcast_row(attn_norm[0:1, :], 128)), writes=['gB'])
            P.dma('sp', lambda e: e.dma_start(out=invf[:], in_=c_invf), writes=['invf'])
            P.dma('sp', lambda e: e.dma_start(out=posi[:], in_=bcast_row(posr[0:1, :], 128)), writes=['posi'])
            P.op('dve', lambda e: e.tensor_scalar(out=sinT[:], in0=posi[:], scalar1=16.0, scalar2=None, op0=ALU.add),
                 reads=['posi'], writes=['sinT'])
            P.op('dve', lambda e: e.tensor_scalar(out=sinT[:], in0=sinT[:], scalar1=invf[:, 0:1], scalar2=None, op0=ALU.mult),
                 reads=['sinT', 'invf'], writes=['sinT'])
            ki = sb("s0_ki", [128, NS], I32)
            kf = sb("s0_kf", [128, NS], F32)
            P.op('dve', lambda e: e.tensor_scalar(out=cosT[:], in0=sinT[:], scalar1=0.5 * math.pi, scalar2=None, op0=ALU.add),
                 reads=['sinT'], writes=['cosT'])
            for nm, tt_ in (('sinT', sinT), ('cosT', cosT)):
                P.op('dve', lambda e, tt_=tt_: e.tensor_scalar(out=kf[:], in0=tt_[:], scalar1=1.0 / (2 * math.pi), scalar2=None, op0=ALU.mult),
                     reads=[nm], writes=['kf'])
                P.op('dve', lambda e: e.tensor_copy(out=ki[:], in_=kf[:]), reads=['kf'], writes=['ki'])
                P.op('dve', lambda e: e.tensor_copy(out=kf[:], in_=ki[:]), reads=['ki'], writes=['kf'])
                P.op('dve', lambda e, tt_=tt_: e.scalar_tensor_tensor(out=tt_[:], in0=kf[:], scalar=-2 * math.pi, in1=tt_[:], op0=ALU.mult, op1=ALU.add),
                     reads=['kf', nm], writes=[nm])
                P.op('dve', lambda e, tt_=tt_: e.tensor_scalar(out=kf[:], in0=tt_[:], scalar1=math.pi, scalar2=2 * math.pi, op0=ALU.is_gt, op1=ALU.mult),
                     reads=[nm], writes=['kf'])
                P.op('dve', lambda e, tt_=tt_: e.tensor_tensor(out=tt_[:], in0=tt_[:], in1=kf[:], op=ALU.subtract),
                     reads=[nm, 'kf'], writes=[nm])
            P.op('act', lambda e: e.activation(out=sinT[:], in_=sinT[:], func=AF.Sin), reads=['sinT'], writes=['sinT'])
            P.op('act', lambda e: e.activation(out=cosT[:], in_=cosT[:], func=AF.Sin), reads=['cosT'], writes=['cosT'])
            P.dma('sp', lambda e: e.dma_start(out=sinD, in_=sinT[:]), reads=['sinT'], writes=['sinD'])
            P.dma('sp', lambda e: e.dma_start(out=cosD, in_=cosT[:]), reads=['cosT'], writes=['cosD'])
            P.op('dve', lambda e: e.memset(ss[:], 0.0), writes=[('ss', t_) for t_ in range(cfg.NCH)])
            epsT = sb("s0_eps", [128, 1], F32)
            P.op('dve', lambda e: e.memset(epsT[:], EPS), writes=['epsT'])
            for t in range(cfg.NCH):
                b = t % 2
                P.dma('sp', lambda e, t=t, b=b: e.dma_start(out=xt[b][:], in_=xs[t * 128:(t + 1) * 128, :]),
                      writes=[('xt', b)])
                P.op('act', lambda e, t=t, b=b: e.activation(out=junk[b][:], in_=xt[b][:], func=AF.Square,
                                                             accum_out=ss[:, 2 * t:2 * t + 1]),
                     reads=[('xt', b)], writes=[('junk', b), ('ss', t)])
                P.op('act', lambda e, t=t: e.activation(out=ss[:, 2 * t + 1:2 * t + 2], in_=ss[:, 2 * t:2 * t + 1], func=AF.Sqrt,
                                                        bias=epsT[:, 0:1], scale=1.0 / D), reads=[('ss', t), 'epsT'], writes=[('ss', t)])
                P.op('dve', lambda e, t=t: e.reciprocal(out=ss[:, 2 * t + 1:2 * t + 2], in_=ss[:, 2 * t + 1:2 * t + 2]),
                     reads=[('ss', t)], writes=[('ss', t)])
                P.op('dve', lambda e, t=t, b=b: e.scalar_tensor_tensor(out=yb[b][:], in0=xt[b][:],
                                                                       scalar=ss[:, 2 * t + 1:2 * t + 2], in1=gB[:],
                                                                       op0=ALU.mult, op1=ALU.mult),
                     reads=[('xt', b), ('ss', t), 'gB'], writes=[('yb', b)])
                for g4 in range(KT // 8):
                    pi = (t * (KT // 8) + g4) % 4
                    for j in range(8):
                        kt = g4 * 8 + j
                        P.op('pe', lambda e, kt=kt, j=j, pi=pi, b=b: e.transpose(pt[pi][:, j, :], yb[b][:, kt * 128:(kt + 1) * 128], ident[:]),
                             reads=[('yb', b), 'ident'], writes=[('pt', pi)], signal=(j == 7))
                    P.op('act', lambda e, g4=g4, pi=pi, b=b: e.activation(out=hT[b][:, g4 * 8:(g4 + 1) * 8, :], in_=pt[pi][:], func=AF.Copy),
                         reads=[('pt', pi)], writes=[('hT', b)])
                for g2 in range(0, KT, 8):
                    P.dma('act', lambda e, t=t, b=b, g2=g2: e.dma_start(
                        out=hnT[g2:g2 + 8, :, t * 128:(t + 1) * 128].rearrange("k p n -> p k n"),
                        in_=hT[b][:, g2:g2 + 8, :]), reads=[('hT', b)], writes=[('hnT', t, g2)])
            P.barrier()
            P.flush()

        with ExitStack() as es:
            def sb(name, shape, dt):
                return es.enter_context(nc.sbuf_tensor(name, list(shape), dt))

            def psm(name, shape, dt):
                return es.enter_context(nc.psum_tensor(name, list(shape), dt))
            MAXB = cfg.MAXB
            NTB = MAXB // 128
            ident = sb("p1_ident", [128, 128], BF16)
            cosT = sb("p1_cos", [128, MAXB], F32)
            sinT = sb("p1_sin", [128, MAXB], F32)
            yr = sb("p1_yr", [128, NTB, 1024], BF16)
            tg = sb("p1_tg", [128, 256], F32)
            NW = 3
            WCOLS = 256
            wb = [sb(f"p1_w{i}", [128, KT * WCOLS], BF16) for i in range(NW)]
            hnb = sb("p1_hnb", [128, KT, MAXB], BF16)
            qT = sb("p1_qT", [128, 4, MAXB], BF16)
            PREB = max(cfg.pre_blocks)
            kTs = [sb("p1_kT", [128, 4, MAXB], BF16), sb("p1_kT1", [128, 4, PREB * 128], BF16)]
            vtoks = [sb("p1_vtok", [128, NTB, 1024], BF16), sb("p1_vtok1", [128, PREB, 1024], BF16)]
            yn = sb("p1_yn", [128, NTB, 1024], BF16)
            S = sb("p1_S", [128, 4, 1024], F32)
            Sbf = sb("p1_Sbf", [128, 4, 1024], BF16)
            tabk = sb("p1_tabk", [128, 256], F32)
            tabq = sb("p1_tabq", [128, 256], F32)
            maskT = sb("p1_mask", [128, 128], F32)
            gmask = sb("p1_gmask", [128, 128], F32)
            tri = sb("p1_tri", [128, 128], F32)
            gnB = sb("p1_gnB", [128, 1024], F32)
            wg = sb("p1_wg", [17, 512], BF16)
            zTs = [sb("p1_zT", [17, MAXB], BF16), sb("p1_zT1", [17, PREB * 128], BF16)]
            t1 = sb("p1_t1", [128, 512], F32)
            t2 = sb("p1_t2", [128, 512], F32)
            t3 = t1
            t4 = t2
            khT = [sb(f"p1_khT{i}", [128, 4, 128], BF16) for i in range(2)]
            khat = [sb(f"p1_khat{i}", [128, 512], BF16) for i in range(2)]
            qd = [sb(f"p1_qd{i}", [128, 4, 128], BF16) for i in range(2)]
            kd = [sb(f"p1_kd{i}", [128, 4, 128], BF16) for i in range(2)]
            sT = [sb(f"p1_sT{i}", [128, 128], BF16) for i in range(2)]
            spf = [sb(f"p1_spf{i}", [128, 512], F32) for i in range(2)]
            eb = [sb(f"p1_eb{i}", [128, 4, 128], F32) for i in range(2)]
            enb = [sb(f"p1_enb{i}", [128, 4, 128], F32) for i in range(2)]
            st = sb("p1_st", [128, 8], F32)
            pA = [psm(f"p1_pA{i}", [128, 512], F32) for i in range(4)]
            pB = [psm(f"p1_pB{i}", [128, 512], F32) for i in range(2)]
            pC = psm("p1_pC", [128, 512], F32)
            pT = psm("p1_pT", [128, 4, 128], BF16)
            pa_ctr = [0]
            epsT = sb("p1_eps", [128, 1], F32)
            oneT = sb("p1_one", [128, 1], F32)
            P.op('dve', lambda e: e.memset(epsT[:], EPS), writes=['epsT'])
            P.op('dve', lambda e: e.memset(oneT[:], 1.0), writes=['oneT'])

            def next_pA():
                i = pa_ctr[0] % 4
                pa_ctr[0] += 1
                return i

            P.dma('sp', lambda e: e.dma_start(out=ident[:], in_=c_ident_bf), writes=['ident'])
            P.dma('sp', lambda e: e.dma_start(out=gmask[:], in_=c_gmask), writes=['gmask'])
            P.dma('sp', lambda e: e.dma_start(out=tri[:], in_=c_tri), writes=['tri'])

            wctr = [0]

            def wload(c0, ncols):
                s = wctr[0] % NW
                wctr[0] += 1
                view = wb[s][:, 0:KT * ncols].rearrange("p (k c) -> p k c", k=KT)
                step = 8
                for k0 in range(0, KT, step):
                    P.dma('pool', lambda e, k0=k0, view=view: e.dma_start(
                        out=view[:, k0:k0 + step, :],
                        in_=w_in[k0 * 128:(k0 + step) * 128, c0:c0 + ncols].rearrange("(k p) c -> p k c", p=128)),
                        writes=[('w', s, k0 // step)])
                return view, (lambda kt, s=s: ('w', s, kt // 8))

            def blocks():
                s0 = 0
                for nb in cfg.pre_blocks:
                    yield (False, s0, nb)
                    s0 += nb * 128
                for nb in cfg.main_blocks:
                    yield (True, s0, nb)
                    s0 += nb * 128

            def pieces(nb):
                res, c = [], 0
                while c < nb:
                    n = min(4, nb - c)
                    if nb - c > 4 and nb - c < 8:
                        n = (nb - c + 1) // 2
                    res.append((c * 128, n * 128))
                    c += n
                return res

            def load_hn(s0, ntok):
                for k0 in range(0, KT, 8):
                    P.dma('act', lambda e, k0=k0: e.dma_start(
                        out=hnb[:, k0:k0 + 8, 0:ntok],
                        in_=hnT[k0:k0 + 8, :, s0:s0 + ntok].rearrange("k p n -> p k n")),
                        writes=[('hnb', k0 // 8)])

            def fproj(c0, ncols, ntok, evac):
                for cc in range(0, ncols, WCOLS):
                    nc_ = min(WCOLS, ncols - cc)
                    view, wres = wload(c0 + cc, nc_)
                    for j in range((nc_ + 127) // 128):
                        m = min(128, nc_ - j * 128)
                        for (po, pn) in pieces(ntok // 128):
                            pi = next_pA()
                            for kt in range(KT):
                                P.op('pe', lambda e, pi=pi, view=view, j=j, m=m, kt=kt, po=po, pn=pn: e.matmul(
                                    pA[pi][0:m, 0:pn], lhsT=view[:, kt, j * 128:j * 128 + m], rhs=hnb[:, kt, po:po + pn],
                                    start=(kt == 0), stop=(kt == KT - 1)),
                                    reads=[wres(kt), ('hnb', kt // 8)], writes=[('pA', pi)], signal=(kt == KT - 1))
                                if kt == KT // 2 - 1:
                                    yield
                            evac(cc // 128 + j, po, pn, pi)
                            yield

            def tproj(c0, ncols, ntok, evac):
                for cc in range(0, ncols, WCOLS):
                    nv = min(WCOLS, ncols - cc)
                    view, wres = wload(c0 + cc, nv)
                    for tt in range(ntok // 128):
                        pi = next_pA()
                        for kt in range(KT):
                            P.op('pe', lambda e, pi=pi, view=view, nv=nv, kt=kt, tt=tt: e.matmul(
                                pA[pi][:, 0:nv], lhsT=hnb[:, kt, tt * 128:(tt + 1) * 128], rhs=view[:, kt, :],
                                start=(kt == 0), stop=(kt == KT - 1)),
                                reads=[wres(kt), ('hnb', kt // 8)], writes=[('pA', pi)], signal=(kt == KT - 1))
                            if kt == KT // 2 - 1:
                                yield
                        evac(tt, cc, nv, pi)
                        yield

            def drain(gen):
                for _ in gen:
                    pass

            def interleave(ga, gb):
                a_done = b_done = False
                while not (a_done and b_done):
                    if not a_done:
                        try:
                            next(ga)
                        except StopIteration:
                            a_done = True
                    if not b_done:
                        try:
                            next(gb)
                        except StopIteration:
                            b_done = True

            def gates(u, ntok, dv, c_o, c_m):
                ch0 = u * dv

                def ev_o(tt, co, ncp, pi):
                    P.op('act', lambda e: e.activation(out=tg[:, 0:ncp], in_=pA[pi][:, 0:ncp], func=AF.Exp, scale=-1.0),
                         reads=[('pA', pi)], writes=['tg'])
                    P.op('act', lambda e: e.activation(out=tg[:, 0:ncp], in_=tg[:, 0:ncp], func=AF.Ln, bias=oneT[:, 0:1]), reads=['tg', 'oneT'], writes=['tg'])
                    P.op('act', lambda e: e.activation(out=tg[:, 0:ncp], in_=tg[:, 0:ncp], func=AF.Exp, scale=-1.0), reads=['tg'], writes=['tg'])
                    P.op('dve', lambda e: e.tensor_tensor(out=tg[:, 0:ncp], in0=pA[pi][:, 0:ncp], in1=tg[:, 0:ncp], op=ALU.mult),
                         reads=[('pA', pi), 'tg'], writes=['tg'])
                    P.op('dve', lambda e: e.tensor_tensor(out=yn[:, tt, co:co + ncp], in0=tg[:, 0:ncp], in1=gnB[:, co:co + ncp], op=ALU.mult),
                         reads=['tg', 'gnB'], writes=[('yn', tt)])
                yield from tproj(c_o + ch0, dv, ntok, ev_o)

                def ev_m(tt, co, ncp, pi):
                    P.op('act', lambda e: e.activation(out=tg[:, 0:ncp], in_=pA[pi][:, 0:ncp], func=AF.Exp, scale=-1.0),
                         reads=[('pA', pi)], writes=['tg'])
                    P.op('act', lambda e: e.activation(out=tg[:, 0:ncp], in_=tg[:, 0:ncp], func=AF.Ln, bias=oneT[:, 0:1]), reads=['tg', 'oneT'], writes=['tg'])
                    P.op('act', lambda e: e.activation(out=tg[:, 0:ncp], in_=tg[:, 0:ncp], func=AF.Exp, scale=-1.0), reads=['tg'], writes=['tg'])
                    P.op('dve', lambda e: e.tensor_tensor(out=yn[:, tt, co:co + ncp], in0=yn[:, tt, co:co + ncp], in1=tg[:, 0:ncp], op=ALU.mult),
                         reads=[('yn', tt), 'tg'], writes=[('yn', tt)])
                yield from tproj(c_m + ch0, dv, ntok, ev_m)

            def store_y(is_ret, u, s0, c, dv, dst):
                m0 = s0 - PRE0
                ch0 = u * dv
                P.op('dve', lambda e: e.tensor_tensor(out=yn[:, c, 0:dv], in0=yr[:, c, 0:dv], in1=yn[:, c, 0:dv], op=ALU.mult),
                     reads=[('yr', c), ('yn', c)], writes=[('yn', c)])
                P.dma('sp', lambda e: e.dma_start(out=dst[m0 + c * 128:m0 + (c + 1) * 128, ch0:ch0 + dv], in_=yn[:, c, 0:dv]),
                      reads=[('yn', c)], writes=[('dst', is_ret, u, s0, c)])

            def copy_evac(dstT, nm, scale=1.0):
                def ev(j, po, pn, pi):
                    P.op('act', lambda e: e.activation(out=dstT[:, j, po:po + pn], in_=pA[pi][:, 0:pn], func=AF.Copy, scale=scale),
                         reads=[('pA', pi)], writes=[nm])
                return ev

            def v_evac_to(vt, bs):
                def v_evac(tt, co, ncp, pi):
                    P.op('act', lambda e: e.activation(out=vt[:, tt, co:co + ncp], in_=pA[pi][:, 0:ncp], func=AF.Copy),
                         reads=[('pA', pi)], writes=[('vtok', bs)])
                return v_evac

            def skewed(nb, pre, post):
                yield from pre(0)
                for c in range(nb):
                    if c + 1 < nb:
                        yield from pre(c + 1)
                    yield from post(c)

            blks = list(blocks())

            def bufset(bi):
                is_main, _, _ = blks[bi]
                return 0 if is_main else bi % 2

            def run_unit(gemm, scan, gates_fn, finalize):
                drain(gemm(0))
                for bi in range(len(blks)):
                    is_main = blks[bi][0]
                    nxt = bi + 1 < len(blks)
                    if not is_main:
                        if nxt:
                            interleave(scan(bi), gemm(bi + 1))
                        else:
                            drain(scan(bi))
                    else:
                        interleave(scan(bi), gates_fn(bi))
                        finalize(bi)
                        if nxt:
                            drain(gemm(bi + 1))

            def ret_unit(h):
                P.dma('sp', lambda e: e.dma_start(out=tabk[:], in_=bcast_row(c_rdk[h:h + 1, :], 128)), writes=['tabk'])
                P.dma('sp', lambda e: e.dma_start(out=tabq[:], in_=bcast_row(c_rdq[h:h + 1, :], 128)), writes=['tabq'])
                P.dma('sp', lambda e: e.dma_start(out=maskT[:], in_=c_rmask[h]), writes=['maskT'])
                P.dma('sp', lambda e: e.dma_start(out=gnB[:, 0:512], in_=bcast_row(ret_norm[0:1, h * 512:(h + 1) * 512], 128)), writes=['gnB'])
                P.op('dve', lambda e: e.memset(S[:, 0:2, 0:512], 0.0), writes=['S'])
                P.op('dve', lambda e: e.memset(Sbf[:, 0:2, 0:512], 0.0), writes=['Sbf'])
                gC = gam[h] ** 128

                def rope_evac(dstT, nm):
                    hold = {}

                    def ev(j, po, pn, pi):
                        hold[(j, po)] = pi
                        if j == 0:
                            return
                        p1, p2 = hold[(0, po)], pi
                        cs, sn = cosT[:, po:po + pn], sinT[:, po:po + pn]
                        P.op('dve', lambda e: e.tensor_tensor(out=t1[:, 0:pn], in0=pA[p1][:, 0:pn], in1=cs, op=ALU.mult),
                             reads=[('pA', p1), 'cosT'], writes=['t1'])
                        P.op('dve', lambda e: e.tensor_tensor(out=t2[:, 0:pn], in0=pA[p2][:, 0:pn], in1=sn, op=ALU.mult),
                             reads=[('pA', p2), 'sinT'], writes=['t2'])
                        P.op('dve', lambda e: e.tensor_tensor(out=dstT[:, 0, po:po + pn], in0=t1[:, 0:pn], in1=t2[:, 0:pn], op=ALU.subtract),
                             reads=['t1', 't2'], writes=[nm])
                        P.op('dve', lambda e: e.tensor_tensor(out=t1[:, 0:pn], in0=pA[p2][:, 0:pn], in1=cs, op=ALU.mult),
                             reads=[('pA', p2), 'cosT'], writes=['t1'])
                        P.op('dve', lambda e: e.tensor_tensor(out=t2[:, 0:pn], in0=pA[p1][:, 0:pn], in1=sn, op=ALU.mult),
                             reads=[('pA', p1), 'sinT'], writes=['t2'])
                        P.op('dve', lambda e: e.tensor_tensor(out=dstT[:, 1, po:po + pn], in0=t1[:, 0:pn], in1=t2[:, 0:pn], op=ALU.add),
                             reads=['t1', 't2'], writes=[nm])
                    return ev

                def gemm(bi):
                    is_main, s0, nb = blks[bi]
                    bs = bufset(bi)
                    ntok = nb * 128
                    load_hn(s0, ntok)
                    P.dma('act', lambda e: e.dma_start(out=sinT[:, 0:ntok], in_=sinD[:, s0:s0 + ntok]), writes=['sinT'])
                    P.dma('act', lambda e: e.dma_start(out=cosT[:, 0:ntok], in_=cosD[:, s0:s0 + ntok]), writes=['cosT'])
                    if is_main:
                        yield from fproj(cfg.c_qr + h * 256, 256, ntok, rope_evac(qT, 'qT'))
                    yield from fproj(cfg.c_kr + h * 256, 256, ntok, rope_evac(kTs[bs], ('kT', bs)))
                    yield from tproj(cfg.c_vr + h * 512, 512, ntok, v_evac_to(vtoks[bs], bs))

                def scan(bi):
                    is_main, s0, nb = blks[bi]
                    bs = bufset(bi)
                    kT, vtok = kTs[bs], vtoks[bs]

                    def pre(c):
                        o = c * 128
                        b = c % 2
                        P.op('dve', lambda e: e.tensor_tensor(out=khT[b][:, 0:2, :], in0=kT[:, 0:2, o:o + 128],
                                                              in1=tabk[:].rearrange("p (j n) -> p j n", j=2), op=ALU.mult),
                             reads=[('kT', bs), 'tabk'], writes=[('khT', b)])
                        for j in range(2):
                            P.op('pe', lambda e, j=j: e.transpose(pT[:, j, :], khT[b][:, j, :], ident[:]),
                                 reads=[('khT', b), 'ident'], writes=['pT'], signal=(j == 1))
                        P.op('act', lambda e: e.activation(out=khat[b][:, 0:256], in_=pT[:, 0:2, :].rearrange("p j n -> p (j n)"), func=AF.Copy),
                             reads=['pT'], writes=[('khat', b)])
                        yield
                        if is_main:
                            for j in range(2):
                                P.op('pe', lambda e, j=j: e.matmul(pC[:, 0:128], lhsT=kT[:, j, o:o + 128], rhs=qT[:, j, o:o + 128],
                                                                   start=(j == 0), stop=(j == 1)),
                                     reads=[('kT', bs), 'qT'], writes=['pC'], signal=(j == 1))
                            P.op('dve', lambda e: e.tensor_tensor(out=sT[b][:], in0=pC[:, 0:128], in1=maskT[:], op=ALU.mult),
                                 reads=['pC', 'maskT'], writes=[('sT', b)])
                            P.op('dve', lambda e: e.tensor_tensor(out=qd[b][:, 0:2, :], in0=qT[:, 0:2, o:o + 128],
                                                                  in1=tabq[:].rearrange("p (j n) -> p j n", j=2), op=ALU.mult),
                                 reads=['qT', 'tabq'], writes=[('qd', b)])
                            yield

                    def post(c):
                        b = c % 2
                        if is_main:
                            P.op('pe', lambda e: e.matmul(pB[0][:, :], lhsT=sT[b][:], rhs=vtok[:, c, 0:512], start=True, stop=False),
                                 reads=[('sT', b), ('vtok', bs)], writes=[('pB', 0)], signal=False)
                            for j in range(2):
                                P.op('pe', lambda e, j=j: e.matmul(pB[0][:, :], lhsT=qd[b][:, j, :], rhs=Sbf[:, j, 0:512], start=False, stop=(j == 1)),
                                     reads=[('qd', b), 'Sbf'], writes=[('pB', 0)], signal=(j == 1))
                            P.op('dve', lambda e: e.bn_stats(out=st[:, 0:6], in_=pB[0][:, :]), reads=[('pB', 0)], writes=['st'])
                            P.op('dve', lambda e: e.bn_aggr(out=st[:, 6:8], in_=st[:, 0:6]), reads=['st'], writes=['st'])
                            P.op('act', lambda e: e.activation(out=st[:, 7:8], in_=st[:, 7:8], func=AF.Ln, bias=epsT[:, 0:1], scale=1.0),
                                 reads=['st', 'epsT'], writes=['st'])
                            P.op('act', lambda e: e.activation(out=st[:, 7:8], in_=st[:, 7:8], func=AF.Exp, scale=-0.5), reads=['st'], writes=['st'])
                            P.op('dve', lambda e: e.tensor_scalar(out=yr[:, c, 0:512], in0=pB[0][:, :], scalar1=st[:, 6:7], scalar2=st[:, 7:8],
                                                                  op0=ALU.subtract, op1=ALU.mult),
                                 reads=[('pB', 0), 'st'], writes=[('yr', c)])
                            yield
                        for j in range(2):
                            P.op('pe', lambda e, j=j: e.matmul(pB[1][:, :], lhsT=khat[b][:, j * 128:(j + 1) * 128], rhs=vtok[:, c, 0:512], start=True, stop=True),
                                 reads=[('khat', b), ('vtok', bs)], writes=[('pB', 1)])
                            P.op('dve', lambda e, j=j: e.scalar_tensor_tensor(out=S[:, j, 0:512], in0=S[:, j, 0:512], scalar=gC, in1=pB[1][:, :],
                                                                              op0=ALU.mult, op1=ALU.add),
                                 reads=['S', ('pB', 1)], writes=['S'])
                            P.op('act', lambda e, j=j: e.activation(out=Sbf[:, j, 0:512], in_=S[:, j, 0:512], func=AF.Copy),
                                 reads=['S'], writes=['Sbf'])
                            yield
                    yield from skewed(nb, pre, post)

                def gates_fn(bi):
                    _, s0, nb = blks[bi]
                    yield from gates(h, nb * 128, 512, cfg.c_or, cfg.c_mr)

                def finalize(bi):
                    _, s0, nb = blks[bi]
                    for c in range(nb):
                        store_y(True, h, s0, c, 512, MR)
                run_unit(gemm, scan, gates_fn, finalize)

            def gla_unit(g):
                P.dma('sp', lambda e: e.dma_start(out=gnB[:, 0:1024], in_=bcast_row(gla_norm[0:1, g * 1024:(g + 1) * 1024], 128)), writes=['gnB'])
                P.dma('pool', lambda e: e.dma_start(out=wg[:], in_=w_gate[:, g * 512:(g + 1) * 512]), writes=['wg'])
                P.op('dve', lambda e: e.memset(S[:], 0.0), writes=['S'])
                P.op('dve', lambda e: e.memset(Sbf[:], 0.0), writes=['Sbf'])
                for i_ in range(2):
                    P.op('dve', lambda e, i_=i_: e.memset(zTs[i_][:], 1.0), writes=[('zT', i_)])

                def gemm(bi):
                    is_main, s0, nb = blks[bi]
                    bs = bufset(bi)
                    ntok = nb * 128
                    load_hn(s0, ntok)
                    if is_main:
                        yield from fproj(cfg.c_qg + g * 512, 512, ntok, copy_evac(qT, 'qT', scale=512 ** -0.5))
                    yield from fproj(cfg.c_kg + g * 512, 512, ntok, copy_evac(kTs[bs], ('kT', bs)))

                    def z_evac(j, po, pn, pi):
                        P.op('act', lambda e: e.activation(out=zTs[bs][0:16, po:po + pn], in_=pA[pi][0:16, 0:pn], func=AF.Copy),
                             reads=[('pA', pi)], writes=[('zT', bs)])
                    yield from fproj(cfg.c_zg, 16, ntok, z_evac)
                    yield from tproj(cfg.c_vg + g * 1024, 1024, ntok, v_evac_to(vtoks[bs], bs))

                def scan(bi):
                    is_main, s0, nb = blks[bi]
                    bs = bufset(bi)
                    kT, vtok, zT = kTs[bs], vtoks[bs], zTs[bs]

                    def pre(c):
                        o = c * 128
                        b = c % 2
                        P.op('pe', lambda e: e.matmul(pC[:, :], lhsT=zT[:, o:o + 128], rhs=wg[:, :], start=True, stop=True),
                             reads=[('zT', bs), 'wg'], writes=['pC'])
                        yield
                        P.op('act', lambda e: e.activation(out=spf[b][:], in_=pC[:, :], func=AF.Exp, scale=-1.0), reads=['pC'], writes=[('spf', b)])
                        P.op('act', lambda e: e.activation(out=spf[b][:], in_=spf[b][:], func=AF.Ln, bias=oneT[:, 0:1]), reads=[('spf', b), 'oneT'], writes=[('spf', b)])
                        for j in range(4):
                            P.op('pe', lambda e, j=j: e.matmul(pC[:, j * 128:(j + 1) * 128], lhsT=spf[b][:, j * 128:(j + 1) * 128], rhs=tri[:, :], start=True, stop=True),
                                 reads=[('spf', b), 'tri'], writes=['pC'], signal=(j == 3))
                        yield
                        P.op('act', lambda e: e.activation(out=eb[b][:].rearrange("p j n -> p (j n)"), in_=pC[:, :], func=AF.Exp), reads=['pC'], writes=[('eb', b)])
                        P.op('act', lambda e: e.activation(out=enb[b][:].rearrange("p j n -> p (j n)"), in_=pC[:, :], func=AF.Exp, scale=-1.0), reads=['pC'], writes=[('enb', b)])
                        P.op('dve', lambda e: e.tensor_tensor(out=kd[b][:], in0=kT[:, :, o:o + 128], in1=enb[b][:], op=ALU.mult),
                             reads=[('kT', bs), ('enb', b)], writes=[('kd', b)])
                        for j in range(4):
                            P.op('dve', lambda e, j=j: e.tensor_scalar(out=khT[b][:, j, :], in0=kd[b][:, j, :], scalar1=eb[b][:, j, 127:128], scalar2=None, op0=ALU.mult),
                                 reads=[('kd', b), ('eb', b)], writes=[('khT', b)])
                        for j in range(4):
                            P.op('pe', lambda e, j=j: e.transpose(pT[:, j, :], khT[b][:, j, :], ident[:]),
                                 reads=[('khT', b), 'ident'], writes=['pT'], signal=(j == 3))
                        P.op('act', lambda e: e.activation(out=khat[b][:], in_=pT[:].rearrange("p j n -> p (j n)"), func=AF.Copy),
                             reads=['pT'], writes=[('khat', b)])
                        yield
                        if is_main:
                            P.op('dve', lambda e: e.tensor_tensor(out=qd[b][:], in0=qT[:, :, o:o + 128], in1=eb[b][:], op=ALU.mult),
                                 reads=['qT', ('eb', b)], writes=[('qd', b)])
                            for j in range(4):
                                P.op('pe', lambda e, j=j: e.matmul(pC[:, 0:128], lhsT=kd[b][:, j, :], rhs=qd[b][:, j, :], start=(j == 0), stop=(j == 3)),
                                     reads=[('kd', b), ('qd', b)], writes=['pC'], signal=(j == 3))
                            P.op('dve', lambda e: e.tensor_tensor(out=sT[b][:], in0=pC[:, 0:128], in1=gmask[:], op=ALU.mult),
                                 reads=['pC', 'gmask'], writes=[('sT', b)])
                            yield

                    def post(c):
                        b = c % 2
                        if is_main:
                            P.op('dve', lambda e: e.memset(st[:, 0:2], 0.0), writes=['st'])
                            for vh in range(2):
                                P.op('pe', lambda e, vh=vh: e.matmul(pB[vh][:, :], lhsT=sT[b][:], rhs=vtok[:, c, vh * 512:(vh + 1) * 512], start=True, stop=False),
                                     reads=[('sT', b), ('vtok', bs)], writes=[('pB', vh)], signal=False)
                                for j in range(4):
                                    P.op('pe', lambda e, j=j, vh=vh: e.matmul(pB[vh][:, :], lhsT=qd[b][:, j, :], rhs=Sbf[:, j, vh * 512:(vh + 1) * 512], start=False, stop=(j == 3)),
                                         reads=[('qd', b), 'Sbf'], writes=[('pB', vh)], signal=(j == 3))
                                P.op('act', lambda e, vh=vh: e.activation(out=t1[:], in_=pB[vh][:, :], func=AF.Square, accum_out=st[:, vh:vh + 1]),
                                     reads=[('pB', vh)], writes=['t1', 'st'])
                            P.op('dve', lambda e: e.tensor_tensor(out=st[:, 2:3], in0=st[:, 0:1], in1=st[:, 1:2], op=ALU.add), reads=['st'], writes=['st'])
                            P.op('act', lambda e: e.activation(out=st[:, 2:3], in_=st[:, 2:3], func=AF.Ln, bias=epsT[:, 0:1], scale=1.0 / 1024),
                                 reads=['st', 'epsT'], writes=['st'])
                            P.op('act', lambda e: e.activation(out=st[:, 2:3], in_=st[:, 2:3], func=AF.Exp, scale=-0.5), reads=['st'], writes=['st'])
                            for vh in range(2):
                                P.op('act', lambda e, vh=vh: e.activation(out=yr[:, c, vh * 512:(vh + 1) * 512], in_=pB[vh][:, :], func=AF.Copy, scale=st[:, 2:3]),
                                     reads=[('pB', vh), 'st'], writes=[('yr', c)])
                            yield
                        for j in range(4):
                            for vh in range(2):
                                P.op('pe', lambda e, j=j, vh=vh: e.matmul(pB[vh][:, :], lhsT=khat[b][:, j * 128:(j + 1) * 128], rhs=vtok[:, c, vh * 512:(vh + 1) * 512], start=True, stop=True),
                                     reads=[('khat', b), ('vtok', bs)], writes=[('pB', vh)])
                                P.op('dve', lambda e, j=j, vh=vh: e.scalar_tensor_tensor(out=S[:, j, vh * 512:(vh + 1) * 512], in0=S[:, j, vh * 512:(vh + 1) * 512],
                                                                                          scalar=eb[b][:, j, 127:128], in1=pB[vh][:, :], op0=ALU.mult, op1=ALU.add),
                                     reads=['S', ('pB', vh), ('eb', b)], writes=['S'])
                            P.op('act', lambda e, j=j: e.activation(out=Sbf[:, j, :], in_=S[:, j, :], func=AF.Copy), reads=['S'], writes=['Sbf'])
                            yield
                    yield from skewed(nb, pre, post)

                def gates_fn(bi):
                    _, s0, nb = blks[bi]
                    yield from gates(g, nb * 128, 1024, cfg.c_og, cfg.c_mg)

                def finalize(bi):
                    _, s0, nb = blks[bi]
                    for c in range(nb):
                        store_y(False, g, s0, c, 1024, MG)
                run_unit(gemm, scan, gates_fn, finalize)

            for g in range(cfg.GH):
                gla_unit(g)
            for h in range(cfg.RH):
                ret_unit(h)
            P.barrier()
            P.flush()

        NT = NREAL // 128
        with ExitStack() as es:
            def sb(name, shape, dt):
                return es.enter_context(nc.sbuf_tensor(name, list(shape), dt))

            def psm(name, shape, dt):
                return es.enter_context(nc.psum_tensor(name, list(shape), dt))
            ident = sb("p2_ident", [128, 128], BF16)
            aT = sb("p2_aT", [128, KT, NREAL], BF16)
            aTh = sb("p2_aTh", [128, KT, 2], BF16)
            m1 = [sb(f"p2_m1{i}", [128, D], BF16) for i in range(2)]
            m2s = sb("p2_m2", [128, D], BF16)
            m2 = [m2s, m2s]
            OC = 256
            warena = sb("p2_warena", [128, KT * OC * 3], BF16)
            xp = [sb(f"p2_xp{i}", [128, OC], F32) for i in range(2)]
            gB = sb("p2_gB", [128, D], F32)
            hrow = sb("p2_hrow", [128, D], F32)
            ss = sb("p2_ss", [128, 4], F32)
            cw = sb("p2_cw", [128, 4, FT], F32)
            upb = sb("p2_upb", [128, NREAL + 2], F32)
            cc = sb("p2_cc", [128, NREAL], F32)
            sg = sb("p2_sg", [128, NREAL], F32)
            ab = [sb(f"p2_ab{i}", [128, NREAL], BF16) for i in range(2)]
            pA = [psm(f"p2_pA{i}", [128, 512], F32) for i in range(6)]
            pH = psm("p2_pH", [128, 512], F32)
            pT = psm("p2_pT", [128, 8, 128], BF16)
            epsT = sb("p2_eps", [128, 1], F32)
            P.op('dve', lambda e: e.memset(epsT[:], EPS), writes=['epsT'])
            P.dma('sp', lambda e: e.dma_start(out=ident[:], in_=c_ident_bf), writes=['ident'])
            P.dma('sp', lambda e: e.dma_start(out=gB[:], in_=bcast_row(ffn_norm[0:1, :], 128)), writes=['gB'])
            for i in range(3):
                P.dma('sp', lambda e, i=i: e.dma_start(out=cw[:, i, :], in_=conv_w[i:i + 1, :].rearrange("o (f p) -> p (o f)", p=128), allow_slow_non_contiguous=True), writes=['cw'])
            P.dma('sp', lambda e: e.dma_start(out=cw[:, 3, :], in_=conv_b[0:1, :].rearrange("o (f p) -> p (o f)", p=128), allow_slow_non_contiguous=True), writes=['cw'])

            def transposes_to(src_bf, srcres, dst_fn, rows=128, keep=None):
                for g4 in range(KT // 8):
                    for j in range(8):
                        kt = g4 * 8 + j
                        P.op('pe', lambda e, kt=kt, j=j: e.transpose(pT[:, j, 0:rows], src_bf[0:rows, kt * 128:(kt + 1) * 128], ident[0:rows, 0:rows]),
                             reads=[srcres, 'ident'], writes=['pT'], signal=(j == 7))
                    dst, dres = dst_fn(g4 * 8)
                    k0_, k1_ = keep if keep else (0, rows)
                    P.op('act', lambda e, dst=dst, k0_=k0_, k1_=k1_: e.activation(out=dst, in_=pT[:, :, k0_:k1_], func=AF.Copy),
                         reads=['pT'], writes=[dres])

            def merged_pass(t):
                b = t % 2
                P.dma('sp', lambda e: e.dma_start(out=m1[b][:], in_=MR[t * 128:(t + 1) * 128, :]), writes=[('m1', b)])
                P.dma('sp', lambda e: e.dma_start(out=m2[b][:], in_=MG[t * 128:(t + 1) * 128, :]), writes=['m2'])
                P.op('dve', lambda e: e.tensor_tensor(out=m1[b][:], in0=m1[b][:], in1=m2[b][:], op=ALU.add),
                     reads=[('m1', b), 'm2'], writes=[('m1', b)])
                if t == 0:
                    transposes_to(m1[b], ('m1', b), lambda k0: (aTh[:, k0:k0 + 8, :], 'aTh'), rows=128, keep=(126, 128))
                else:
                    transposes_to(m1[b], ('m1', b), lambda k0: (aT[:, k0:k0 + 8, (t - 1) * 128:t * 128], ('aT', t)))

            def norm_pass(t):
                b = t % 2
                rows, r0 = (2, 126) if t == 0 else (128, t * 128)
                P.dma('sp', lambda e: e.dma_start(out=hrow[0:rows, :], in_=H1[r0:r0 + rows, :]), reads=[('H1', t, cg_) for cg_ in range(D // OC)], writes=['hrow'])
                P.op('dve', lambda e: e.memset(ss[:, 0:1], 0.0), writes=['ss'])
                P.op('act', lambda e: e.activation(out=m2[b][0:rows, :], in_=hrow[0:rows, :], func=AF.Square, accum_out=ss[0:rows, 0:1]),
                     reads=['hrow'], writes=['m2', 'ss'])
                P.op('act', lambda e: e.activation(out=ss[0:rows, 1:2], in_=ss[0:rows, 0:1], func=AF.Sqrt, bias=epsT[0:rows, 0:1], scale=1.0 / D),
                     reads=['ss', 'epsT'], writes=['ss'])
                P.op('dve', lambda e: e.reciprocal(out=ss[0:rows, 1:2], in_=ss[0:rows, 1:2]), reads=['ss'], writes=['ss'])
                P.op('dve', lambda e: e.scalar_tensor_tensor(out=m1[b][0:rows, :], in0=hrow[0:rows, :], scalar=ss[0:rows, 1:2], in1=gB[0:rows, :],
                                                             op0=ALU.mult, op1=ALU.mult),
                     reads=['hrow', 'ss', 'gB'], writes=[('m1', b)])
                if t == 0:
                    transposes_to(m1[b], ('m1', b), lambda k0: (aTh[:, k0:k0 + 8, :], 'aTh'), rows=2, keep=(0, 2))
                else:
                    transposes_to(m1[b], ('m1', b), lambda k0: (aT[:, k0:k0 + 8, (t - 1) * 128:t * 128], ('aT', t)))

            w2ctr = [0]

            def wload2(src, r0, nrows_t, c0, ncols, nslots, tag):
                s = w2ctr[0] % nslots
                w2ctr[0] += 1
                sz = nrows_t * ncols
                view = warena[:, s * sz:(s + 1) * sz].rearrange("p (k c) -> p k c", k=nrows_t)
                step = 8
                for k0 in range(0, nrows_t, step):
                    k1 = min(nrows_t, k0 + step)
                    P.dma('pool', lambda e, k0=k0, k1=k1, view=view: e.dma_start(
                        out=view[:, k0:k1, :],
                        in_=src[r0 + k0 * 128:r0 + k1 * 128, c0:c0 + ncols].rearrange("(k p) c -> p k c", p=128)),
                        writes=[(tag, s, k0 // step)])
                return view, (lambda kt, s=s, tag=tag: (tag, s, kt // 8))

            pa2 = [0]
            NCG = D // OC
            merged_pass(0)
            merged_pass(1)
            for cg in range(NCG):
                view, wres = wload2(w_out, 0, KT, cg * OC, OC, 3, 'wo')
                for t in range(0, NT + 1):
                    pi = pa2[0] % 6
                    pa2[0] += 1
                    b = pa2[0] % 2
                    if t == 0:
                        lhs = lambda kt: aTh[:, kt, :]
                        rows, r0, lres = 2, 126, 'aTh'
                    else:
                        lhs = lambda kt, t=t: aT[:, kt, (t - 1) * 128:t * 128]
                        rows, r0, lres = 128, t * 128, ('aT', t)
                    for kt in range(KT):
                        P.op('pe', lambda e, kt=kt, pi=pi, lhs=lhs, rows=rows, view=view: e.matmul(pA[pi][0:rows, 0:OC], lhsT=lhs(kt), rhs=view[:, kt, :],
                                                                                                  start=(kt == 0), stop=(kt == KT - 1)),
                             reads=[wres(kt), lres], writes=[('pA', pi)], signal=(kt == KT - 1))
                    P.dma('sp', lambda e, b=b, rows=rows, r0=r0, cg=cg: e.dma_start(out=xp[b][0:rows, :], in_=xs[PRE0 + r0:PRE0 + r0 + rows, cg * OC:(cg + 1) * OC]),
                          writes=[('xp', b)])
                    P.op('dve', lambda e, b=b, rows=rows, pi=pi: e.tensor_tensor(out=xp[b][0:rows, :], in0=pA[pi][0:rows, 0:OC], in1=xp[b][0:rows, :], op=ALU.add),
                         reads=[('pA', pi), ('xp', b)], writes=[('xp', b)])
                    P.dma('act', lambda e, b=b, rows=rows, r0=r0, cg=cg: e.dma_start(out=H1[r0:r0 + rows, cg * OC:(cg + 1) * OC], in_=xp[b][0:rows, :]),
                          reads=[('xp', b)], writes=[('H1', t, cg)])
                    if cg == 0 and t + 2 <= NT:
                        merged_pass(t + 2)
                    if cg == NCG - 1 and t >= 2:
                        norm_pass(t - 2)
            for t in range(max(0, NT - 1), NT + 1):
                norm_pass(t)

            P.barrier()
            NP = [(o, min(512, NREAL - o)) for o in range(0, NREAL, 512)]
            assert len(NP) <= 2
            for m in range(FT):
                view, wres = wload2(w_ffn_in, 0, KT, m * 128, 128, 6, 'wf')
                viewg, wresg = wload2(w_ffn_in, 0, KT, DFF + m * 128, 128, 6, 'wf')
                for kt in range(KT):
                    P.op('pe', lambda e, kt=kt, view=view: e.matmul(pH[:, 0:2], lhsT=view[:, kt, :], rhs=aTh[:, kt, :], start=(kt == 0), stop=(kt == KT - 1)),
                         reads=[wres(kt), 'aTh'], writes=['pH'], signal=(kt == KT - 1))
                P.op('act', lambda e: e.activation(out=upb[:, 0:2], in_=pH[:, 0:2], func=AF.Copy), reads=['pH'], writes=['upb'])
                ups, gts = [], []
                for (po, pn) in NP:
                    pi = pa2[0] % 6
                    pa2[0] += 1
                    for kt in range(KT):
                        P.op('pe', lambda e, kt=kt, pi=pi, po=po, pn=pn, view=view: e.matmul(pA[pi][:, 0:pn], lhsT=view[:, kt, :], rhs=aT[:, kt, po:po + pn],
                                                                                             start=(kt == 0), stop=(kt == KT - 1)),
                             reads=[wres(kt)] + [('aT', t_) for t_ in range(po // 128 + 1, (po + pn) // 128 + 1)], writes=[('pA', pi)], signal=(kt == KT - 1))
                    P.op('act', lambda e, pi=pi, po=po, pn=pn: e.activation(out=upb[:, 2 + po:2 + po + pn], in_=pA[pi][:, 0:pn], func=AF.Copy),
                         reads=[('pA', pi)], writes=['upb'])
                for (po, pn) in NP:
                    pi = pa2[0] % 6
                    pa2[0] += 1
                    for kt in range(KT):
                        P.op('pe', lambda e, kt=kt, pi=pi, po=po, pn=pn, viewg=viewg: e.matmul(pA[pi][:, 0:pn], lhsT=viewg[:, kt, :], rhs=aT[:, kt, po:po + pn],
                                                                                               start=(kt == 0), stop=(kt == KT - 1)),
                             reads=[wresg(kt)] + [('aT', t_) for t_ in range(po // 128 + 1, (po + pn) // 128 + 1)], writes=[('pA', pi)], signal=(kt == KT - 1))
                    gts.append((pi, po, pn))
                P.op('dve', lambda e, m=m: e.tensor_scalar(out=cc[:], in0=upb[:, 2:2 + NREAL], scalar1=cw[:, 2, m:m + 1], scalar2=cw[:, 3, m:m + 1], op0=ALU.mult, op1=ALU.add),
                     reads=['upb', 'cw'], writes=['cc'])
                P.op('dve', lambda e, m=m: e.scalar_tensor_tensor(out=cc[:], in0=upb[:, 1:1 + NREAL], scalar=cw[:, 1, m:m + 1], in1=cc[:], op0=ALU.mult, op1=ALU.add),
                     reads=['upb', 'cw', 'cc'], writes=['cc'])
                P.op('dve', lambda e, m=m: e.scalar_tensor_tensor(out=cc[:], in0=upb[:, 0:NREAL], scalar=cw[:, 0, m:m + 1], in1=cc[:], op0=ALU.mult, op1=ALU.add),
                     reads=['upb', 'cw', 'cc'], writes=['cc'])
                P.op('act', lambda e: e.activation(out=sg[:], in_=cc[:], func=AF.Sigmoid), reads=['cc'], writes=['sg'])
                P.op('dve', lambda e: e.tensor_tensor(out=cc[:], in0=cc[:], in1=sg[:], op=ALU.mult), reads=['cc', 'sg'], writes=['cc'])
                ba = m % 2
                for (pi, po, pn) in gts:
                    P.op('dve', lambda e, pi=pi, po=po, pn=pn, ba=ba: e.tensor_tensor(out=ab[ba][:, po:po + pn], in0=pA[pi][:, 0:pn], in1=cc[:, po:po + pn], op=ALU.mult),
                         reads=[('pA', pi), 'cc'], writes=[('ab', ba)])
                P.dma('sp', lambda e, m=m, ba=ba: e.dma_start(out=ACTS[m, :, :], in_=ab[ba][:]), reads=[('ab', ba)], writes=[('ACTS', m)])
            P.barrier()
            P.flush()

        with ExitStack() as es:
            def sb(name, shape, dt):
                return es.enter_context(nc.sbuf_tensor(name, list(shape), dt))

            def psm(name, shape, dt):
                return es.enter_context(nc.psum_tensor(name, list(shape), dt))
            identf = sb("p3_identf", [128, 128], F32)
            actT = sb("p3_actT", [128, FT, 512], BF16)
            wb3 = [sb(f"p3_w{i}", [128, 8 * 512], BF16) for i in range(4)]
            o2 = [sb(f"p3_o2{i}", [128, 512], F32) for i in range(2)]
            fo = [sb(f"p3_fo{i}", [128, 4, 128], F32) for i in range(2)]
            gB = sb("p3_gB", [128, D], F32)
            r1 = sb("p3_r1", [128, D], F32)
            r2 = sb("p3_r2", [128, D], F32)
            ss = sb("p3_ss", [128, 4], F32)
            pA = [psm(f"p3_pA{i}", [128, 512], F32) for i in range(4)]
            pT = [psm(f"p3_pT{i}", [128, 4, 128], F32) for i in range(2)]
            epsT = sb("p3_eps", [128, 1], F32)
            P.op('dve', lambda e: e.memset(epsT[:], EPS), writes=['epsT'])
            P.dma('sp', lambda e: e.dma_start(out=identf[:], in_=c_ident_f), writes=['identf'])
            P.dma('sp', lambda e: e.dma_start(out=gB[:], in_=bcast_row(final_norm[0:1, :], 128)), writes=['gB'])
            w3ctr = [0]
            cnt = 0
            JG = 8
            NW3 = 4
            CG = min(512, D)
            for (po, pn) in [(o, min(512, NREAL - o)) for o in range(0, NREAL, 512)]:
                for f0 in range(0, FT, 8):
                    f1 = min(FT, f0 + 8)
                    P.dma('sp', lambda e, f0=f0, f1=f1, po=po, pn=pn: e.dma_start(out=actT[:, f0:f1, 0:pn],
                                                                              in_=ACTS[f0:f1, :, po:po + pn].rearrange("f p n -> p f n")),
                          writes=[('actT', f0 // 8)])
                nsub = pn // 128
                for cg in range(D // CG):
                    nct = CG // 128
                    for j0 in range(0, FT, JG):
                        j1 = min(FT, j0 + JG)
                        s_ = w3ctr[0] % NW3
                        w3ctr[0] += 1
                        view = wb3[s_][:, 0:(j1 - j0) * CG].rearrange("p (k c) -> p k c", k=j1 - j0)
                        P.dma('pool', lambda e, j0=j0, j1=j1, view=view, cg=cg: e.dma_start(
                            out=view,
                            in_=w_ffn_out[j0 * 128:j1 * 128, cg * CG:(cg + 1) * CG].rearrange("(k p) c -> p k c", p=128)),
                            writes=[('w3', s_)])
                        for ct in range(nct):
                            for jt in range(j0, j1):
                                last = (ct == nct - 1 and jt == j1 - 1)
                                P.op('pe', lambda e, jt=jt, j0=j0, ct=ct, view=view, pn=pn: e.matmul(
                                    pA[ct][:, 0:pn], lhsT=view[:, jt - j0, ct * 128:(ct + 1) * 128], rhs=actT[:, jt, 0:pn],
                                    start=(jt == 0), stop=(jt == FT - 1)),
                                    reads=[('w3', s_), ('actT', jt // 8)], writes=[('pA', ct)], signal=(last or jt == FT - 1))
                    for ct in range(nct):
                        f = cg * nct + ct
                        b = cnt % 2
                        cnt += 1
                        P.op('act', lambda e, ct=ct, b=b, pn=pn: e.activation(out=o2[b][:, 0:pn], in_=pA[ct][:, 0:pn], func=AF.Copy),
                             reads=[('pA', ct)], writes=[('o2', b)])
                        for sidx in range(nsub):
                            P.op('pe', lambda e, sidx=sidx, b=b: e.transpose(pT[b][:, sidx, :], o2[b][:, sidx * 128:(sidx + 1) * 128], identf[:]),
                                 reads=[('o2', b), 'identf'], writes=[('pT', b)], signal=(sidx == nsub - 1))
                        P.op('dve', lambda e, b=b, nsub=nsub: e.tensor_copy(out=fo[b][:, 0:nsub, :], in_=pT[b][:, 0:nsub, :]),
                             reads=[('pT', b)], writes=[('fo', b)])
                        P.dma('sp', lambda e, b=b, nsub=nsub, po=po, f=f: e.dma_start(
                            out=FO[po:po + nsub * 128, f * 128:(f + 1) * 128].rearrange("(s p) c -> p s c", p=128),
                            in_=fo[b][:, 0:nsub, :]), reads=[('fo', b)], writes=[('FO', po, f)])
                for t in range(po // 128, (po + pn) // 128):
                    P.dma('sp', lambda e, t=t: e.dma_start(out=r1[:], in_=FO[t * 128:(t + 1) * 128, :]),
                          reads=[('FO', po, f_) for f_ in range(D // 128)], writes=['r1'])
                    P.dma('sp', lambda e, t=t: e.dma_start(out=r2[:], in_=H1[(t + 1) * 128:(t + 2) * 128, :]), writes=['r2'])
                    P.op('dve', lambda e: e.tensor_tensor(out=r1[:], in0=r1[:], in1=r2[:], op=ALU.add), reads=['r1', 'r2'], writes=['r1'])
                    P.op('dve', lambda e: e.memset(ss[:, 0:1], 0.0), writes=['ss'])
                    P.op('act', lambda e: e.activation(out=r2[:], in_=r1[:], func=AF.Square, accum_out=ss[:, 0:1]), reads=['r1'], writes=['r2', 'ss'])
                    P.op('act', lambda e: e.activation(out=ss[:, 1:2], in_=ss[:, 0:1], func=AF.Sqrt, bias=epsT[:, 0:1], scale=1.0 / D), reads=['ss', 'epsT'], writes=['ss'])
                    P.op('dve', lambda e: e.reciprocal(out=ss[:, 1:2], in_=ss[:, 1:2]), reads=['ss'], writes=['ss'])
                    P.op('dve', lambda e: e.scalar_tensor_tensor(out=r2[:], in0=r1[:], scalar=ss[:, 1:2], in1=gB[:], op0=ALU.mult, op1=ALU.mult),
                         reads=['r1', 'ss', 'gB'], writes=['r2'])
                    P.dma('sp', lambda e, t=t: e.dma_start(out=out[t * 128:(t + 1) * 128, :], in_=r2[:]), reads=['r2'], writes=[('out', t)])
            P.barrier()
            P.flush()
    P.check_deadlock()
    return nc


def host_constants(cfg):
    n = np.arange(128)
    c = {}
    c["c_ident_bf"] = np.eye(128, dtype=np.float32).astype(ml_dtypes.bfloat16)
    c["c_ident_f"] = np.eye(128, dtype=np.float32)
    c["c_invf"] = (10000.0 ** (-np.arange(128, dtype=np.float32) / 128)).astype(np.float32)[:, None]
    lg = np.log1p(-np.exp2(-5.0 - np.arange(cfg.RH, dtype=np.float64)))
    rdk = np.exp(lg[:, None] * (127 - n)[None, :]) / 16.0
    rdq = np.exp(lg[:, None] * (n + 1)[None, :])
    c["c_rdk"] = np.concatenate([rdk, rdk], axis=1).astype(np.float32)
    c["c_rdq"] = np.concatenate([rdq, rdq], axis=1).astype(np.float32)
    rel = n[None, :] - n[:, None]
    c["c_rmask"] = np.where(rel[None] >= 0, np.exp(lg[:, None, None] * np.maximum(rel, 0)[None]) / 16.0, 0.0).astype(np.float32)
    c["c_gmask"] = (rel >= 0).astype(np.float32)
    c["c_tri"] = np.where(rel >= 0, -1.0 / 16.0, 0.0).astype(np.float32)
    return c


def make_in_maps(cfg, x, positions, meta_tokens, attn_norm, w_in, w_gate_up, b_gate, ret_norm, gla_norm,
                 w_out, ffn_norm, w_ffn_in, conv_w, conv_b, w_ffn_out, final_norm):
    B, SEQ, D = x.shape
    f32 = np.float32
    consts = host_constants(cfg)
    shared = {
        "attn_norm": np.ascontiguousarray(attn_norm[0][None], f32),
        "w_in": np.ascontiguousarray(w_in[0], f32),
        "w_gate": np.ascontiguousarray(np.concatenate([w_gate_up[0], b_gate[0][None]], axis=0), f32),
        "ret_norm": np.ascontiguousarray(ret_norm[0][None], f32),
        "gla_norm": np.ascontiguousarray(gla_norm[0][None], f32),
        "w_out": np.ascontiguousarray(w_out[0], f32),
        "ffn_norm": np.ascontiguousarray(ffn_norm[0][None], f32),
        "w_ffn_in": np.ascontiguousarray(w_ffn_in[0], f32),
        "conv_w": np.ascontiguousarray(conv_w[0], f32),
        "conv_b": np.ascontiguousarray(conv_b[0][None], f32),
        "w_ffn_out": np.ascontiguousarray(w_ffn_out[0], f32),
        "final_norm": np.ascontiguousarray(final_norm[None], f32),
    }
    shared.update(consts)
    NPAD = 112
    in_maps = []
    half = SEQ // 2
    assert half == cfg.NREAL and NPAD + 16 + SEQ == cfg.NS
    metapos = (np.arange(16) - 16).astype(np.int32)
    for b in range(B):
        seq = np.concatenate([np.zeros((NPAD, D), f32), np.asarray(meta_tokens, f32), np.asarray(x[b], f32)], axis=0)
        pos = np.concatenate([np.zeros(NPAD, np.int32), metapos, np.asarray(positions[b], np.int32)])
        for s in range(2):
            if s == 0:
                xs = np.concatenate([np.zeros((cfg.NPRE * 128, D), f32), seq[0:cfg.NMAIN * 128]], axis=0)
                ps = np.concatenate([np.zeros(cfg.NPRE * 128, np.int32), pos[0:cfg.NMAIN * 128]])
            else:
                xs, ps = seq, pos
            m = dict(shared)
            m["xs"] = np.ascontiguousarray(xs)
            m["posr"] = np.ascontiguousarray(ps[None])
            in_maps.append(m)
    return in_maps


def run(cfg, inputs, trace=False):
    inputs = {k: np.asarray(v) for k, v in inputs.items()}
    nc = build_nc(cfg)
    in_maps = make_in_maps(cfg, **inputs)
    res = run_bass_kernel_spmd(nc, in_maps, core_ids=list(range(8)))
    B, SEQ, D = inputs["x"].shape
    out = np.zeros((B, SEQ, D), np.float32)
    for b in range(B):
        for s in range(2):
            out[b, s * cfg.NREAL:(s + 1) * cfg.NREAL] = res.results[2 * b + s]["out"]
    return out


def kernel(**inputs):
    return run(Cfg(), inputs)
```
